# Optimizing a Trainium2 kernel written in Bass

```python
import math
import jax, jax.numpy as jnp
from jax import lax
import numpy as np

D_MODEL = 1024
BATCH = 2
SEQ = 8192
DEPTH = 4

N_MIXERS = 2
N_CONV_LAYERS = (DEPTH + N_MIXERS - 1) // N_MIXERS
N_NSA_LAYERS = DEPTH // N_MIXERS
D_FF = 2816
CONV_WIDTH = 3
NSA_HEADS = 16
HEAD_DIM = D_MODEL // NSA_HEADS
NSA_GROUPS = 4
HEADS_PER_GROUP = NSA_HEADS // NSA_GROUPS
KV_DIM = NSA_GROUPS * HEAD_DIM
CMP_BLOCK = 32
CMP_STRIDE = 16
CMP_HIDDEN = 256
SEL_BLOCK = 64
SEL_TOP_N = 16
WINDOW = 512
Q_BLOCK = 128
REL_BUCKETS = 32
REL_MAX_DIST = 128
NSA_IN_DIM = NSA_HEADS * HEAD_DIM + 6 * KV_DIM + 3 * NSA_HEADS
RMS_EPS = 1e-6
NEG_INF = -1e30
FORCE = 1e9
HALF = 0.5

kernel_name = "hybrid_conv_nsa_macaron_adaln"


def rms_norm(x, g):
    xf = x.astype(jnp.float32)
    y = xf * lax.rsqrt(jnp.mean(xf * xf, axis=-1, keepdims=True) + RMS_EPS)
    return (y * g.astype(jnp.float32)).astype(x.dtype)


def adaln(x, g, shift, scale):
    return rms_norm(x, g) * (1 + scale[:, None, :]) + shift[:, None, :]


def t5_bucket(dist):
    n = jnp.maximum(dist, 0)
    max_exact = REL_BUCKETS // 2
    nf = jnp.maximum(n, 1).astype(jnp.float32)
    large = max_exact + (jnp.log(nf / max_exact) / math.log(REL_MAX_DIST / max_exact)
                         * (REL_BUCKETS - max_exact)).astype(jnp.int32)
    large = jnp.minimum(large, REL_BUCKETS - 1)
    return jnp.where(n < max_exact, n, large)


def masked_softmax(logits, valid):
    p = jax.nn.softmax(jnp.where(valid, logits, NEG_INF), axis=-1)
    return jnp.where(valid, p, 0.0)


def swiglu(h, w_in, w_out):
    g, u = jnp.split(h @ w_in, 2, axis=-1)
    return (jax.nn.silu(g) * u) @ w_out


def short_conv_mixer(h, w_in, conv_w, w_out):
    b_gate, c_gate, v = jnp.split(h @ w_in, 3, axis=-1)
    u = c_gate * v
    y = lax.conv_general_dilated(
        u, conv_w[:, None, :].astype(u.dtype), window_strides=(1,),
        padding=[(CONV_WIDTH - 1, 0)], dimension_numbers=("NWC", "WIO", "NWC"),
        feature_group_count=D_MODEL)
    return (b_gate * y) @ w_out


def nsa_mixer(h, w_in, cmp_pos, cmp_w1, cmp_w2, q_gain, k_gain, w_out, rel_bias):
    bsz, seq, _ = h.shape
    G, Hg, hd = NSA_GROUPS, HEADS_PER_GROUP, HEAD_DIM
    n_cmp = (seq - CMP_BLOCK) // CMP_STRIDE + 1
    n_sel = seq // SEL_BLOCK
    n_top = min(SEL_TOP_N, n_sel)
    n_qblk = seq // Q_BLOCK
    scale = HEAD_DIM ** -0.5
    f32 = jnp.float32

    offs = [NSA_HEADS * HEAD_DIM + i * KV_DIM for i in range(7)]
    q, kc, vc, ks, vs, kw, vw, gl = jnp.split(h @ w_in, offs, axis=-1)
    q = rms_norm(q.reshape(bsz, seq, G, Hg, hd), q_gain).transpose(0, 2, 3, 1, 4)
    gates = jax.nn.sigmoid(gl.astype(f32)).reshape(bsz, seq, G, Hg, 3).transpose(0, 2, 3, 1, 4)

    def kv(t):
        return t.reshape(bsz, seq, G, hd)

    blk_idx = jnp.arange(n_cmp)[:, None] * CMP_STRIDE + jnp.arange(CMP_BLOCK)[None, :]

    def compress(t, w1, w2):
        blocks = kv(t)[:, blk_idx] + cmp_pos[None, None, :, None, :]
        flat = blocks.transpose(0, 3, 1, 2, 4).reshape(bsz, G, n_cmp, CMP_BLOCK * hd)
        return jax.nn.gelu(flat @ w1) @ w2

    k_cmp = rms_norm(compress(kc, cmp_w1[0], cmp_w2[0]), k_gain[0])
    v_cmp = compress(vc, cmp_w1[1], cmp_w2[1])
    k_sel = rms_norm(kv(ks), k_gain[1]).transpose(0, 2, 1, 3).reshape(bsz, G, n_sel, SEL_BLOCK, hd)
    v_sel = kv(vs).transpose(0, 2, 1, 3).reshape(bsz, G, n_sel, SEL_BLOCK, hd)
    pad = ((0, 0), (0, 0), (WINDOW, 0), (0, 0))
    k_win = jnp.pad(rms_norm(kv(kw), k_gain[2]).transpose(0, 2, 1, 3), pad)
    v_win = jnp.pad(kv(vw).transpose(0, 2, 1, 3), pad)

    rb = rel_bias.astype(f32).T.reshape(G, Hg, REL_BUCKETS)
    g_ix = jnp.arange(G)[None, :, None, None, None]
    h_ix = jnp.arange(Hg)[None, None, :, None, None]
    cmp_start = jnp.arange(n_cmp) * CMP_STRIDE
    cmp_end = cmp_start + CMP_BLOCK - 1
    sel_start = jnp.arange(n_sel) * SEL_BLOCK
    overlap = ((cmp_start[:, None] <= sel_start[None, :] + SEL_BLOCK - 1)
               & (cmp_end[:, None] >= sel_start[None, :])).astype(f32)
    sel_ids = jnp.arange(n_sel)
    gather = jax.vmap(jax.vmap(lambda kb, ib: kb[ib]))

    def block(qi):
        t0 = qi * Q_BLOCK
        t = t0 + jnp.arange(Q_BLOCK)
        qb = lax.dynamic_slice_in_dim(q, t0, Q_BLOCK, axis=3)
        gb = lax.dynamic_slice_in_dim(gates, t0, Q_BLOCK, axis=3)

        dist_c = t[:, None] - cmp_end[None, :]
        s_c = (jnp.einsum('bghqd,bgnd->bghqn', qb, k_cmp).astype(f32) * scale
               + rb[:, :, t5_bucket(dist_c)])
        p_c = masked_softmax(s_c, dist_c >= 0)
        o_c = jnp.einsum('bghqn,bgnd->bghqd', p_c.astype(v_cmp.dtype), v_cmp)

        imp = jnp.einsum('bghqn,ns->bgqs', p_c, overlap)
        cur = t // SEL_BLOCK
        forced = ((sel_ids[None, :] == 0) | (sel_ids[None, :] == cur[:, None])
                  | (sel_ids[None, :] == cur[:, None] - 1))
        imp = jnp.where(sel_start[None, :] > t[:, None], NEG_INF, imp)
        imp = jnp.where(forced, FORCE, imp)
        top = lax.top_k(imp, n_top)[1]
        kg = gather(k_sel, top).reshape(bsz, G, Q_BLOCK, n_top * SEL_BLOCK, hd)
        vg = gather(v_sel, top).reshape(bsz, G, Q_BLOCK, n_top * SEL_BLOCK, hd)
        pos_s = (top[..., None] * SEL_BLOCK + jnp.arange(SEL_BLOCK)).reshape(bsz, G, Q_BLOCK, -1)
        dist_s = t[:, None] - pos_s
        s_s = (jnp.einsum('bghqd,bgqkd->bghqk', qb, kg).astype(f32) * scale
               + rb[g_ix, h_ix, t5_bucket(dist_s)[:, :, None]])
        p_s = masked_softmax(s_s, (dist_s >= 0)[:, :, None])
        o_s = jnp.einsum('bghqk,bgqkd->bghqd', p_s.astype(vg.dtype), vg)

        kwb = lax.dynamic_slice_in_dim(k_win, t0, WINDOW + Q_BLOCK, axis=2)
        vwb = lax.dynamic_slice_in_dim(v_win, t0, WINDOW + Q_BLOCK, axis=2)
        pos_w = t0 - WINDOW + jnp.arange(WINDOW + Q_BLOCK)
        dist_w = t[:, None] - pos_w[None, :]
        valid_w = (dist_w >= 0) & (dist_w < WINDOW) & (pos_w[None, :] >= 0)
        s_w = (jnp.einsum('bghqd,bgkd->bghqk', qb, kwb).astype(f32) * scale
               + rb[:, :, t5_bucket(dist_w)])
        p_w = masked_softmax(s_w, valid_w)
        o_w = jnp.einsum('bghqk,bgkd->bghqd', p_w.astype(vwb.dtype), vwb)

        out = gb[..., 0:1] * o_c + gb[..., 1:2] * o_s + gb[..., 2:3] * o_w
        return out.astype(h.dtype)

    o = lax.map(block, jnp.arange(n_qblk))
    o = o.transpose(1, 0, 4, 2, 3, 5).reshape(bsz, seq, NSA_HEADS * HEAD_DIM)
    return o @ w_out


def setup_inputs(seed: int = 0) -> dict:
    key = jax.random.key(seed)
    ks = jax.random.split(key, 20)
    f32 = jnp.float32

    def nrm(k, shape, s):
        return jax.random.normal(k, shape, dtype=f32) * s

    return {
        "x": nrm(ks[0], (BATCH, SEQ, D_MODEL), 1.0),
        "c": nrm(ks[1], (BATCH, D_MODEL), 1.0),
        "norm_g": 1.0 + nrm(ks[2], (DEPTH, 3, D_MODEL), 0.02),
        "ada_w": nrm(ks[3], (DEPTH, D_MODEL, 9 * D_MODEL), 0.5 * D_MODEL ** -0.5),
        "ada_b": nrm(ks[4], (DEPTH, 9 * D_MODEL), 0.02),
        "ffn_w_in": nrm(ks[5], (DEPTH, 2, D_MODEL, 2 * D_FF), D_MODEL ** -0.5),
        "ffn_w_out": nrm(ks[6], (DEPTH, 2, D_FF, D_MODEL), D_FF ** -0.5),
        "conv_w_in": nrm(ks[7], (N_CONV_LAYERS, D_MODEL, 3 * D_MODEL), D_MODEL ** -0.5),
        "conv_w": nrm(ks[8], (N_CONV_LAYERS, CONV_WIDTH, D_MODEL), CONV_WIDTH ** -0.5),
        "conv_w_out": nrm(ks[9], (N_CONV_LAYERS, D_MODEL, D_MODEL), D_MODEL ** -0.5),
        "nsa_w_in": nrm(ks[10], (N_NSA_LAYERS, D_MODEL, NSA_IN_DIM), D_MODEL ** -0.5),
        "nsa_cmp_pos": nrm(ks[11], (N_NSA_LAYERS, CMP_BLOCK, HEAD_DIM), 0.5),
        "nsa_cmp_w1": nrm(ks[12], (N_NSA_LAYERS, 2, CMP_BLOCK * HEAD_DIM, CMP_HIDDEN), (CMP_BLOCK * HEAD_DIM) ** -0.5),
        "nsa_cmp_w2": nrm(ks[13], (N_NSA_LAYERS, 2, CMP_HIDDEN, HEAD_DIM), CMP_HIDDEN ** -0.5),
        "nsa_q_gain": 1.0 + nrm(ks[14], (N_NSA_LAYERS, HEAD_DIM), 0.02),
        "nsa_k_gain": 1.0 + nrm(ks[15], (N_NSA_LAYERS, 3, HEAD_DIM), 0.02),
        "nsa_w_out": nrm(ks[16], (N_NSA_LAYERS, D_MODEL, D_MODEL), D_MODEL ** -0.5),
        "rel_bias": nrm(ks[17], (REL_BUCKETS, NSA_HEADS), 0.5),
    }


def reference(x, c, norm_g, ada_w, ada_b, ffn_w_in, ffn_w_out, conv_w_in, conv_w, conv_w_out,
              nsa_w_in, nsa_cmp_pos, nsa_cmp_w1, nsa_cmp_w2, nsa_q_gain, nsa_k_gain, nsa_w_out,
              rel_bias):
    cond = jax.nn.silu(c)
    bsz = c.shape[0]
    for i in range(DEPTH):
        mod = (cond @ ada_w[i] + ada_b[i]).reshape(bsz, 3, 3, D_MODEL)
        shift, scl, gate = mod[:, :, 0], mod[:, :, 1], mod[:, :, 2]

        h = adaln(x, norm_g[i, 0], shift[:, 0], scl[:, 0])
        x = x + HALF * gate[:, 0, None, :] * swiglu(h, ffn_w_in[i, 0], ffn_w_out[i, 0])

        h = adaln(x, norm_g[i, 1], shift[:, 1], scl[:, 1])
        j = i // N_MIXERS
        if i % N_MIXERS == 0:
            y = short_conv_mixer(h, conv_w_in[j], conv_w[j], conv_w_out[j])
        else:
            y = nsa_mixer(h, nsa_w_in[j], nsa_cmp_pos[j], nsa_cmp_w1[j], nsa_cmp_w2[j],
                          nsa_q_gain[j], nsa_k_gain[j], nsa_w_out[j], rel_bias)
        x = x + gate[:, 1, None, :] * y

        h = adaln(x, norm_g[i, 2], shift[:, 2], scl[:, 2])
        x = x + HALF * gate[:, 2, None, :] * swiglu(h, ffn_w_in[i, 1], ffn_w_out[i, 1])
    return x
```

```python
import numpy as np
from contextlib import ExitStack
import concourse.bass as bass
import concourse.mybir as mybir
from concourse.bass_utils import run_bass_kernel_spmd

F32 = mybir.dt.float32
BF16 = mybir.dt.bfloat16
ALU = mybir.AluOpType
AF = mybir.ActivationFunctionType
AX = mybir.AxisListType

D = 1024
DFF = 2816
NCH = 8
NJ = 22
B = 2
S = 8192
NCORE = 8
TOK = 2048
EPS = 1e-6
NSA_IN = 2608
NEG = -30000.0


class Buf:
    __slots__ = ("name", "last_w", "readers")

    def __init__(self, name):
        self.name = name
        self.last_w = None
        self.readers = []


class Op:
    __slots__ = ("eng", "fn", "deps", "dma", "key", "signal", "count", "kcount")

    def __init__(self, eng, fn, dma, key):
        self.eng = eng
        self.fn = fn
        self.dma = dma
        self.key = key
        self.deps = []
        self.signal = False
        self.count = 0
        self.kcount = 0


ENGS = ["tensor", "vector", "scalar", "gpsimd", "sync"]


class Prog:
    def __init__(self, nc):
        self.nc = nc
        self.q = {e: [] for e in ENGS}
        self.stack = ExitStack()
        self.nbuf = 0

    def buf(self, name=None):
        self.nbuf += 1
        return Buf(name or f"b{self.nbuf}")

    def sbuf(self, name, shape, dt):
        return self.stack.enter_context(self.nc.sbuf_tensor(name, list(shape), dt))

    def psum(self, name, shape, dt):
        return self.stack.enter_context(self.nc.psum_tensor(name, list(shape), dt))

    def op(self, eng, fn, r=(), w=(), dma=False, key=None):
        if dma and key is None:
            key = (w[0].name if len(w) else r[0].name)
        o = Op(eng, fn, dma, key)
        deps = set()
        raw = set()
        for b in r:
            if b.last_w is not None:
                deps.add(b.last_w)
                raw.add(b.last_w)
        for b in w:
            if b.last_w is not None:
                deps.add(b.last_w)
            for rd in b.readers:
                deps.add(rd)
        for d in deps:
            if d is o:
                continue
            if d.dma or dma or d.eng != eng or (d in raw and eng != "tensor"):
                o.deps.append(d)
        for b in r:
            b.readers.append(o)
        for b in w:
            b.last_w = o
            b.readers = []
        self.q[eng].append(o)
        return o

    def dma(self, eng, out, in_, r=(), w=(), key=None, **kw):
        return self.op(eng, lambda e: e.dma_start(out=out, in_=in_, **kw), r=r, w=w, dma=True, key=key)

    def emit(self):
        nc = self.nc
        for e in ENGS:
            for o in self.q[e]:
                for d in o.deps:
                    d.signal = True
        keytot = {}
        for e in ENGS:
            cnt = 0
            for o in self.q[e]:
                if o.dma:
                    keytot[o.key] = keytot.get(o.key, 0) + 16
                    o.kcount = keytot[o.key]
                elif o.signal:
                    cnt += 1
                    o.count = cnt
        st = self.stack
        esem = {e: st.enter_context(nc.semaphore("se_" + e)) for e in ENGS}
        ksem = {}
        for i, k in enumerate(keytot):
            ksem[k] = st.enter_context(nc.semaphore(f"sk{i}"))
        self.nsem = len(ksem) + len(esem)

        def run(ename, e):
            waited = {}
            for o in self.q[ename]:
                need = {}
                for d in o.deps:
                    if d.dma:
                        k = ("k", d.key)
                        v = d.kcount
                    else:
                        k = ("e", d.eng)
                        v = d.count
                    if v > need.get(k, 0):
                        need[k] = v
                for k, v in need.items():
                    if waited.get(k, 0) >= v:
                        continue
                    waited[k] = v
                    sem = ksem[k[1]] if k[0] == "k" else esem[k[1]]
                    e.wait_ge(sem, v)
                ins = o.fn(e)
                if o.dma:
                    ins.then_inc(ksem[o.key], 16)
                elif o.signal:
                    ins.then_inc(esem[ename], 1)
            if ename == "sync":
                for k, tot in keytot.items():
                    if waited.get(("k", k), 0) < tot:
                        e.wait_ge(ksem[k], tot)

        with nc.Block() as block:
            @block.tensor
            def _(e):
                run("tensor", e)

            @block.vector
            def _(e):
                run("vector", e)

            @block.scalar
            def _(e):
                run("scalar", e)

            @block.gpsimd
            def _(e):
                run("gpsimd", e)

            @block.sync
            def _(e):
                run("sync", e)
        self.stack.close()


class WStream:
    def __init__(self, P, nstg=3, nw=4, elems=2048):
        self.P = P
        self.elems = elems
        self.stg = [P.sbuf(f"wstg{i}", [128, elems], F32) for i in range(nstg)]
        self.stgb = [P.buf(f"wstg{i}") for i in range(nstg)]
        self.wt = [P.sbuf(f"wbf{i}", [128, elems], BF16) for i in range(nw)]
        self.wtb = [P.buf(f"wbf{i}") for i in range(nw)]
        self.i = 0
        self.j = 0
        self.nd = 0

    def load(self, src, shape, parts=128):
        P = self.P
        n = int(np.prod(shape[1:]))
        assert n <= self.elems
        si = self.i % len(self.stg)
        wi = self.j % len(self.wt)
        self.i += 1
        self.j += 1
        stg = self.stg[si]
        wt = self.wt[wi]
        if len(shape) == 3:
            sv = stg[0:parts, 0:n].rearrange("p (a b) -> p a b", a=shape[1])
            wv = wt[0:parts, 0:n].rearrange("p (a b) -> p a b", a=shape[1])
        else:
            sv = stg[0:parts, 0:n]
            wv = wt[0:parts, 0:n]
        deng = "sync" if (self.nd % 2 == 0) else "gpsimd"
        self.nd += 1
        P.dma("sync", sv, src, w=[self.stgb[si]])
        ceng = "gpsimd"
        P.op(ceng, lambda e: e.tensor_copy(out=wv, in_=sv), r=[self.stgb[si]], w=[self.wtb[wi]])
        return wv, self.wtb[wi]


def mm_group(P, out, pairs, r, w):
    n = len(pairs)

    def fn(e):
        ins = None
        for i, (l, rr) in enumerate(pairs):
            ins = e.matmul(out, l, rr, start=(i == 0), stop=(i == n - 1))
        return ins
    return P.op("tensor", fn, r=r, w=w)


def act(P, out, in_, func, r, w, bias=None, scale=None, accum=None):
    kw = {}
    if bias is not None:
        kw["bias"] = bias
    if scale is not None:
        kw["scale"] = scale
    if accum is not None:
        kw["accum_out"] = accum
    return P.op("scalar", lambda e: e.activation(out=out, in_=in_, func=func, **kw), r=r, w=w)


def tt(P, eng, out, in0, in1, op, r, w):
    return P.op(eng, lambda e: e.tensor_tensor(out=out, in0=in0, in1=in1, op=op), r=r, w=w)


def ts(P, eng, out, in0, s1, s2, op0, op1, r, w, accum=None):
    kw = {}
    if accum is not None:
        kw["accum_out"] = accum
    if op1 is None:
        return P.op(eng, lambda e: e.tensor_scalar(out=out, in0=in0, scalar1=s1, scalar2=None, op0=op0, **kw), r=r, w=w)
    return P.op(eng, lambda e: e.tensor_scalar(out=out, in0=in0, scalar1=s1, scalar2=s2, op0=op0, op1=op1, **kw), r=r, w=w)


def stt(P, eng, out, in0, scalar, in1, op0, op1, r, w):
    return P.op(eng, lambda e: e.scalar_tensor_tensor(out=out, in0=in0, scalar=scalar, in1=in1, op0=op0, op1=op1), r=r, w=w)


def cp(P, eng, out, in_, r, w):
    return P.op(eng, lambda e: e.tensor_copy(out=out, in_=in_), r=r, w=w)


def mset(P, eng, out, val, w):
    return P.op(eng, lambda e: e.memset(out, val), r=[], w=w)


class Rot:
    def __init__(self, items):
        self.items = items
        self.i = 0

    def next(self):
        it = self.items[self.i % len(self.items)]
        self.i += 1
        return it


HALO = 2
NT = HALO + TOK
TILES = [(0, HALO)] + [(HALO + 512 * i, 512) for i in range(4)]
FFN_PARTS = [(0, 4), (4, 4), (8, 4), (12, 4), (16, 4), (20, 2)]


class TCtx:
    pass


def t_setup(P, nvec):
    C = TCtx()
    C.x = P.sbuf("x", [128, NCH, NT], F32)
    C.xb = [[P.buf(f"x{t}_{o}") for o in range(NCH)] for t in range(len(TILES))]
    C.xall = [bb for l in C.xb for bb in l]
    C.h = P.sbuf("h", [128, NCH, NT], BF16)
    C.hb = [P.buf(f"h{t}") for t in range(len(TILES))]
    C.a = [P.sbuf(f"a{i}", [128, 4, NT], BF16) for i in range(2)]
    C.ab = [[P.buf(f"a{i}_{t}") for t in range(len(TILES))] for i in range(2)]
    C.scr = P.sbuf("scr", [128, 4104], F32)
    C.scrL = [P.buf("scrA"), P.buf("scrB")]
    C.rstd = P.sbuf("rstd", [128, 512], F32)
    C.rstdb = P.buf("rstd")
    C.sg = Rot([(P.sbuf(f"sg{i}", [128, 512], F32), P.buf(f"sg{i}")) for i in range(2)])
    C.ostg = Rot(C.sg.items + [(P.sbuf(f"ostg{i}", [128, 512], F32), P.buf(f"ostg{i}")) for i in range(3)])
    C.ones = P.sbuf("ones", [128, 128], F32)
    C.onesb = P.buf("ones")
    C.vec = P.sbuf("vec_sb", [128, nvec + 8 + 36, 8], F32)
    C.vecb = P.buf("vec")
    C.nvec = nvec
    C.ntmp = 0
    C.halo_on = P.sbuf("halo_sb", [128, 1], F32)
    C.halob = P.buf("halo_on")
    ps = [(P.psum(f"ps{i}", [128, 512], F32), P.buf(f"ps{i}")) for i in range(8)]
    C.ps_g = Rot(ps[0:2])
    C.ps_u = Rot(ps[2:4])
    C.ps_y = Rot(ps[4:7])
    C.ps_s = Rot(ps[7:8])
    C.ws = WStream(P, nstg=3, nw=6)
    mset(P, "vector", C.ones[:], 1.0, w=[C.onesb])
    return C


def t_norm(P, C, gs, shift, halo=True):
    for t, (t0, n) in enumerate(TILES):
        if t == 0 and not halo:
            continue
        sq = C.scr[:, 0:NCH * n].rearrange("p (k n) -> p k n", k=NCH)
        act(P, sq, C.x[:, :, t0:t0 + n], AF.Square, r=C.xb[t], w=C.scrL)
        ps, psb = C.ps_s.next()
        mm_group(P, ps[:, 0:n], [(C.ones[:], sq[:, k, :]) for k in range(NCH)], r=C.scrL + [C.onesb], w=[psb])
        ts(P, "vector", C.rstd[:, 0:n], ps[:, 0:n], 1.0 / D, EPS, ALU.mult, ALU.add, r=[psb], w=[C.rstdb])
        act(P, C.rstd[:, 0:n], C.rstd[:, 0:n], AF.Sqrt, r=[C.rstdb], w=[C.rstdb])
        P.op("vector", lambda e, n=n: e.reciprocal(out=C.rstd[:, 0:n], in_=C.rstd[:, 0:n]), r=[C.rstdb], w=[C.rstdb])
        for k in range(NCH):
            tt(P, "vector", sq[:, k, :], C.x[:, k, t0:t0 + n], C.rstd[:, 0:n], ALU.mult,
               r=[C.xb[t][k], C.rstdb], w=C.scrL)
        for k in range(NCH):
            act(P, C.h[:, k, t0:t0 + n], sq[:, k, :], AF.Identity, r=C.scrL + [C.vecb], w=[C.hb[t]],
                bias=shift[:, k:k + 1], scale=gs[:, k:k + 1])


def t_ffn(P, C, w_in, w_out, gs, shift, ghalf, halo=True, gen=None):
    t_norm(P, C, gs, shift, halo)
    TL = [(t, t0, n) for t, (t0, n) in enumerate(TILES) if (halo or t > 0)]
    win_v = w_in.rearrange("(k p) c -> p k c", p=128)
    wout_v = w_out.rearrange("(j p) c -> p j c", p=128)

    def phase1(pi, j0, nj):
        ab = pi % 2
        for jp in range(0, nj, 2):
            c0 = (j0 + jp) * 128
            wg, wgb = C.ws.load(win_v[:, :, c0:c0 + 256], [128, NCH, 256])
            wu, wub = C.ws.load(win_v[:, :, DFF + c0:DFF + c0 + 256], [128, NCH, 256])
            for jj in range(2):
                for t, t0, n in TL:
                    pg, pgb = C.ps_g.next()
                    pu, pub = C.ps_u.next()
                    mm_group(P, pg[:, 0:n], [(wg[:, k, jj * 128:(jj + 1) * 128], C.h[:, k, t0:t0 + n]) for k in range(NCH)],
                             r=[wgb, C.hb[t]], w=[pgb])
                    mm_group(P, pu[:, 0:n], [(wu[:, k, jj * 128:(jj + 1) * 128], C.h[:, k, t0:t0 + n]) for k in range(NCH)],
                             r=[wub, C.hb[t]], w=[pub])
                    sg, sgb = C.sg.next()
                    act(P, sg[:, 0:n], pg[:, 0:n], AF.Silu, r=[pgb], w=[sgb])
                    tt(P, "vector", C.a[ab][:, jp + jj, t0:t0 + n], sg[:, 0:n], pu[:, 0:n], ALU.mult,
                       r=[sgb, pub], w=[C.ab[ab][t]])
                    if gen is not None and t % 2 == 0:
                        next(gen, None)

    def phase2(pi, j0, nj):
        ab = pi % 2
        wos = []
        for jp in range(0, nj, 2):
            r0 = j0 + jp
            wo, wob = C.ws.load(wout_v[:, r0:r0 + 2, :], [128, 2, D])
            wos.append((wo, wob))
        for t, t0, n in TL:
            for o in range(NCH):
                py, pyb = C.ps_y.next()
                pairs = []
                for ji in range(nj):
                    wo, wob = wos[ji // 2]
                    pairs.append((wo[:, ji % 2, o * 128:(o + 1) * 128], C.a[ab][:, ji, t0:t0 + n]))
                mm_group(P, py[:, 0:n], pairs, r=[b for _, b in wos] + [C.ab[ab][t]], w=[pyb])
                stt(P, "vector", C.x[:, o, t0:t0 + n], py[:, 0:n], ghalf[:, o:o + 1], C.x[:, o, t0:t0 + n],
                    ALU.mult, ALU.add, r=[pyb, C.vecb, C.xb[t][o]], w=[C.xb[t][o]])

    prev = None
    for pi, (j0, nj) in enumerate(FFN_PARTS):
        phase1(pi, j0, nj)
        if prev is not None:
            phase2(*prev)
        prev = (pi, j0, nj)
    phase2(*prev)


def t_conv(P, C, w_in, w_out, gs, shift, gate, cw):
    t_norm(P, C, gs, shift)
    win_v = w_in.rearrange("(k p) c -> p k c", p=128)
    wout_v = w_out.rearrange("(k p) c -> p k c", p=128)
    bsb = C.scr[:, 0:NT]
    usb = C.scr[:, 2052:2052 + NT]
    for o in range(NCH):
        ws3 = []
        for part in range(3):
            c0 = part * D + o * 128
            ws3.append(C.ws.load(win_v[:, :, c0:c0 + 128], [128, NCH, 128]))
        for t, (t0, n) in enumerate(TILES):
            pb, pbb = C.ps_g.next()
            pc, pcb = C.ps_u.next()
            pv, pvb = C.ps_y.next()
            for (pp, ppb), (wv, wb) in zip([(pb, pbb), (pc, pcb), (pv, pvb)], ws3):
                mm_group(P, pp[:, 0:n], [(wv[:, k, :], C.h[:, k, t0:t0 + n]) for k in range(NCH)], r=[wb, C.hb[t]], w=[ppb])
            act(P, bsb[:, t0:t0 + n], pb[:, 0:n], AF.Identity, r=[pbb], w=[C.scrL[0]])
            sg, sgb = C.sg.next()
            act(P, sg[:, 0:n], pc[:, 0:n], AF.Identity, r=[pcb], w=[sgb])
            tt(P, "vector", usb[:, t0:t0 + n], sg[:, 0:n], pv[:, 0:n], ALU.mult, r=[sgb, pvb], w=[C.scrL[1]])
        ts(P, "vector", usb[:, 0:HALO], usb[:, 0:HALO], C.halo_on[:, 0:1], None, ALU.mult, None, r=[C.scrL[1], C.halob], w=[C.scrL[1]])
        zt = C.a[o // 4]
        for t, (t0, n) in enumerate(TILES):
            if t == 0:
                continue
            ysb, ysbb = C.sg.next()
            ts(P, "vector", ysb[:, 0:n], usb[:, t0 - 2:t0 - 2 + n], cw[:, 0, o:o + 1], None, ALU.mult, None, r=[C.scrL[1], C.vecb], w=[ysbb])
            stt(P, "vector", ysb[:, 0:n], usb[:, t0 - 1:t0 - 1 + n], cw[:, 1, o:o + 1], ysb[:, 0:n], ALU.mult, ALU.add, r=[C.scrL[1], C.vecb, ysbb], w=[ysbb])
            stt(P, "vector", ysb[:, 0:n], usb[:, t0:t0 + n], cw[:, 2, o:o + 1], ysb[:, 0:n], ALU.mult, ALU.add, r=[C.scrL[1], C.vecb, ysbb], w=[ysbb])
            tt(P, "gpsimd", zt[:, o % 4, t0:t0 + n], bsb[:, t0:t0 + n], ysb[:, 0:n], ALU.mult, r=[C.scrL[0], ysbb], w=[C.ab[o // 4][t]])
    for oo2 in range(0, NCH, 2):
        wo, wob = C.ws.load(wout_v[:, :, oo2 * 128:(oo2 + 2) * 128], [128, NCH, 256])
        for oi in range(2):
            o = oo2 + oi
            for t, (t0, n) in enumerate(TILES):
                if t == 0:
                    continue
                py, pyb = C.ps_y.next()
                mm_group(P, py[:, 0:n], [(wo[:, k, oi * 128:(oi + 1) * 128], C.a[k // 4][:, k % 4, t0:t0 + n]) for k in range(NCH)],
                         r=[wob, C.ab[0][t], C.ab[1][t]], w=[pyb])
                stt(P, "vector", C.x[:, o, t0:t0 + n], py[:, 0:n], gate[:, o:o + 1], C.x[:, o, t0:t0 + n],
                    ALU.mult, ALU.add, r=[pyb, C.vecb, C.xb[t][o]], w=[C.xb[t][o]])


def t_nsa_pre(P, C, w_in, projT, gs, shift):
    t_norm(P, C, gs, shift, False)
    win_v = w_in.rearrange("(k p) c -> p k c", p=128)
    nchunk = (NSA_IN + 127) // 128
    wcur = None
    for oc in range(nchunk):
        rows = min(128, NSA_IN - oc * 128)
        if oc % 2 == 0:
            cols = min(256, NSA_IN - oc * 128)
            wcur = C.ws.load(win_v[:, :, oc * 128:oc * 128 + cols], [128, NCH, cols])
        wv, wb = wcur
        c0 = (oc % 2) * 128
        for t, (t0, n) in enumerate(TILES):
            if t == 0:
                continue
            pp, ppb = C.ps_y.next()
            mm_group(P, pp[0:rows, 0:n], [(wv[:, k, c0:c0 + rows], C.h[:, k, t0:t0 + n]) for k in range(NCH)],
                     r=[wb, C.hb[t]], w=[ppb])
            sg, sgb = C.ostg.next()
            act(P, sg[0:rows, 0:n], pp[0:rows, 0:n], AF.Identity, r=[ppb], w=[sgb])
            P.dma("sync", projT[oc * 128:oc * 128 + rows, t0 - HALO:t0 - HALO + n], sg[0:rows, 0:n], r=[sgb], key=sgb.name + "_o")


def t_nsa_post(P, C, oT, w_out, gate):
    wout_v = w_out.rearrange("(k p) c -> p k c", p=128)
    oT_v = oT.rearrange("(k p) t -> p k t", p=128)
    for k in range(NCH):
        for hf in range(2):
            c0 = hf * (NT // 2)
            n = NT // 2
            stg = C.scr[:, hf * 2052: hf * 2052 + n]
            P.dma("sync", stg, oT_v[:, k, c0:c0 + n], w=[C.scrL[hf]])
            cp(P, "gpsimd", C.h[:, k, c0:c0 + n], stg, r=[C.scrL[hf]], w=C.hb)
    for oo2 in range(0, NCH, 2):
        wo, wob = C.ws.load(wout_v[:, :, oo2 * 128:(oo2 + 2) * 128], [128, NCH, 256])
        for oi in range(2):
            o = oo2 + oi
            for t, (t0, n) in enumerate(TILES):
                py, pyb = C.ps_y.next()
                mm_group(P, py[:, 0:n], [(wo[:, k, oi * 128:(oi + 1) * 128], C.h[:, k, t0:t0 + n]) for k in range(NCH)],
                         r=[wob, C.hb[t]], w=[pyb])
                stt(P, "vector", C.x[:, o, t0:t0 + n], py[:, 0:n], gate[:, o:o + 1], C.x[:, o, t0:t0 + n],
                    ALU.mult, ALU.add, r=[pyb, C.vecb, C.xb[t][o]], w=[C.xb[t][o]])


def vec_names(stages):
    names = []
    for st in stages:
        kind, L = st[0], st[1]
        if kind == "ffn":
            s = 0 if st[2] == 0 else 2
            names += [f"L{L}_g{s}"]
        elif kind in ("conv", "nsa_pre"):
            names += [f"L{L}_g1"]
            if kind == "conv":
                names += [f"L{L}_cw0", f"L{L}_cw1", f"L{L}_cw2"]
    out = []
    for n in names:
        if n not in out:
            out.append(n)
    if not out:
        out = ["L0_g0"]
    return out


def t_ada_gen(P, C, adaw, L, modrow_d, modb):
    wv = adaw.rearrange("(k p) c -> p k c", p=128)
    rowbufs = []
    for c4 in range(36):
        hf = c4 % 2
        stgv = C.scr[:, hf * 2052:hf * 2052 + 2048].rearrange("p (k c) -> p k c", k=NCH)
        P.dma("sync", stgv, wv[:, :, c4 * 256:(c4 + 1) * 256], w=[C.scrL[hf]])
        ps, psb = C.ps_y.next()
        mm_group(P, ps[0:1, 0:256], [(C.cond[:, k:k + 1], stgv[:, k, :]) for k in range(NCH)], r=[C.scrL[hf], C.condb], w=[psb])
        sg, sgb = C.sg.next()
        act(P, sg[0:1, 0:256], ps[0:1, 0:256], AF.Identity, r=[psb], w=[sgb])
        rb_ = P.buf(f"modrow{L}_{c4}")
        P.dma("sync", modrow_d[L:L + 1, c4 * 256:(c4 + 1) * 256], sg[0:1, 0:256], r=[sgb], w=[rb_], key=sgb.name + "_o")
        rowbufs.append(rb_)
        yield
    base = C.nvec + 8 + L * 9
    dst = C.vec[:, base:base + 9, :].rearrange("p a b -> p (a b)")
    P.dma("sync", dst, modrow_d[L].rearrange("(c p) -> p c", p=128), r=rowbufs, w=[modb], key="modT",
          allow_slow_non_contiguous=True)
    yield


def t_ada_finish(P, C, adab, adabb, L, modb):
    base = C.nvec + 8 + L * 9
    dst = C.vec[:, base:base + 9, :].rearrange("p a b -> p (a b)")
    tt(P, "vector", dst, dst, adab, ALU.add, r=[adabb, modb, C.vecb], w=[C.vecb, modb])


def build_T(stages):
    nc = bass.Bass("TRN2", target_bir_lowering=False)
    names = vec_names(stages)
    nvec = len(names)
    vi = {n: i for i, n in enumerate(names)}
    xT = nc.dram_tensor("xT", [D, NT], F32, kind="ExternalInput").ap()
    vec = nc.dram_tensor("vec", [128, nvec, 8], F32, kind="ExternalInput").ap()
    halo_on = nc.dram_tensor("halo_on", [128, 1], F32, kind="ExternalInput").ap()
    xoT = nc.dram_tensor("xoT", [D, NT], F32, kind="ExternalOutput").ap()
    wd = {}
    cT_d = nc.dram_tensor("cT", [128, NCH], F32, kind="ExternalInput").ap()
    modin_d = nc.dram_tensor("modin", [128, 36, 8], F32, kind="ExternalInput").ap()
    modout_d = nc.dram_tensor("modout", [128, 36, 8], F32, kind="ExternalOutput").ap()
    modrow_d = nc.dram_tensor("modrow", [4, 9 * D], F32).ap()
    for si, st in enumerate(stages):
        kind = st[0]
        if kind == "ada":
            wd[si] = (nc.dram_tensor(f"adaw{st[1]}", [D, 9 * D], F32, kind="ExternalInput").ap(),
                      nc.dram_tensor(f"adab{st[1]}", [128, 72], F32, kind="ExternalInput").ap())
        elif kind == "ffn":
            wd[si] = (nc.dram_tensor(f"w{si}_in", [D, 2 * DFF], F32, kind="ExternalInput").ap(),
                      nc.dram_tensor(f"w{si}_out", [DFF, D], F32, kind="ExternalInput").ap())
        elif kind == "conv":
            wd[si] = (nc.dram_tensor(f"w{si}_in", [D, 3 * D], F32, kind="ExternalInput").ap(),
                      nc.dram_tensor(f"w{si}_out", [D, D], F32, kind="ExternalInput").ap())
        elif kind == "nsa_pre":
            wd[si] = (nc.dram_tensor(f"w{si}_in", [D, NSA_IN], F32, kind="ExternalInput").ap(),
                      nc.dram_tensor(f"projT{si}", [NSA_IN, TOK], F32, kind="ExternalOutput").ap())
        elif kind == "nsa_post":
            wd[si] = (nc.dram_tensor(f"oT{si}", [D, NT], F32, kind="ExternalInput").ap(),
                      nc.dram_tensor(f"w{si}_out", [D, D], F32, kind="ExternalInput").ap())
    P = Prog(nc)
    C = t_setup(P, nvec)
    P.dma("sync", C.vec[:, 0:nvec, :], vec, w=[C.vecb])
    P.dma("sync", C.halo_on[:], halo_on, w=[C.halob])
    xT_v = xT.rearrange("(k p) t -> p k t", p=128)
    P.dma("sync", C.x[:], xT_v, w=C.xall, key="xin")

    C.cond = P.sbuf("cond", [128, NCH], F32)
    C.condb = P.buf("cond")
    P.dma("sync", C.cond[:], cT_d, w=[C.condb])
    act(P, C.cond[:], C.cond[:], AF.Silu, r=[C.condb], w=[C.condb])
    P.dma("sync", C.vec[:, nvec + 8:nvec + 44, :], modin_d, w=[C.vecb])
    adab_sb = {}
    for si, st in enumerate(stages):
        if st[0] == "ada":
            t_ = P.sbuf(f"adab_sb{st[1]}", [128, 72], F32)
            tb_ = P.buf(f"adab_sb{st[1]}")
            P.dma("sync", t_[:], wd[si][1], w=[tb_])
            adab_sb[st[1]] = (t_, tb_)

    def V(name):
        L_ = int(name[1:name.index("_")])
        f = name[name.index("_") + 1:]
        for kind_i, kn in enumerate(("shift", "scale", "gate")):
            if f.startswith(kn):
                sub = int(f[len(kn):])
                return C.vec[:, nvec + 8 + L_ * 9 + sub * 3 + kind_i, :]
        return C.vec[:, vi[name], :]

    def tmpvec():
        i = nvec + C.ntmp % 8
        C.ntmp += 1
        return C.vec[:, i, :]

    ada_done = set()
    for si, st in enumerate(stages):
        kind, L = st[0], st[1]
        if kind == "ada":
            if si in ada_done:
                continue
            mb_ = P.buf(f"modb{L}")
            for _ in t_ada_gen(P, C, wd[si][0], L, modrow_d, mb_):
                pass
            t_ada_finish(P, C, adab_sb[L][0][:], adab_sb[L][1], L, mb_)
        elif kind == "ffn":
            s = 0 if st[2] == 0 else 2
            gs = tmpvec()
            gh = tmpvec()
            stt(P, "vector", gs, V(f"L{L}_scale{s}"), 1.0, V(f"L{L}_g{s}"), ALU.add, ALU.mult, r=[C.vecb], w=[C.vecb])
            ts(P, "vector", gh, V(f"L{L}_gate{s}"), 0.5, None, ALU.mult, None, r=[C.vecb], w=[C.vecb])
            need_halo = any(st2[0] == "conv" for st2 in stages[si + 1:])
            g_ = None
            if si + 1 < len(stages) and stages[si + 1][0] == "ada":
                L2 = stages[si + 1][1]
                mb_ = P.buf(f"modb{L2}")
                g_ = t_ada_gen(P, C, wd[si + 1][0], L2, modrow_d, mb_)
                ada_done.add(si + 1)
            t_ffn(P, C, wd[si][0], wd[si][1], gs, V(f"L{L}_shift{s}"), gh, halo=need_halo, gen=g_)
            if g_ is not None:
                for _ in g_:
                    pass
                t_ada_finish(P, C, adab_sb[L2][0][:], adab_sb[L2][1], L2, mb_)
        elif kind == "conv":
            gs = tmpvec()
            stt(P, "vector", gs, V(f"L{L}_scale1"), 1.0, V(f"L{L}_g1"), ALU.add, ALU.mult, r=[C.vecb], w=[C.vecb])
            cwv = C.vec[:, vi[f"L{L}_cw0"]:vi[f"L{L}_cw0"] + 3, :]
            t_conv(P, C, wd[si][0], wd[si][1], gs, V(f"L{L}_shift1"), V(f"L{L}_gate1"), cwv)
        elif kind == "nsa_pre":
            gs = tmpvec()
            stt(P, "vector", gs, V(f"L{L}_scale1"), 1.0, V(f"L{L}_g1"), ALU.add, ALU.mult, r=[C.vecb], w=[C.vecb])
            t_nsa_pre(P, C, wd[si][0], wd[si][1], gs, V(f"L{L}_shift1"))
        elif kind == "nsa_post":
            t_nsa_post(P, C, wd[si][0], wd[si][1], V(f"L{L}_gate1"))
    P.dma("sync", modout_d, C.vec[:, nvec + 8:nvec + 44, :], r=[C.vecb], key="modout")
    xo_v = xoT.rearrange("(k p) t -> p k t", p=128)
    P.dma("sync", xo_v, C.x[:], r=C.xall, key="xout")
    P.emit()
    return nc, names


def vec_pack(names, inputs):
    cols = []
    for n in names:
        L = int(n[1:n.index("_")])
        f = n[n.index("_") + 1:]
        if f.startswith("g"):
            v = inputs["norm_g"][L, int(f[1:])]
        elif f.startswith("cw"):
            v = inputs["conv_w"][L // 2, int(f[2:])]
        else:
            raise KeyError(n)
        cols.append(np.asarray(v, np.float32).reshape(8, 128).T)
    return np.ascontiguousarray(np.stack(cols, axis=1))


def stage_weights(stages, inputs):
    m = {}
    for si, st in enumerate(stages):
        kind, L = st[0], st[1]
        if kind == "ada":
            m[f"adaw{L}"] = inputs["ada_w"][L]
            m[f"adab{L}"] = np.ascontiguousarray(np.asarray(inputs["ada_b"][L], np.float32).reshape(72, 128).T)
        elif kind == "ffn":
            m[f"w{si}_in"] = inputs["ffn_w_in"][L, st[2]]
            m[f"w{si}_out"] = inputs["ffn_w_out"][L, st[2]]
        elif kind == "conv":
            m[f"w{si}_in"] = inputs["conv_w_in"][L // 2]
            m[f"w{si}_out"] = inputs["conv_w_out"][L // 2]
        elif kind == "nsa_pre":
            m[f"w{si}_in"] = inputs["nsa_w_in"][L // 2]
        elif kind == "nsa_post":
            m[f"w{si}_out"] = inputs["nsa_w_out"][L // 2]
    return m


def x_to_cores(xfull):
    outs = []
    for c in range(NCORE):
        b, ch = c // 4, c % 4
        t0 = ch * TOK
        xt = np.zeros((D, NT), np.float32)
        xt[:, HALO:] = xfull[b, t0:t0 + TOK].T
        if ch > 0:
            xt[:, 0:HALO] = xfull[b, t0 - HALO:t0].T
        outs.append(xt)
    return outs


def cores_to_x(xoTs):
    x = np.zeros((B, S, D), np.float32)
    for c in range(NCORE):
        b, ch = c // 4, c % 4
        x[b, ch * TOK:(ch + 1) * TOK] = xoTs[c][:, HALO:].T
    return x


_T_CACHE = {}


def run_T(stages, xfull, inputs, extra=None, modin=None):
    key = tuple(stages)
    if key not in _T_CACHE:
        _T_CACHE[key] = build_T(list(stages))
    nc, names = _T_CACHE[key]
    xs = x_to_cores(xfull)
    wm = stage_weights(stages, inputs)
    in_maps = []
    for c in range(NCORE):
        b, ch = c // 4, c % 4
        m = {"xT": xs[c], "vec": vec_pack(names, inputs),
             "modin": (modin[c] if modin is not None else np.zeros((128, 36, 8), np.float32)),
             "cT": np.ascontiguousarray(np.asarray(inputs["c"], np.float32)[b].reshape(8, 128).T),
             "halo_on": np.full((128, 1), 0.0 if ch == 0 else 1.0, np.float32)}
        m.update(wm)
        if extra is not None:
            m.update(extra[c])
        in_maps.append(m)
    res = run_bass_kernel_spmd(nc, in_maps, core_ids=list(range(NCORE)))
    return res.results


ADA_COLS = 4608


def build_ada():
    nc = bass.Bass("TRN2", target_bir_lowering=False)
    cT = nc.dram_tensor("cT", [128, NCH, B], F32, kind="ExternalInput").ap()
    w = nc.dram_tensor("w", [D, ADA_COLS], F32, kind="ExternalInput").ap()
    bias = nc.dram_tensor("bias", [1, ADA_COLS], F32, kind="ExternalInput").ap()
    out = nc.dram_tensor("mod", [B, ADA_COLS], F32, kind="ExternalOutput").ap()
    P = Prog(nc)
    ct = P.sbuf("ct", [128, NCH, B], F32)
    ctb = P.buf("ct")
    bsb = P.sbuf("bsb", [1, ADA_COLS], F32)
    bsbb = P.buf("bsb")
    ones = P.sbuf("ones1", [1, B], F32)
    onesb = P.buf("ones1")
    osb = P.sbuf("osb", [B, ADA_COLS], F32)
    osbb = P.buf("osb")
    wt = [(P.sbuf(f"wt{i}", [128, NCH, 512], F32), P.buf(f"wt{i}")) for i in range(2)]
    ps = [(P.psum(f"ps{i}", [128, 512], F32), P.buf(f"ps{i}")) for i in range(2)]
    P.dma("sync", ct[:], cT, w=[ctb])
    P.dma("sync", bsb[:], bias, w=[bsbb])
    mset(P, "vector", ones[:], 1.0, w=[onesb])
    act(P, ct[:], ct[:], AF.Silu, r=[ctb], w=[ctb])
    wv = w.rearrange("(k p) c -> p k c", p=128)
    for ci in range(ADA_COLS // 512):
        wtile, wb = wt[ci % 2]
        P.dma("sync", wtile[:], wv[:, :, ci * 512:(ci + 1) * 512], w=[wb])
        pp, ppb = ps[ci % 2]
        pairs = [(ct[:, k, :], wtile[:, k, :]) for k in range(NCH)]
        pairs.append((ones[:], bsb[:, ci * 512:(ci + 1) * 512]))
        mm_group(P, pp[0:B, :], pairs, r=[ctb, wb, onesb, bsbb], w=[ppb])
        cp(P, "vector", osb[:, ci * 512:(ci + 1) * 512], pp[0:B, :], r=[ppb], w=[osbb])
    P.dma("sync", out, osb[:], r=[osbb])
    P.emit()
    return nc


_ADA = []


def run_ada(inputs):
    if not _ADA:
        _ADA.append(build_ada())
    nc = _ADA[0]
    cT = np.ascontiguousarray(np.asarray(inputs["c"], np.float32).T.reshape(NCH, 128, B).transpose(1, 0, 2))
    in_maps = []
    for c in range(NCORE):
        L, hf = c // 2, c % 2
        in_maps.append({"cT": cT,
                        "w": np.ascontiguousarray(inputs["ada_w"][L][:, hf * ADA_COLS:(hf + 1) * ADA_COLS]),
                        "bias": np.ascontiguousarray(inputs["ada_b"][L][None, hf * ADA_COLS:(hf + 1) * ADA_COLS])})
    res = run_bass_kernel_spmd(nc, in_maps, core_ids=list(range(NCORE))).results
    mod = np.zeros((B, 4, 9 * D), np.float32)
    for c in range(NCORE):
        L, hf = c // 2, c % 2
        mod[:, L, hf * ADA_COLS:(hf + 1) * ADA_COLS] = res[c]["mod"]
    return mod.reshape(B, 4, 3, 3, D)


NQT = 64
NCMP = 511
FORCE0, FORCE1, FORCE2 = 3.0e9, 1.0e9, 2.0e9


DBG_I = None


def build_A():
    nc = bass.Bass("TRN2", target_bir_lowering=False)

    def din(name, shape):
        return nc.dram_tensor(name, list(shape), F32, kind="ExternalInput").ap()
    q_d = din("q", [NQT, 64, 512])
    kT_d = din("kT", [4, 64, S])
    v_d = din("v", [2, 128, 64, 64])
    gl_d = din("gl", [NQT, 128, 12])
    gains_d = din("gains", [64, 4])
    posT_d = din("posT", [64, 32])
    w1_d = din("w1", [2, 2048, 256])
    w2_d = din("w2", [2, 256, 64])
    rb31_d = din("rb31", [65, 4])
    toep_d = din("toep", [2, 128, 512])
    mw4_d = din("mw4", [128, 512])
    bcw_d = din("bcw", [128, 4, 1015])
    selAB_d = din("selAB", [128, 2, 256])
    ident_d = din("ident", [128, 128])
    out_d = nc.dram_tensor("oT", [NQT, 64, 512], F32, kind="ExternalOutput").ap()
    dbg_d = nc.dram_tensor("dbg", [128, 1024], F32, kind="ExternalOutput").ap() if DBG_I is not None else None

    P = Prog(nc)

    def SB(name, shape, dt):
        return P.sbuf(name, shape, dt), P.buf(name)

    stg = Rot([SB(f"stg{i}", [128, 2048], F32) for i in range(2)])
    sqr = Rot([SB(f"sq{i}", [128, 512], F32) for i in range(2)])
    ones, onesb = SB("ones", [128, 64], F32)
    identf, identfb = SB("identf", [128, 128], F32)
    identb, identbb = SB("identb", [128, 128], BF16)
    I4, I4b = SB("I4", [128, 512], BF16)
    T0b, T0bb = SB("T0b", [128, 512], BF16)
    T1b, T1bb = SB("T1b", [128, 512], BF16)
    Mw4, Mw4b = SB("Mw4", [128, 512], BF16)
    selAB, selABb = SB("selAB_sb", [128, 2, 256], F32)
    gains, gainsb = SB("gains_sb", [64, 8], F32)
    rb31, rb31b = SB("rb31_sb", [65, 4], F32)
    posf, posfb = SB("posf", [64, 32], F32)
    posTb, posTbb = SB("posTb", [64, 32], BF16)
    KsT, KsTb = SB("KsT", [65, S], BF16)
    KwT, KwTb = SB("KwT", [65, S], BF16)
    Vs, Vsb = SB("Vs", [128, 64, 65], BF16)
    Vw, Vwb = SB("Vw", [128, 64, 65], BF16)
    KcT, KcTb = SB("KcT", [64, 512], BF16)
    Vc, Vcb = SB("Vc", [128, 4, 65], BF16)
    big1, big1b = SB("big1", [128, S], BF16)
    big2, big2b = SB("big2", [128, 4096], F32)
    w2b, w2bb = SB("w2b", [128, 2, 64], BF16)
    c1, c1b = SB("c1", [128, 2], F32)
    xg, xgb = SB("xg", [128, 512], F32)
    tg, tgb = SB("tg", [128, 512], F32)
    ge, geb = SB("ge", [128, 2, 512], BF16)
    kcf, kcfb = SB("kcf", [64, 512], F32)

    psS = Rot([(P.psum(f"psS{i}", [128, 512], F32), P.buf(f"psS{i}")) for i in range(2)])
    psC = Rot([(P.psum(f"psC{i}", [128, 512], F32), P.buf(f"psC{i}")) for i in range(2)])
    pOc, pOcb = P.psum("pOc", [128, 512], F32), P.buf("pOc")
    pOs, pOsb = P.psum("pOs", [128, 512], F32), P.buf("pOs")
    pOw, pOwb = P.psum("pOw", [128, 512], F32), P.buf("pOw")
    pT, pTb = P.psum("pT", [128, 512], F32), P.buf("pT")

    mset(P, "vector", ones[:], 1.0, w=[onesb])
    P.dma("sync", identf[:], ident_d, w=[identfb])
    cp(P, "vector", identb[:], identf[:], r=[identfb], w=[identbb])
    for h in range(4):
        cp(P, "vector", I4[:, h * 128:(h + 1) * 128], identf[:], r=[identfb], w=[I4b])
    P.dma("sync", selAB[:], selAB_d, w=[selABb])
    P.dma("sync", gains[:, 0:4], gains_d, w=[gainsb])
    ts(P, "vector", gains[:, 4:5], gains[:, 0:1], 0.125, None, ALU.mult, None, r=[gainsb], w=[gainsb])
    P.dma("sync", rb31[:], rb31_d, w=[rb31b])
    P.dma("sync", posf[:], posT_d, w=[posfb])
    cp(P, "vector", posTb[:], posf[:], r=[posfb], w=[posTbb])

    lc_n = [0]

    def load_cast(src, dst, parts, n, w, view=None):
        st, sb = stg.next()
        sv = st[0:parts, 0:n]
        if view is not None:
            sv = sv.rearrange(view[0], **view[1])
        P.dma("sync", sv, src, w=[sb])
        lc_n[0] += 1
        cp(P, "gpsimd" if lc_n[0] % 2 == 0 else "vector", dst, sv, r=[sb], w=w)

    load_cast(toep_d[0], T0b[:], 128, 512, [T0bb])
    load_cast(toep_d[1], T1b[:], 128, 512, [T1bb])
    load_cast(mw4_d, Mw4[:], 128, 512, [Mw4b])
    mset(P, "vector", KsT[64:65, :], 1.0, w=[KsTb])
    mset(P, "vector", KwT[64:65, :], 1.0, w=[KwTb])
    mset(P, "gpsimd", Vs[:], 1.0, w=[Vsb])
    mset(P, "gpsimd", Vw[:], 1.0, w=[Vwb])
    mset(P, "gpsimd", Vc[:], 0.0, w=[Vcb])
    mset(P, "gpsimd", ge[:], 0.0, w=[geb])
    mset(P, "gpsimd", KcT[:], 0.0, w=[KcTb])

    def rms_scale(src, srcb, n, gcol, dst, dstb, parts=64):
        sq, sqb = sqr.next()
        act(P, sq[0:parts, 0:n], src, AF.Square, r=[srcb], w=[sqb])
        pp, ppb = psC.next()
        mm_group(P, pp[0:parts, 0:n], [(ones[0:parts, 0:parts], sq[0:parts, 0:n])], r=[onesb, sqb], w=[ppb])
        ts(P, "vector", sq[0:parts, 0:n], pp[0:parts, 0:n], 1.0 / 64, EPS, ALU.mult, ALU.add, r=[ppb], w=[sqb])
        act(P, sq[0:parts, 0:n], sq[0:parts, 0:n], AF.Ln, r=[sqb], w=[sqb])
        act(P, sq[0:parts, 0:n], sq[0:parts, 0:n], AF.Exp, r=[sqb], w=[sqb], scale=-0.5)
        stt(P, "vector", dst, src, gains[0:parts, gcol:gcol + 1], sq[0:parts, 0:n], ALU.mult, ALU.mult,
            r=[srcb, gainsb, sqb], w=[dstb])

    for kidx, gcol, dT, dTb in ((2, 2, KsT, KsTb), (3, 3, KwT, KwTb)):
        for c4 in range(4):
            st, sb = stg.next()
            P.dma("sync", st[0:64, :], kT_d[kidx, :, c4 * 2048:(c4 + 1) * 2048], w=[sb])
            for c in range(4):
                col = c * 512
                rms_scale(st[0:64, col:col + 512], sb, 512, gcol, dT[0:64, c4 * 2048 + col:c4 * 2048 + col + 512], dTb)
    for vi_, (Vt, Vtb) in enumerate(((Vs, Vsb), (Vw, Vwb))):
        for j2 in range(2):
            load_cast(v_d[vi_, :, j2 * 32:(j2 + 1) * 32, :], Vt[:, j2 * 32:(j2 + 1) * 32, 0:64], 128, 2048, [Vtb],
                      view=("p (j d) -> p j d", dict(j=32)))

    kcv = big1[0:64, :]
    kcv3 = kcv.rearrange("p (n r) -> p n r", r=16)
    w1b = big2[0:64, :].bitcast(BF16)[:, 0:8192].rearrange("p (l c) -> p l c", l=32)
    for widx, kidx in ((0, 0), (1, 1)):
        for c4 in range(4):
            load_cast(kT_d[kidx, :, c4 * 2048:(c4 + 1) * 2048], kcv[:, c4 * 2048:(c4 + 1) * 2048], 64, 2048, [big1b])
        w1v = w1_d[widx].rearrange("(l d) c -> d l c", d=64)
        for l8 in range(4):
            load_cast(w1v[:, l8 * 8:(l8 + 1) * 8, :], w1b[:, l8 * 8:(l8 + 1) * 8, :], 64, 2048, [big2b],
                      view=("p (l c) -> p l c", dict(l=8)))
        load_cast(w2_d[widx].rearrange("(h p) d -> p h d", p=128), w2b[:], 128, 128, [w2bb],
                  view=("p (h d) -> p h d", dict(h=2)))
        for half in range(2):
            pp, ppb = psC.next()
            mm_group(P, pp[:, 0:1], [(w1b[:, l, half * 128:(half + 1) * 128], posTb[:, l:l + 1]) for l in range(32)],
                     r=[big2b, posTbb], w=[ppb])
            cp(P, "vector", c1[:, half:half + 1], pp[:, 0:1], r=[ppb], w=[c1b])
        for half in range(2):
            pp, ppb = psC.next()
            pairs = []
            for l in range(32):
                a_, r_ = l // 16, l % 16
                pairs.append((w1b[:, l, half * 128:(half + 1) * 128], kcv3[:, a_:a_ + NCMP, r_]))
            mm_group(P, pp[:, 0:NCMP], pairs, r=[big2b, big1b], w=[ppb])
            act(P, xg[:, 0:NCMP], pp[:, 0:NCMP], AF.Identity, r=[ppb, c1b], w=[xgb], bias=c1[:, half:half + 1])
            tt(P, "vector", tg[:, 0:NCMP], xg[:, 0:NCMP], xg[:, 0:NCMP], ALU.mult, r=[xgb], w=[tgb])
            ts(P, "vector", tg[:, 0:NCMP], tg[:, 0:NCMP], 0.044715, 1.0, ALU.mult, ALU.add, r=[tgb], w=[tgb])
            tt(P, "vector", tg[:, 0:NCMP], tg[:, 0:NCMP], xg[:, 0:NCMP], ALU.mult, r=[tgb, xgb], w=[tgb])
            act(P, tg[:, 0:NCMP], tg[:, 0:NCMP], AF.Sigmoid, r=[tgb], w=[tgb], scale=1.5957691216057308)
            tt(P, "vector", ge[:, half, 0:NCMP], xg[:, 0:NCMP], tg[:, 0:NCMP], ALU.mult, r=[xgb, tgb], w=[geb])
        if widx == 0:
            pp, ppb = psC.next()
            mm_group(P, pp[0:64, 0:NCMP], [(w2b[:, half, :], ge[:, half, 0:NCMP]) for half in range(2)],
                     r=[w2bb, geb], w=[ppb])
            cp(P, "vector", kcf[:, 0:NCMP], pp[0:64, 0:NCMP], r=[ppb], w=[kcfb])
            rms_scale(kcf[:, 0:NCMP], kcfb, NCMP, 1, KcT[:, 0:NCMP], KcTb)
        else:
            for nt in range(4):
                pp, ppb = psC.next()
                mm_group(P, pp[:, 0:64], [(ge[:, half, nt * 128:(nt + 1) * 128], w2b[:, half, :]) for half in range(2)],
                         r=[w2bb, geb], w=[ppb])
                cp(P, "vector", Vc[:, nt, 0:64], pp[:, 0:64], r=[ppb], w=[Vcb])
        if widx == 1:
            mset(P, "gpsimd", Vc[:, :, 64:65], 1.0, w=[Vcb])

    bcw = big2[:, :].rearrange("p (h m) -> p h m", h=4)
    P.dma("sync", bcw[:, :, 0:1015], bcw_d, w=[big2b])
    maskexp = big1
    qrawr = Rot([SB(f"qraw{i}", [64, 512], F32) for i in range(2)])
    QTr = Rot([SB(f"QT{i}", [65, 512], BF16) for i in range(2)])
    glr = Rot([SB(f"glr{i}", [128, 12], F32) for i in range(2)])
    eg, egb = SB("eg", [128, 12], F32)
    dm, dmb = SB("dm", [128, 12], F32)
    drow, drowb = SB("drow", [65, 1024], F32)
    I4f, I4fb = SB("I4f", [128, 512], F32)
    Fr = Rot([SB(f"Ft{i}", [128, 512], F32) for i in range(2)])
    for h in range(4):
        cp(P, "vector", I4f[:, h * 128:(h + 1) * 128], identf[:], r=[identfb], w=[I4fb])
    scr_ = Rot([SB(f"sc{i}", [128, 512], F32) for i in range(2)])
    ec, ecb = SB("ec", [128, 4, 512], BF16)
    ps1, ps1b = SB("psum1", [128, 520], F32)
    den, denb = SB("den", [128, 8], F32)
    imp, impb = SB("imp", [128, 128], F32)
    imp2, imp2b = SB("imp2", [128, 128], F32)
    mx8, mx8b = SB("mx8", [128, 16], F32)
    thr, thrb = SB("thr", [128, 1], F32)
    PTr = Rot([SB(f"PT{i}", [128, 512], BF16) for i in range(4)])
    ETr = Rot([SB(f"ET{i}", [128, 512], BF16) for i in range(2)])
    fbr = Rot([SB(f"fb{i}", [64, 512], F32) for i in range(2)])
    oaccr = Rot([SB(f"oacc{i}", [64, 512], F32) for i in range(2)])
    otmp, otmpb = SB("otmp", [64, 512], F32)
    mset(P, "gpsimd", ec[:], 0.0, w=[ecb])
    mset(P, "gpsimd", ps1[:], 0.0, w=[ps1b])
    for (QT, QTb) in QTr.items:
        for h in range(4):
            cp(P, "vector", QT[64:65, h * 128:(h + 1) * 128], rb31[64:65, h:h + 1].to_broadcast([1, 128]), r=[rb31b], w=[QTb])

    mask2, mask2b = SB("mask2", [128, S], BF16)
    masks = [(big1, big1b), (mask2, mask2b)]
    ec2, ec2b = SB("ec2", [128, 4, 512], BF16)
    mset(P, "gpsimd", ec2[:], 0.0, w=[ec2b])
    ecs = [(ec, ecb), (ec2, ec2b)]
    den2, den2b = SB("den2", [128, 8], F32)
    dens = [(den, denb), (den2, den2b)]
    occs = [SB(f"occ{k}", [65, 512], F32) for k in range(2)]
    psS3 = Rot(psS.items + [(pT, pTb)])
    osss = [SB(f"oss{k}", [65, 512], F32) for k in range(2)]
    owss = [SB(f"ows{k}", [65, 512], F32) for k in range(2)]
    state = {}

    def preamble(i):
        st = {}
        state[i] = st
        qraw, qrb = qrawr.next()
        P.dma("sync", qraw[:], q_d[i], w=[qrb])
        gl_, glb = glr.next()
        P.dma("sync", gl_[:], gl_d[i], w=[glb])
        QT, QTb = QTr.next()
        st["QT"] = (QT, QTb)
        st["gl"] = (gl_, glb)
        den_, denb_ = dens[i % 2]
        st["den"] = (den_, denb_)
        ec_, ecb_ = ecs[i % 2]
        maskexp, maskb = masks[i % 2]
        st["mask"] = (maskexp, maskb)
        occ, occb = occs[i % 2]
        st["occ"] = (occ, occb)
        rms_scale(qraw[:], qrb, 512, 4, QT[0:64, :], QTb)
        yield
        nvis = min(NCMP, 8 * i + 7)
        m0 = 504 - 8 * i
        mset(P, "vector", den_[:], 0.0, w=[denb_])
        for h in range(4):
            pp, ppb = psC.next()
            mm_group(P, pp[:, 0:nvis], [(QT[0:64, h * 128:(h + 1) * 128], KcT[0:64, 0:nvis])], r=[QTb, KcTb], w=[ppb])
            sc, scb = scr_.next()
            tt(P, "vector", sc[:, 0:nvis], pp[:, 0:nvis], bcw[:, h, m0:m0 + nvis], ALU.add, r=[ppb, big2b], w=[scb])
            yield
            act(P, sc[:, 0:nvis], sc[:, 0:nvis], AF.Exp, r=[scb], w=[scb, denb_], accum=den_[:, h:h + 1])
            yield
            ts(P, "vector", den_[:, 4 + h:5 + h], den_[:, h:h + 1], 1e-30, None, ALU.max, None, r=[denb_], w=[denb_])
            P.op("vector", lambda e, h=h, den_=den_: e.reciprocal(out=den_[:, 4 + h:5 + h], in_=den_[:, 4 + h:5 + h]), r=[denb_], w=[denb_])
            if h == 0:
                ts(P, "vector", ps1[:, 1:1 + nvis], sc[:, 0:nvis], den_[:, 4:5], None, ALU.mult, None, r=[scb, denb_], w=[ps1b])
            else:
                stt(P, "vector", ps1[:, 1:1 + nvis], sc[:, 0:nvis], den_[:, 4 + h:5 + h], ps1[:, 1:1 + nvis], ALU.mult, ALU.add,
                    r=[scb, denb_, ps1b], w=[ps1b])
            cp(P, "gpsimd", ec_[:, h, 0:nvis], sc[:, 0:nvis], r=[scb], w=[ecb_])
            yield
        P.op("vector", lambda e: e.tensor_reduce(out=imp[:], in_=ps1[:, 0:512].rearrange("p (s r) -> p s r", r=4),
                                                 axis=AX.X, op=ALU.add), r=[ps1b], w=[impb])
        tt(P, "vector", imp[:], imp[:], ps1[:, 4:516].rearrange("p (s r) -> p s r", r=4)[:, :, 0], ALU.add, r=[impb, ps1b], w=[impb])
        w0 = 126 - 2 * i
        tt(P, "vector", imp[:], imp[:], selAB[:, 0, w0:w0 + 128], ALU.mult, r=[impb, selABb], w=[impb])
        yield
        tt(P, "vector", imp[:], imp[:], selAB[:, 1, w0:w0 + 128], ALU.add, r=[impb, selABb], w=[impb])
        mset(P, "vector", imp[:, 0:1], FORCE0, w=[impb])
        yield
        P.op("vector", lambda e: e.max(out=mx8[:, 0:8], in_=imp[:]), r=[impb], w=[mx8b])
        yield
        P.op("vector", lambda e: e.match_replace(out=imp2[:], in_to_replace=mx8[:, 0:8], in_values=imp[:], imm_value=-2.0),
             r=[impb, mx8b], w=[imp2b])
        yield
        P.op("vector", lambda e: e.max(out=mx8[:, 8:16], in_=imp2[:]), r=[imp2b], w=[mx8b])
        yield
        P.op("vector", lambda e: e.tensor_reduce(out=thr[:], in_=mx8[:, 8:16], axis=AX.X, op=ALU.min), r=[mx8b], w=[thrb])
        yield
        nblk = 2 * (i + 1)
        half = max(2, (nblk // 2) // 2 * 2)
        for (b0, b1) in ((0, half), (half, nblk)):
            if b1 <= b0:
                continue
            ts(P, "gpsimd" if False else "vector", maskexp[:, b0 * 64:b1 * 64].rearrange("p (s k) -> p s k", k=64),
               imp[:, b0:b1].unsqueeze(2).to_broadcast([128, b1 - b0, 64]), thr[:, 0:1], NEG, ALU.is_lt, ALU.mult,
               r=[impb, thrb], w=[maskb])
            yield
        ntile = (nvis + 127) // 128
        pO_, pOb_ = pOc, pOcb
        for nt in range(ntile):
            pTc, pTcb = psC.next()
            pTv = pTc[:].bitcast(BF16)

            def trs(e, nt=nt, ec_=ec_, pTv=pTv):
                ins = None
                for h in range(4):
                    ins = e.transpose(out=pTv[:, h * 128:(h + 1) * 128], in_=ec_[:, h, nt * 128:(nt + 1) * 128], identity=identb[:])
                return ins
            P.op("tensor", trs, r=[ecb_, identbb], w=[pTcb])
            yield
            et, etb = ETr.next()
            cp(P, "vector", et[:], pTv[:, 0:512], r=[pTcb], w=[etb])
            yield
            P.op("tensor", lambda e, nt=nt, et=et, ntile=ntile, pO_=pO_: e.matmul(pO_[0:65, :], Vc[:, nt, :], et[:], start=(nt == 0), stop=(nt == ntile - 1)),
                 r=[Vcb, etb], w=[pOb_])
            yield
        cp(P, "vector", occ[:], pO_[0:65, :], r=[pOb_], w=[occb])
        yield

    def postamble(i):
        st = state[i]
        gl_, glb = st["gl"]
        den_, denb_ = st["den"]
        occ, occb = st["occ"]
        oss, ossb = osss[i % 2]
        ows, owsb = owss[i % 2]
        act(P, eg[:], gl_[:], AF.Exp, r=[glb], w=[egb], scale=-1.0)
        pd, pdb = psC.next()

        def dcols(e, pd=pd, oss=oss, ows=ows):
            ins = None
            for bi, src in enumerate((oss, ows)):
                for h in range(4):
                    c_ = 4 + bi * 4 + h
                    ins = e.transpose(out=pd[:, c_:c_ + 1], in_=src[64:65, h * 128:(h + 1) * 128], identity=ones[64:65, 0:1])
            return ins
        P.op("tensor", dcols, r=[ossb, owsb, onesb], w=[pdb])
        yield
        ts(P, "vector", dm[:, 0:4], den_[:, 0:4], 1e-30, None, ALU.max, None, r=[denb_], w=[dmb])
        ts(P, "vector", dm[:, 4:12], pd[:, 4:12], 1e-30, None, ALU.max, None, r=[pdb], w=[dmb])
        yield
        stt(P, "vector", dm[:], eg[:], 1.0, dm[:], ALU.add, ALU.mult, r=[egb, dmb], w=[dmb])
        yield
        P.op("vector", lambda e: e.reciprocal(out=dm[:], in_=dm[:]), r=[dmb], w=[dmb])
        yield
        oacc, oaccb = oaccr.next()
        for br, (pO, pOb) in enumerate(((occ, occb), (oss, ossb), (ows, owsb))):
            Ft, Ftb = Fr.next()
            tt(P, "vector", Ft[:].rearrange("p (h q) -> p h q", h=4), I4f[:].rearrange("p (h q) -> p h q", h=4),
               dm[:, br * 4:(br + 1) * 4].unsqueeze(2).to_broadcast([128, 4, 128]), ALU.mult, r=[I4fb, dmb], w=[Ftb])
            yield
            pb_, pbb_ = psC.next()
            mm_group(P, pb_[0:64, :], [(ones[:, 0:64], Ft[:])], r=[onesb, Ftb], w=[pbb_])
            yield
            if br == 0:
                tt(P, "vector", oacc[:], pO[0:64, :], pb_[0:64, :], ALU.mult, r=[pOb, pbb_], w=[oaccb])
            else:
                tt(P, "vector", otmp[:], pO[0:64, :], pb_[0:64, :], ALU.mult, r=[pOb, pbb_], w=[otmpb])
                tt(P, "gpsimd", oacc[:], oacc[:], otmp[:], ALU.add, r=[oaccb, otmpb], w=[oaccb])
            yield
        P.dma("sync", out_d[i], oacc[:], r=[oaccb], key=oaccb.name + "_o")
        yield

    def chain2(g1, g2):
        if g1 is not None:
            for _ in g1:
                yield
        if g2 is not None:
            for _ in g2:
                yield

    def run_all(g):
        for _ in g:
            pass

    def step(g):
        if g is not None:
            next(g, None)

    run_all(preamble(0))
    for i in range(NQT):
        st = state[i]
        QT, QTb = st["QT"]
        maskexp, maskb = st["mask"]
        gen = chain2(postamble(i - 1) if i >= 1 else None, preamble(i + 1) if i + 1 < NQT else None)
        nkt = (i + 1) + min(5, i + 1)
        nstep = max(1, -(-40 // nkt))

        def branch(KT, KTb, Vt, Vtb, pO, pOb, j0, masked):
            def Sc(j):
                dj = i - j
                kk = 64 if dj <= 1 else 65
                ps, psb = psS3.next()
                pairs = [(KT[0:kk, j * 128:(j + 1) * 128], QT[0:kk, :])]
                rr = [KTb, QTb]
                if masked:
                    pairs.append((maskexp[:, j * 128:(j + 1) * 128], I4[:]))
                    rr += [maskb, I4b]
                if dj == 0:
                    pairs.append((identb[:], T0b[:]))
                    rr += [identbb, T0bb]
                elif dj == 1:
                    pairs.append((identb[:], T1b[:]))
                    rr += [identbb, T1bb]
                elif dj == 4 and not masked:
                    pairs.append((identb[:], Mw4[:]))
                    rr += [identbb, Mw4b]
                mm_group(P, ps[:], pairs, r=rr, w=[psb])
                return ps, psb
            q_ = [Sc(j) for j in range(j0, min(j0 + 2, i + 1))]
            for j in range(j0, i + 1):
                if j + 2 <= i:
                    q_.append(Sc(j + 2))
                ps, psb = q_.pop(0)
                pt, ptb = PTr.next()
                act(P, pt[:], ps[:], AF.Exp, r=[psb], w=[ptb])
                P.op("tensor", lambda e, j=j, pt=pt, i=i: e.matmul(pO[0:65, :], Vt[:, j, :], pt[:], start=(j == j0), stop=(j == i)),
                     r=[Vtb, ptb], w=[pOb])
                for _ in range(nstep):
                    step(gen)
        branch(KwT, KwTb, Vw, Vwb, pOw, pOwb, max(0, i - 4), False)
        branch(KsT, KsTb, Vs, Vsb, pOs, pOsb, 0, True)

        oss, ossb = osss[i % 2]
        ows, owsb = owss[i % 2]
        cp(P, "vector", oss[:], pOs[0:65, :], r=[pOsb], w=[ossb])
        cp(P, "vector", ows[:], pOw[0:65, :], r=[pOwb], w=[owsb])
        run_all(gen)
    run_all(postamble(NQT - 1))

    P.emit()
    return nc


def _t5_bucket_np(dist):
    n = np.maximum(dist, 0)
    nf = np.maximum(n, 1).astype(np.float32)
    large = 16 + (np.log(nf / np.float32(16)) / np.float32(np.log(8.0)) * np.float32(16)).astype(np.int32)
    large = np.minimum(large, 31)
    return np.where(n < 16, n, large)


def a_tables(rel_bias, g):
    rb = np.asarray(rel_bias, np.float32)[:, g * 4:(g + 1) * 4]
    k = np.arange(128)[:, None]
    q = np.arange(128)[None, :]
    d0 = q - k
    T0 = np.where((d0 >= 0)[:, None, :], rb[_t5_bucket_np(d0)].transpose(0, 2, 1), np.float32(NEG))
    T1 = rb[_t5_bucket_np(128 + d0)].transpose(0, 2, 1)
    toep = np.stack([T0.reshape(128, 512), T1.reshape(128, 512)]).astype(np.float32)
    mw4 = np.where(q < k, np.float32(0), np.float32(NEG))[:, None, :].repeat(4, axis=1).reshape(128, 512).astype(np.float32)
    m = np.arange(1015)[None, :]
    tq = np.arange(128)[:, None]
    dc = tq - 16 * (m - 504) - 31
    bcw = np.where((dc >= 0)[:, None, :], rb[_t5_bucket_np(dc)].transpose(0, 2, 1), np.float32(NEG)).astype(np.float32)
    w = np.arange(256)[None, :]
    srel = w - 126
    c = (tq >= 64).astype(np.int64)
    A = np.ones((128, 256), np.float32)
    Bv = np.zeros((128, 256), np.float32)
    fut = srel > c
    A[fut] = 0.0
    Bv[fut] = -1.0
    f1 = srel == c
    A[f1] = 0.0
    Bv[f1] = FORCE1
    f2 = srel == c - 1
    A[f2] = 0.0
    Bv[f2] = FORCE2
    selAB = np.stack([A, Bv], axis=1).astype(np.float32)
    rb31 = np.zeros((65, 4), np.float32)
    rb31[64] = rb[31]
    return dict(toep=np.ascontiguousarray(toep), mw4=np.ascontiguousarray(mw4), bcw=np.ascontiguousarray(bcw),
                selAB=np.ascontiguousarray(selAB), rb31=rb31, ident=np.eye(128, dtype=np.float32))


def a_inputs(proj_full, inputs, j):
    in_maps = []
    for c in range(NCORE):
        b, g = c // 4, c % 4
        pf = proj_full[b]
        Qg = pf[g * 256:(g + 1) * 256].reshape(4, 64, NQT, 128)
        m = {"q": np.ascontiguousarray(Qg.transpose(2, 1, 0, 3).reshape(NQT, 64, 512))}
        kts = []
        for idx in (0, 1, 2, 4):
            r0 = 1024 + idx * 256 + g * 64
            kts.append(pf[r0:r0 + 64])
        m["kT"] = np.ascontiguousarray(np.stack(kts))
        vts = []
        for idx in (3, 5):
            r0 = 1024 + idx * 256 + g * 64
            vts.append(pf[r0:r0 + 64].reshape(64, 64, 128).transpose(2, 1, 0))
        m["v"] = np.ascontiguousarray(np.stack(vts))
        G = pf[2560 + g * 12:2560 + (g + 1) * 12].reshape(4, 3, NQT, 128)
        m["gl"] = np.ascontiguousarray(G.transpose(2, 3, 1, 0).reshape(NQT, 128, 12))
        m["gains"] = np.ascontiguousarray(np.stack([inputs["nsa_q_gain"][j], inputs["nsa_k_gain"][j, 0],
                                                    inputs["nsa_k_gain"][j, 1], inputs["nsa_k_gain"][j, 2]], axis=1).astype(np.float32))
        m["posT"] = np.ascontiguousarray(np.asarray(inputs["nsa_cmp_pos"][j], np.float32).T)
        m["w1"] = np.ascontiguousarray(inputs["nsa_cmp_w1"][j])
        m["w2"] = np.ascontiguousarray(inputs["nsa_cmp_w2"][j])
        m.update(a_tables(inputs["rel_bias"], g))
        in_maps.append(m)
    return in_maps


_A = []


def run_A(proj_full, inputs, j):
    if not _A:
        _A.append(build_A())
    res = run_bass_kernel_spmd(_A[0], a_inputs(proj_full, inputs, j), core_ids=list(range(NCORE))).results
    o_full = [np.zeros((D, S), np.float32) for _ in range(B)]
    for c in range(NCORE):
        b, g = c // 4, c % 4
        o = res[c]["oT"].reshape(NQT, 64, 4, 128).transpose(2, 1, 0, 3).reshape(256, S)
        o_full[b][g * 256:(g + 1) * 256] = o
    return o_full


def _o_to_cores(o_full):
    outs = []
    for c in range(NCORE):
        b, ch = c // 4, c % 4
        t0 = ch * TOK
        ot = np.zeros((D, NT), np.float32)
        ot[:, HALO:] = o_full[b][:, t0:t0 + TOK]
        if ch > 0:
            ot[:, 0:HALO] = o_full[b][:, t0 - HALO:t0]
        outs.append(ot)
    return outs


def _proj_full(res, si):
    pf = []
    for b in range(B):
        pf.append(np.ascontiguousarray(np.concatenate([res[b * 4 + ch][f"projT{si}"] for ch in range(4)], axis=1)))
    return pf


def kernel(**inputs):
    inputs = {k: np.asarray(v) for k, v in inputs.items()}
    x = np.asarray(inputs["x"], np.float32)
    st1 = (("ada", 0), ("ffn", 0, 0), ("conv", 0), ("ffn", 0, 1), ("ada", 1), ("ffn", 1, 0), ("nsa_pre", 1))
    res = run_T(st1, x, inputs)
    x = cores_to_x([r["xoT"] for r in res])
    o_full = run_A(_proj_full(res, 6), inputs, 0)
    st2 = (("nsa_post", 1), ("ffn", 1, 1), ("ada", 2), ("ffn", 2, 0), ("conv", 2), ("ffn", 2, 1),
           ("ada", 3), ("ffn", 3, 0), ("nsa_pre", 3))
    oc = _o_to_cores(o_full)
    mods = [r["modout"] for r in res]
    res = run_T(st2, x, inputs, extra=[{"oT0": oc[c]} for c in range(NCORE)], modin=mods)
    x = cores_to_x([r["xoT"] for r in res])
    o_full = run_A(_proj_full(res, 8), inputs, 1)
    st3 = (("nsa_post", 3), ("ffn", 3, 1))
    oc = _o_to_cores(o_full)
    mods = [r["modout"] for r in res]
    res = run_T(st3, x, inputs, extra=[{"oT0": oc[c]} for c in range(NCORE)], modin=mods)
    x = cores_to_x([r["xoT"] for r in res])
    return x.astype(np.float32)
```

```python
import numpy as np
from contextlib import ExitStack
import concourse.bass as bass
import concourse.mybir as mybir
from concourse.bass_utils import run_bass_kernel_spmd

F32 = mybir.dt.float32
BF16 = mybir.dt.bfloat16
ALU = mybir.AluOpType
AF = mybir.ActivationFunctionType
AX = mybir.AxisListType

D = 1024
DFF = 2816
NCH = 8
NJ = 22
B = 2
S = 8192
NCORE = 8
TOK = 2048
EPS = 1e-6
NSA_IN = 2608
NEG = -30000.0


class Buf:
    __slots__ = ("name", "last_w", "readers")

    def __init__(self, name):
        self.name = name
        self.last_w = None
        self.readers = []


class Op:
    __slots__ = ("eng", "fn", "deps", "dma", "key", "signal", "count", "kcount")

    def __init__(self, eng, fn, dma, key):
        self.eng = eng
        self.fn = fn
        self.dma = dma
        self.key = key
        self.deps = []
        self.signal = False
        self.count = 0
        self.kcount = 0


ENGS = ["tensor", "vector", "scalar", "gpsimd", "sync"]


class Prog:
    def __init__(self, nc):
        self.nc = nc
        self.q = {e: [] for e in ENGS}
        self.stack = ExitStack()
        self.nbuf = 0

    def buf(self, name=None):
        self.nbuf += 1
        return Buf(name or f"b{self.nbuf}")

    def sbuf(self, name, shape, dt):
        return self.stack.enter_context(self.nc.sbuf_tensor(name, list(shape), dt))

    def psum(self, name, shape, dt):
        return self.stack.enter_context(self.nc.psum_tensor(name, list(shape), dt))

    def op(self, eng, fn, r=(), w=(), dma=False, key=None):
        if dma and key is None:
            key = (w[0].name if len(w) else r[0].name)
        o = Op(eng, fn, dma, key)
        deps = set()
        raw = set()
        for b in r:
            if b.last_w is not None:
                deps.add(b.last_w)
                raw.add(b.last_w)
        for b in w:
            if b.last_w is not None:
                deps.add(b.last_w)
            for rd in b.readers:
                deps.add(rd)
        for d in deps:
            if d is o:
                continue
            if d.dma or dma or d.eng != eng or (d in raw and eng != "tensor"):
                o.deps.append(d)
        for b in r:
            b.readers.append(o)
        for b in w:
            b.last_w = o
            b.readers = []
        self.q[eng].append(o)
        return o

    def dma(self, eng, out, in_, r=(), w=(), key=None, **kw):
        return self.op(eng, lambda e: e.dma_start(out=out, in_=in_, **kw), r=r, w=w, dma=True, key=key)

    def emit(self):
        nc = self.nc
        for e in ENGS:
            for o in self.q[e]:
                for d in o.deps:
                    d.signal = True
        keytot = {}
        for e in ENGS:
            cnt = 0
            for o in self.q[e]:
                if o.dma:
                    keytot[o.key] = keytot.get(o.key, 0) + 16
                    o.kcount = keytot[o.key]
                elif o.signal:
                    cnt += 1
                    o.count = cnt
        st = self.stack
        esem = {e: st.enter_context(nc.semaphore("se_" + e)) for e in ENGS}
        ksem = {}
        for i, k in enumerate(keytot):
            ksem[k] = st.enter_context(nc.semaphore(f"sk{i}"))
        self.nsem = len(ksem) + len(esem)

        def run(ename, e):
            waited = {}
            for o in self.q[ename]:
                need = {}
                for d in o.deps:
                    if d.dma:
                        k = ("k", d.key)
                        v = d.kcount
                    else:
                        k = ("e", d.eng)
                        v = d.count
                    if v > need.get(k, 0):
                        need[k] = v
                for k, v in need.items():
                    if waited.get(k, 0) >= v:
                        continue
                    waited[k] = v
                    sem = ksem[k[1]] if k[0] == "k" else esem[k[1]]
                    e.wait_ge(sem, v)
                ins = o.fn(e)
                if o.dma:
                    ins.then_inc(ksem[o.key], 16)
                elif o.signal:
                    ins.then_inc(esem[ename], 1)
            if ename == "sync":
                for k, tot in keytot.items():
                    if waited.get(("k", k), 0) < tot:
                        e.wait_ge(ksem[k], tot)

        with nc.Block() as block:
            @block.tensor
            def _(e):
                run("tensor", e)

            @block.vector
            def _(e):
                run("vector", e)

            @block.scalar
            def _(e):
                run("scalar", e)

            @block.gpsimd
            def _(e):
                run("gpsimd", e)

            @block.sync
            def _(e):
                run("sync", e)
        self.stack.close()


class WStream:
    def __init__(self, P, nstg=3, nw=4, elems=2048):
        self.P = P
        self.elems = elems
        self.stg = [P.sbuf(f"wstg{i}", [128, elems], F32) for i in range(nstg)]
        self.stgb = [P.buf(f"wstg{i}") for i in range(nstg)]
        self.wt = [P.sbuf(f"wbf{i}", [128, elems], BF16) for i in range(nw)]
        self.wtb = [P.buf(f"wbf{i}") for i in range(nw)]
        self.i = 0
        self.j = 0
        self.nd = 0

    def load(self, src, shape, parts=128):
        P = self.P
        n = int(np.prod(shape[1:]))
        assert n <= self.elems
        si = self.i % len(self.stg)
        wi = self.j % len(self.wt)
        self.i += 1
        self.j += 1
        stg = self.stg[si]
        wt = self.wt[wi]
        if len(shape) == 3:
            sv = stg[0:parts, 0:n].rearrange("p (a b) -> p a b", a=shape[1])
            wv = wt[0:parts, 0:n].rearrange("p (a b) -> p a b", a=shape[1])
        else:
            sv = stg[0:parts, 0:n]
            wv = wt[0:parts, 0:n]
        deng = "sync" if (self.nd % 2 == 0) else "gpsimd"
        self.nd += 1
        P.dma("sync", sv, src, w=[self.stgb[si]])
        ceng = "gpsimd"
        P.op(ceng, lambda e: e.tensor_copy(out=wv, in_=sv), r=[self.stgb[si]], w=[self.wtb[wi]])
        return wv, self.wtb[wi]


def mm_group(P, out, pairs, r, w):
    n = len(pairs)

    def fn(e):
        ins = None
        for i, (l, rr) in enumerate(pairs):
            ins = e.matmul(out, l, rr, start=(i == 0), stop=(i == n - 1))
        return ins
    return P.op("tensor", fn, r=r, w=w)


def act(P, out, in_, func, r, w, bias=None, scale=None, accum=None):
    kw = {}
    if bias is not None:
        kw["bias"] = bias
    if scale is not None:
        kw["scale"] = scale
    if accum is not None:
        kw["accum_out"] = accum
    return P.op("scalar", lambda e: e.activation(out=out, in_=in_, func=func, **kw), r=r, w=w)


def tt(P, eng, out, in0, in1, op, r, w):
    return P.op(eng, lambda e: e.tensor_tensor(out=out, in0=in0, in1=in1, op=op), r=r, w=w)


def ts(P, eng, out, in0, s1, s2, op0, op1, r, w, accum=None):
    kw = {}
    if accum is not None:
        kw["accum_out"] = accum
    if op1 is None:
        return P.op(eng, lambda e: e.tensor_scalar(out=out, in0=in0, scalar1=s1, scalar2=None, op0=op0, **kw), r=r, w=w)
    return P.op(eng, lambda e: e.tensor_scalar(out=out, in0=in0, scalar1=s1, scalar2=s2, op0=op0, op1=op1, **kw), r=r, w=w)


def stt(P, eng, out, in0, scalar, in1, op0, op1, r, w):
    return P.op(eng, lambda e: e.scalar_tensor_tensor(out=out, in0=in0, scalar=scalar, in1=in1, op0=op0, op1=op1), r=r, w=w)


def cp(P, eng, out, in_, r, w):
    return P.op(eng, lambda e: e.tensor_copy(out=out, in_=in_), r=r, w=w)


def mset(P, eng, out, val, w):
    return P.op(eng, lambda e: e.memset(out, val), r=[], w=w)


class Rot:
    def __init__(self, items):
        self.items = items
        self.i = 0

    def next(self):
        it = self.items[self.i % len(self.items)]
        self.i += 1
        return it


HALO = 2
NT = HALO + TOK
TILES = [(0, HALO)] + [(HALO + 512 * i, 512) for i in range(4)]
FFN_PARTS = [(0, 4), (4, 4), (8, 4), (12, 4), (16, 4), (20, 2)]


class TCtx:
    pass


def t_setup(P, nvec):
    C = TCtx()
    C.x = P.sbuf("x", [128, NCH, NT], F32)
    C.xb = [[P.buf(f"x{t}_{o}") for o in range(NCH)] for t in range(len(TILES))]
    C.xall = [bb for l in C.xb for bb in l]
    C.h = P.sbuf("h", [128, NCH, NT], BF16)
    C.hb = [P.buf(f"h{t}") for t in range(len(TILES))]
    C.a = [P.sbuf(f"a{i}", [128, 4, NT], BF16) for i in range(2)]
    C.ab = [[P.buf(f"a{i}_{t}") for t in range(len(TILES))] for i in range(2)]
    C.scr = P.sbuf("scr", [128, 4104], F32)
    C.scrL = [P.buf("scrA"), P.buf("scrB")]
    C.rstd = P.sbuf("rstd", [128, 512], F32)
    C.rstdb = P.buf("rstd")
    C.sg = Rot([(P.sbuf(f"sg{i}", [128, 512], F32), P.buf(f"sg{i}")) for i in range(2)])
    C.ostg = Rot(C.sg.items + [(P.sbuf(f"ostg{i}", [128, 512], F32), P.buf(f"ostg{i}")) for i in range(3)])
    C.ones = P.sbuf("ones", [128, 128], F32)
    C.onesb = P.buf("ones")
    C.vec = P.sbuf("vec_sb", [128, nvec + 8 + 36, 8], F32)
    C.vecb = P.buf("vec")
    C.nvec = nvec
    C.ntmp = 0
    C.halo_on = P.sbuf("halo_sb", [128, 1], F32)
    C.halob = P.buf("halo_on")
    ps = [(P.psum(f"ps{i}", [128, 512], F32), P.buf(f"ps{i}")) for i in range(8)]
    C.ps_g = Rot(ps[0:2])
    C.ps_u = Rot(ps[2:4])
    C.ps_y = Rot(ps[4:7])
    C.ps_s = Rot(ps[7:8])
    C.ws = WStream(P, nstg=3, nw=6)
    mset(P, "vector", C.ones[:], 1.0, w=[C.onesb])
    return C


def t_norm(P, C, gs, shift, halo=True):
    for t, (t0, n) in enumerate(TILES):
        if t == 0 and not halo:
            continue
        sq = C.scr[:, 0:NCH * n].rearrange("p (k n) -> p k n", k=NCH)
        act(P, sq, C.x[:, :, t0:t0 + n], AF.Square, r=C.xb[t], w=C.scrL)
        ps, psb = C.ps_s.next()
        mm_group(P, ps[:, 0:n], [(C.ones[:], sq[:, k, :]) for k in range(NCH)], r=C.scrL + [C.onesb], w=[psb])
        ts(P, "vector", C.rstd[:, 0:n], ps[:, 0:n], 1.0 / D, EPS, ALU.mult, ALU.add, r=[psb], w=[C.rstdb])
        act(P, C.rstd[:, 0:n], C.rstd[:, 0:n], AF.Sqrt, r=[C.rstdb], w=[C.rstdb])
        P.op("vector", lambda e, n=n: e.reciprocal(out=C.rstd[:, 0:n], in_=C.rstd[:, 0:n]), r=[C.rstdb], w=[C.rstdb])
        for k in range(NCH):
            tt(P, "vector", sq[:, k, :], C.x[:, k, t0:t0 + n], C.rstd[:, 0:n], ALU.mult,
               r=[C.xb[t][k], C.rstdb], w=C.scrL)
        for k in range(NCH):
            act(P, C.h[:, k, t0:t0 + n], sq[:, k, :], AF.Identity, r=C.scrL + [C.vecb], w=[C.hb[t]],
                bias=shift[:, k:k + 1], scale=gs[:, k:k + 1])


def t_ffn(P, C, w_in, w_out, gs, shift, ghalf, halo=True, gen=None):
    t_norm(P, C, gs, shift, halo)
    TL = [(t, t0, n) for t, (t0, n) in enumerate(TILES) if (halo or t > 0)]
    win_v = w_in.rearrange("(k p) c -> p k c", p=128)
    wout_v = w_out.rearrange("(j p) c -> p j c", p=128)

    def phase1(pi, j0, nj):
        ab = pi % 2
        for jp in range(0, nj, 2):
            c0 = (j0 + jp) * 128
            wg, wgb = C.ws.load(win_v[:, :, c0:c0 + 256], [128, NCH, 256])
            wu, wub = C.ws.load(win_v[:, :, DFF + c0:DFF + c0 + 256], [128, NCH, 256])
            for jj in range(2):
                for t, t0, n in TL:
                    pg, pgb = C.ps_g.next()
                    pu, pub = C.ps_u.next()
                    mm_group(P, pg[:, 0:n], [(wg[:, k, jj * 128:(jj + 1) * 128], C.h[:, k, t0:t0 + n]) for k in range(NCH)],
                             r=[wgb, C.hb[t]], w=[pgb])
                    mm_group(P, pu[:, 0:n], [(wu[:, k, jj * 128:(jj + 1) * 128], C.h[:, k, t0:t0 + n]) for k in range(NCH)],
                             r=[wub, C.hb[t]], w=[pub])
                    sg, sgb = C.sg.next()
                    act(P, sg[:, 0:n], pg[:, 0:n], AF.Silu, r=[pgb], w=[sgb])
                    tt(P, "vector", C.a[ab][:, jp + jj, t0:t0 + n], sg[:, 0:n], pu[:, 0:n], ALU.mult,
                       r=[sgb, pub], w=[C.ab[ab][t]])
                    if gen is not None and t % 2 == 0:
                        next(gen, None)

    def phase2(pi, j0, nj):
        ab = pi % 2
        wos = []
        for jp in range(0, nj, 2):
            r0 = j0 + jp
            wo, wob = C.ws.load(wout_v[:, r0:r0 + 2, :], [128, 2, D])
            wos.append((wo, wob))
        for t, t0, n in TL:
            for o in range(NCH):
                py, pyb = C.ps_y.next()
                pairs = []
                for ji in range(nj):
                    wo, wob = wos[ji // 2]
                    pairs.append((wo[:, ji % 2, o * 128:(o + 1) * 128], C.a[ab][:, ji, t0:t0 + n]))
                mm_group(P, py[:, 0:n], pairs, r=[b for _, b in wos] + [C.ab[ab][t]], w=[pyb])
                stt(P, "vector", C.x[:, o, t0:t0 + n], py[:, 0:n], ghalf[:, o:o + 1], C.x[:, o, t0:t0 + n],
                    ALU.mult, ALU.add, r=[pyb, C.vecb, C.xb[t][o]], w=[C.xb[t][o]])

    prev = None
    for pi, (j0, nj) in enumerate(FFN_PARTS):
        phase1(pi, j0, nj)
        if prev is not None:
            phase2(*prev)
        prev = (pi, j0, nj)
    phase2(*prev)


def t_conv(P, C, w_in, w_out, gs, shift, gate, cw):
    t_norm(P, C, gs, shift)
    win_v = w_in.rearrange("(k p) c -> p k c", p=128)
    wout_v = w_out.rearrange("(k p) c -> p k c", p=128)
    bsb = C.scr[:, 0:NT]
    usb = C.scr[:, 2052:2052 + NT]
    for o in range(NCH):
        ws3 = []
        for part in range(3):
            c0 = part * D + o * 128
            ws3.append(C.ws.load(win_v[:, :, c0:c0 + 128], [128, NCH, 128]))
        for t, (t0, n) in enumerate(TILES):
            pb, pbb = C.ps_g.next()
            pc, pcb = C.ps_u.next()
            pv, pvb = C.ps_y.next()
            for (pp, ppb), (wv, wb) in zip([(pb, pbb), (pc, pcb), (pv, pvb)], ws3):
                mm_group(P, pp[:, 0:n], [(wv[:, k, :], C.h[:, k, t0:t0 + n]) for k in range(NCH)], r=[wb, C.hb[t]], w=[ppb])
            act(P, bsb[:, t0:t0 + n], pb[:, 0:n], AF.Identity, r=[pbb], w=[C.scrL[0]])
            sg, sgb = C.sg.next()
            act(P, sg[:, 0:n], pc[:, 0:n], AF.Identity, r=[pcb], w=[sgb])
            tt(P, "vector", usb[:, t0:t0 + n], sg[:, 0:n], pv[:, 0:n], ALU.mult, r=[sgb, pvb], w=[C.scrL[1]])
        ts(P, "vector", usb[:, 0:HALO], usb[:, 0:HALO], C.halo_on[:, 0:1], None, ALU.mult, None, r=[C.scrL[1], C.halob], w=[C.scrL[1]])
        zt = C.a[o // 4]
        for t, (t0, n) in enumerate(TILES):
            if t == 0:
                continue
            ysb, ysbb = C.sg.next()
            ts(P, "vector", ysb[:, 0:n], usb[:, t0 - 2:t0 - 2 + n], cw[:, 0, o:o + 1], None, ALU.mult, None, r=[C.scrL[1], C.vecb], w=[ysbb])
            stt(P, "vector", ysb[:, 0:n], usb[:, t0 - 1:t0 - 1 + n], cw[:, 1, o:o + 1], ysb[:, 0:n], ALU.mult, ALU.add, r=[C.scrL[1], C.vecb, ysbb], w=[ysbb])
            stt(P, "vector", ysb[:, 0:n], usb[:, t0:t0 + n], cw[:, 2, o:o + 1], ysb[:, 0:n], ALU.mult, ALU.add, r=[C.scrL[1], C.vecb, ysbb], w=[ysbb])
            tt(P, "gpsimd", zt[:, o % 4, t0:t0 + n], bsb[:, t0:t0 + n], ysb[:, 0:n], ALU.mult, r=[C.scrL[0], ysbb], w=[C.ab[o // 4][t]])
    for oo2 in range(0, NCH, 2):
        wo, wob = C.ws.load(wout_v[:, :, oo2 * 128:(oo2 + 2) * 128], [128, NCH, 256])
        for oi in range(2):
            o = oo2 + oi
            for t, (t0, n) in enumerate(TILES):
                if t == 0:
                    continue
                py, pyb = C.ps_y.next()
                mm_group(P, py[:, 0:n], [(wo[:, k, oi * 128:(oi + 1) * 128], C.a[k // 4][:, k % 4, t0:t0 + n]) for k in range(NCH)],
                         r=[wob, C.ab[0][t], C.ab[1][t]], w=[pyb])
                stt(P, "vector", C.x[:, o, t0:t0 + n], py[:, 0:n], gate[:, o:o + 1], C.x[:, o, t0:t0 + n],
                    ALU.mult, ALU.add, r=[pyb, C.vecb, C.xb[t][o]], w=[C.xb[t][o]])


def t_nsa_pre(P, C, w_in, projT, gs, shift):
    t_norm(P, C, gs, shift, False)
    win_v = w_in.rearrange("(k p) c -> p k c", p=128)
    nchunk = (NSA_IN + 127) // 128
    wcur = None
    for oc in range(nchunk):
        rows = min(128, NSA_IN - oc * 128)
        if oc % 2 == 0:
            cols = min(256, NSA_IN - oc * 128)
            wcur = C.ws.load(win_v[:, :, oc * 128:oc * 128 + cols], [128, NCH, cols])
        wv, wb = wcur
        c0 = (oc % 2) * 128
        for t, (t0, n) in enumerate(TILES):
            if t == 0:
                continue
            pp, ppb = C.ps_y.next()
            mm_group(P, pp[0:rows, 0:n], [(wv[:, k, c0:c0 + rows], C.h[:, k, t0:t0 + n]) for k in range(NCH)],
                     r=[wb, C.hb[t]], w=[ppb])
            sg, sgb = C.ostg.next()
            act(P, sg[0:rows, 0:n], pp[0:rows, 0:n], AF.Identity, r=[ppb], w=[sgb])
            P.dma("sync", projT[oc * 128:oc * 128 + rows, t0 - HALO:t0 - HALO + n], sg[0:rows, 0:n], r=[sgb], key=sgb.name + "_o")


def t_nsa_post(P, C, oT, w_out, gate):
    wout_v = w_out.rearrange("(k p) c -> p k c", p=128)
    oT_v = oT.rearrange("(k p) t -> p k t", p=128)
    for k in range(NCH):
        for hf in range(2):
            c0 = hf * (NT // 2)
            n = NT // 2
            stg = C.scr[:, hf * 2052: hf * 2052 + n]
            P.dma("sync", stg, oT_v[:, k, c0:c0 + n], w=[C.scrL[hf]])
            cp(P, "gpsimd", C.h[:, k, c0:c0 + n], stg, r=[C.scrL[hf]], w=C.hb)
    for oo2 in range(0, NCH, 2):
        wo, wob = C.ws.load(wout_v[:, :, oo2 * 128:(oo2 + 2) * 128], [128, NCH, 256])
        for oi in range(2):
            o = oo2 + oi
            for t, (t0, n) in enumerate(TILES):
                py, pyb = C.ps_y.next()
                mm_group(P, py[:, 0:n], [(wo[:, k, oi * 128:(oi + 1) * 128], C.h[:, k, t0:t0 + n]) for k in range(NCH)],
                         r=[wob, C.hb[t]], w=[pyb])
                stt(P, "vector", C.x[:, o, t0:t0 + n], py[:, 0:n], gate[:, o:o + 1], C.x[:, o, t0:t0 + n],
                    ALU.mult, ALU.add, r=[pyb, C.vecb, C.xb[t][o]], w=[C.xb[t][o]])


def vec_names(stages):
    names = []
    for st in stages:
        kind, L = st[0], st[1]
        if kind == "ffn":
            s = 0 if st[2] == 0 else 2
            names += [f"L{L}_g{s}"]
        elif kind in ("conv", "nsa_pre"):
            names += [f"L{L}_g1"]
            if kind == "conv":
                names += [f"L{L}_cw0", f"L{L}_cw1", f"L{L}_cw2"]
    out = []
    for n in names:
        if n not in out:
            out.append(n)
    if not out:
        out = ["L0_g0"]
    return out


def t_ada_gen(P, C, adaw, L, modrow_d, modb):
    wv = adaw.rearrange("(k p) c -> p k c", p=128)
    rowbufs = []
    for c4 in range(36):
        hf = c4 % 2
        stgv = C.scr[:, hf * 2052:hf * 2052 + 2048].rearrange("p (k c) -> p k c", k=NCH)
        P.dma("sync", stgv, wv[:, :, c4 * 256:(c4 + 1) * 256], w=[C.scrL[hf]])
        ps, psb = C.ps_y.next()
        mm_group(P, ps[0:1, 0:256], [(C.cond[:, k:k + 1], stgv[:, k, :]) for k in range(NCH)], r=[C.scrL[hf], C.condb], w=[psb])
        sg, sgb = C.sg.next()
        act(P, sg[0:1, 0:256], ps[0:1, 0:256], AF.Identity, r=[psb], w=[sgb])
        rb_ = P.buf(f"modrow{L}_{c4}")
        P.dma("sync", modrow_d[L:L + 1, c4 * 256:(c4 + 1) * 256], sg[0:1, 0:256], r=[sgb], w=[rb_], key=sgb.name + "_o")
        rowbufs.append(rb_)
        yield
    base = C.nvec + 8 + L * 9
    dst = C.vec[:, base:base + 9, :].rearrange("p a b -> p (a b)")
    P.dma("sync", dst, modrow_d[L].rearrange("(c p) -> p c", p=128), r=rowbufs, w=[modb], key="modT",
          allow_slow_non_contiguous=True)
    yield


def t_ada_finish(P, C, adab, adabb, L, modb):
    base = C.nvec + 8 + L * 9
    dst = C.vec[:, base:base + 9, :].rearrange("p a b -> p (a b)")
    tt(P, "vector", dst, dst, adab, ALU.add, r=[adabb, modb, C.vecb], w=[C.vecb, modb])


def build_T(stages):
    nc = bass.Bass("TRN2", target_bir_lowering=False)
    names = vec_names(stages)
    nvec = len(names)
    vi = {n: i for i, n in enumerate(names)}
    xT = nc.dram_tensor("xT", [D, NT], F32, kind="ExternalInput").ap()
    vec = nc.dram_tensor("vec", [128, nvec, 8], F32, kind="ExternalInput").ap()
    halo_on = nc.dram_tensor("halo_on", [128, 1], F32, kind="ExternalInput").ap()
    xoT = nc.dram_tensor("xoT", [D, NT], F32, kind="ExternalOutput").ap()
    wd = {}
    cT_d = nc.dram_tensor("cT", [128, NCH], F32, kind="ExternalInput").ap()
    modin_d = nc.dram_tensor("modin", [128, 36, 8], F32, kind="ExternalInput").ap()
    modout_d = nc.dram_tensor("modout", [128, 36, 8], F32, kind="ExternalOutput").ap()
    modrow_d = nc.dram_tensor("modrow", [4, 9 * D], F32).ap()
    for si, st in enumerate(stages):
        kind = st[0]
        if kind == "ada":
            wd[si] = (nc.dram_tensor(f"adaw{st[1]}", [D, 9 * D], F32, kind="ExternalInput").ap(),
                      nc.dram_tensor(f"adab{st[1]}", [128, 72], F32, kind="ExternalInput").ap())
        elif kind == "ffn":
            wd[si] = (nc.dram_tensor(f"w{si}_in", [D, 2 * DFF], F32, kind="ExternalInput").ap(),
                      nc.dram_tensor(f"w{si}_out", [DFF, D], F32, kind="ExternalInput").ap())
        elif kind == "conv":
            wd[si] = (nc.dram_tensor(f"w{si}_in", [D, 3 * D], F32, kind="ExternalInput").ap(),
                      nc.dram_tensor(f"w{si}_out", [D, D], F32, kind="ExternalInput").ap())
        elif kind == "nsa_pre":
            wd[si] = (nc.dram_tensor(f"w{si}_in", [D, NSA_IN], F32, kind="ExternalInput").ap(),
                      nc.dram_tensor(f"projT{si}", [NSA_IN, TOK], F32, kind="ExternalOutput").ap())
        elif kind == "nsa_post":
            wd[si] = (nc.dram_tensor(f"oT{si}", [D, NT], F32, kind="ExternalInput").ap(),
                      nc.dram_tensor(f"w{si}_out", [D, D], F32, kind="ExternalInput").ap())
    P = Prog(nc)
    C = t_setup(P, nvec)
    P.dma("sync", C.vec[:, 0:nvec, :], vec, w=[C.vecb])
    P.dma("sync", C.halo_on[:], halo_on, w=[C.halob])
    xT_v = xT.rearrange("(k p) t -> p k t", p=128)
    P.dma("sync", C.x[:], xT_v, w=C.xall, key="xin")

    C.cond = P.sbuf("cond", [128, NCH], F32)
    C.condb = P.buf("cond")
    P.dma("sync", C.cond[:], cT_d, w=[C.condb])
    act(P, C.cond[:], C.cond[:], AF.Silu, r=[C.condb], w=[C.condb])
    P.dma("sync", C.vec[:, nvec + 8:nvec + 44, :], modin_d, w=[C.vecb])
    adab_sb = {}
    for si, st in enumerate(stages):
        if st[0] == "ada":
            t_ = P.sbuf(f"adab_sb{st[1]}", [128, 72], F32)
            tb_ = P.buf(f"adab_sb{st[1]}")
            P.dma("sync", t_[:], wd[si][1], w=[tb_])
            adab_sb[st[1]] = (t_, tb_)

    def V(name):
        L_ = int(name[1:name.index("_")])
        f = name[name.index("_") + 1:]
        for kind_i, kn in enumerate(("shift", "scale", "gate")):
            if f.startswith(kn):
                sub = int(f[len(kn):])
                return C.vec[:, nvec + 8 + L_ * 9 + sub * 3 + kind_i, :]
        return C.vec[:, vi[name], :]

    def tmpvec():
        i = nvec + C.ntmp % 8
        C.ntmp += 1
        return C.vec[:, i, :]

    ada_done = set()
    for si, st in enumerate(stages):
        kind, L = st[0], st[1]
        if kind == "ada":
            if si in ada_done:
                continue
            mb_ = P.buf(f"modb{L}")
            for _ in t_ada_gen(P, C, wd[si][0], L, modrow_d, mb_):
                pass
            t_ada_finish(P, C, adab_sb[L][0][:], adab_sb[L][1], L, mb_)
        elif kind == "ffn":
            s = 0 if st[2] == 0 else 2
            gs = tmpvec()
            gh = tmpvec()
            stt(P, "vector", gs, V(f"L{L}_scale{s}"), 1.0, V(f"L{L}_g{s}"), ALU.add, ALU.mult, r=[C.vecb], w=[C.vecb])
            ts(P, "vector", gh, V(f"L{L}_gate{s}"), 0.5, None, ALU.mult, None, r=[C.vecb], w=[C.vecb])
            need_halo = any(st2[0] == "conv" for st2 in stages[si + 1:])
            g_ = None
            if si + 1 < len(stages) and stages[si + 1][0] == "ada":
                L2 = stages[si + 1][1]
                mb_ = P.buf(f"modb{L2}")
                g_ = t_ada_gen(P, C, wd[si + 1][0], L2, modrow_d, mb_)
                ada_done.add(si + 1)
            t_ffn(P, C, wd[si][0], wd[si][1], gs, V(f"L{L}_shift{s}"), gh, halo=need_halo, gen=g_)
            if g_ is not None:
                for _ in g_:
                    pass
                t_ada_finish(P, C, adab_sb[L2][0][:], adab_sb[L2][1], L2, mb_)
        elif kind == "conv":
            gs = tmpvec()
            stt(P, "vector", gs, V(f"L{L}_scale1"), 1.0, V(f"L{L}_g1"), ALU.add, ALU.mult, r=[C.vecb], w=[C.vecb])
            cwv = C.vec[:, vi[f"L{L}_cw0"]:vi[f"L{L}_cw0"] + 3, :]
            t_conv(P, C, wd[si][0], wd[si][1], gs, V(f"L{L}_shift1"), V(f"L{L}_gate1"), cwv)
        elif kind == "nsa_pre":
            gs = tmpvec()
            stt(P, "vector", gs, V(f"L{L}_scale1"), 1.0, V(f"L{L}_g1"), ALU.add, ALU.mult, r=[C.vecb], w=[C.vecb])
            t_nsa_pre(P, C, wd[si][0], wd[si][1], gs, V(f"L{L}_shift1"))
        elif kind == "nsa_post":
            t_nsa_post(P, C, wd[si][0], wd[si][1], V(f"L{L}_gate1"))
    P.dma("sync", modout_d, C.vec[:, nvec + 8:nvec + 44, :], r=[C.vecb], key="modout")
    xo_v = xoT.rearrange("(k p) t -> p k t", p=128)
    P.dma("sync", xo_v, C.x[:], r=C.xall, key="xout")
    P.emit()
    return nc, names


def vec_pack(names, inputs):
    cols = []
    for n in names:
        L = int(n[1:n.index("_")])
        f = n[n.index("_") + 1:]
        if f.startswith("g"):
            v = inputs["norm_g"][L, int(f[1:])]
        elif f.startswith("cw"):
            v = inputs["conv_w"][L // 2, int(f[2:])]
        else:
            raise KeyError(n)
        cols.append(np.asarray(v, np.float32).reshape(8, 128).T)
    return np.ascontiguousarray(np.stack(cols, axis=1))


def stage_weights(stages, inputs):
    m = {}
    for si, st in enumerate(stages):
        kind, L = st[0], st[1]
        if kind == "ada":
            m[f"adaw{L}"] = inputs["ada_w"][L]
            m[f"adab{L}"] = np.ascontiguousarray(np.asarray(inputs["ada_b"][L], np.float32).reshape(72, 128).T)
        elif kind == "ffn":
            m[f"w{si}_in"] = inputs["ffn_w_in"][L, st[2]]
            m[f"w{si}_out"] = inputs["ffn_w_out"][L, st[2]]
        elif kind == "conv":
            m[f"w{si}_in"] = inputs["conv_w_in"][L // 2]
            m[f"w{si}_out"] = inputs["conv_w_out"][L // 2]
        elif kind == "nsa_pre":
            m[f"w{si}_in"] = inputs["nsa_w_in"][L // 2]
        elif kind == "nsa_post":
            m[f"w{si}_out"] = inputs["nsa_w_out"][L // 2]
    return m


def x_to_cores(xfull):
    outs = []
    for c in range(NCORE):
        b, ch = c // 4, c % 4
        t0 = ch * TOK
        xt = np.zeros((D, NT), np.float32)
        xt[:, HALO:] = xfull[b, t0:t0 + TOK].T
        if ch > 0:
            xt[:, 0:HALO] = xfull[b, t0 - HALO:t0].T
        outs.append(xt)
    return outs


def cores_to_x(xoTs):
    x = np.zeros((B, S, D), np.float32)
    for c in range(NCORE):
        b, ch = c // 4, c % 4
        x[b, ch * TOK:(ch + 1) * TOK] = xoTs[c][:, HALO:].T
    return x


_T_CACHE = {}


def run_T(stages, xfull, inputs, extra=None, modin=None):
    key = tuple(stages)
    if key not in _T_CACHE:
        _T_CACHE[key] = build_T(list(stages))
    nc, names = _T_CACHE[key]
    xs = x_to_cores(xfull)
    wm = stage_weights(stages, inputs)
    in_maps = []
    for c in range(NCORE):
        b, ch = c // 4, c % 4
        m = {"xT": xs[c], "vec": vec_pack(names, inputs),
             "modin": (modin[c] if modin is not None else np.zeros((128, 36, 8), np.float32)),
             "cT": np.ascontiguousarray(np.asarray(inputs["c"], np.float32)[b].reshape(8, 128).T),
             "halo_on": np.full((128, 1), 0.0 if ch == 0 else 1.0, np.float32)}
        m.update(wm)
        if extra is not None:
            m.update(extra[c])
        in_maps.append(m)
    res = run_bass_kernel_spmd(nc, in_maps, core_ids=list(range(NCORE)))
    return res.results


ADA_COLS = 4608


def build_ada():
    nc = bass.Bass("TRN2", target_bir_lowering=False)
    cT = nc.dram_tensor("cT", [128, NCH, B], F32, kind="ExternalInput").ap()
    w = nc.dram_tensor("w", [D, ADA_COLS], F32, kind="ExternalInput").ap()
    bias = nc.dram_tensor("bias", [1, ADA_COLS], F32, kind="ExternalInput").ap()
    out = nc.dram_tensor("mod", [B, ADA_COLS], F32, kind="ExternalOutput").ap()
    P = Prog(nc)
    ct = P.sbuf("ct", [128, NCH, B], F32)
    ctb = P.buf("ct")
    bsb = P.sbuf("bsb", [1, ADA_COLS], F32)
    bsbb = P.buf("bsb")
    ones = P.sbuf("ones1", [1, B], F32)
    onesb = P.buf("ones1")
    osb = P.sbuf("osb", [B, ADA_COLS], F32)
    osbb = P.buf("osb")
    wt = [(P.sbuf(f"wt{i}", [128, NCH, 512], F32), P.buf(f"wt{i}")) for i in range(2)]
    ps = [(P.psum(f"ps{i}", [128, 512], F32), P.buf(f"ps{i}")) for i in range(2)]
    P.dma("sync", ct[:], cT, w=[ctb])
    P.dma("sync", bsb[:], bias, w=[bsbb])
    mset(P, "vector", ones[:], 1.0, w=[onesb])
    act(P, ct[:], ct[:], AF.Silu, r=[ctb], w=[ctb])
    wv = w.rearrange("(k p) c -> p k c", p=128)
    for ci in range(ADA_COLS // 512):
        wtile, wb = wt[ci % 2]
        P.dma("sync", wtile[:], wv[:, :, ci * 512:(ci + 1) * 512], w=[wb])
        pp, ppb = ps[ci % 2]
        pairs = [(ct[:, k, :], wtile[:, k, :]) for k in range(NCH)]
        pairs.append((ones[:], bsb[:, ci * 512:(ci + 1) * 512]))
        mm_group(P, pp[0:B, :], pairs, r=[ctb, wb, onesb, bsbb], w=[ppb])
        cp(P, "vector", osb[:, ci * 512:(ci + 1) * 512], pp[0:B, :], r=[ppb], w=[osbb])
    P.dma("sync", out, osb[:], r=[osbb])
    P.emit()
    return nc


_ADA = []


def run_ada(inputs):
    if not _ADA:
        _ADA.append(build_ada())
    nc = _ADA[0]
    cT = np.ascontiguousarray(np.asarray(inputs["c"], np.float32).T.reshape(NCH, 128, B).transpose(1, 0, 2))
    in_maps = []
    for c in range(NCORE):
        L, hf = c // 2, c % 2
        in_maps.append({"cT": cT,
                        "w": np.ascontiguousarray(inputs["ada_w"][L][:, hf * ADA_COLS:(hf + 1) * ADA_COLS]),
                        "bias": np.ascontiguousarray(inputs["ada_b"][L][None, hf * ADA_COLS:(hf + 1) * ADA_COLS])})
    res = run_bass_kernel_spmd(nc, in_maps, core_ids=list(range(NCORE))).results
    mod = np.zeros((B, 4, 9 * D), np.float32)
    for c in range(NCORE):
        L, hf = c // 2, c % 2
        mod[:, L, hf * ADA_COLS:(hf + 1) * ADA_COLS] = res[c]["mod"]
    return mod.reshape(B, 4, 3, 3, D)


NQT = 64
NCMP = 511
FORCE0, FORCE1, FORCE2 = 3.0e9, 1.0e9, 2.0e9


DBG_I = None


def build_A():
    nc = bass.Bass("TRN2", target_bir_lowering=False)

    def din(name, shape):
        return nc.dram_tensor(name, list(shape), F32, kind="ExternalInput").ap()
    q_d = din("q", [NQT, 64, 512])
    kT_d = din("kT", [4, 64, S])
    v_d = din("v", [2, 128, 64, 64])
    gl_d = din("gl", [NQT, 128, 12])
    gains_d = din("gains", [64, 4])
    posT_d = din("posT", [64, 32])
    w1_d = din("w1", [2, 2048, 256])
    w2_d = din("w2", [2, 256, 64])
    rb31_d = din("rb31", [65, 4])
    toep_d = din("toep", [2, 128, 512])
    mw4_d = din("mw4", [128, 512])
    bcw_d = din("bcw", [128, 4, 1015])
    selAB_d = din("selAB", [128, 2, 256])
    ident_d = din("ident", [128, 128])
    out_d = nc.dram_tensor("oT", [NQT, 64, 512], F32, kind="ExternalOutput").ap()
    dbg_d = nc.dram_tensor("dbg", [128, 1024], F32, kind="ExternalOutput").ap() if DBG_I is not None else None

    P = Prog(nc)

    def SB(name, shape, dt):
        return P.sbuf(name, shape, dt), P.buf(name)

    stg = Rot([SB(f"stg{i}", [128, 2048], F32) for i in range(2)])
    sqr = Rot([SB(f"sq{i}", [128, 512], F32) for i in range(2)])
    ones, onesb = SB("ones", [128, 64], F32)
    identf, identfb = SB("identf", [128, 128], F32)
    identb, identbb = SB("identb", [128, 128], BF16)
    I4, I4b = SB("I4", [128, 512], BF16)
    T0b, T0bb = SB("T0b", [128, 512], BF16)
    T1b, T1bb = SB("T1b", [128, 512], BF16)
    Mw4, Mw4b = SB("Mw4", [128, 512], BF16)
    selAB, selABb = SB("selAB_sb", [128, 2, 256], F32)
    gains, gainsb = SB("gains_sb", [64, 8], F32)
    rb31, rb31b = SB("rb31_sb", [65, 4], F32)
    posf, posfb = SB("posf", [64, 32], F32)
    posTb, posTbb = SB("posTb", [64, 32], BF16)
    KsT, KsTb = SB("KsT", [65, S], BF16)
    KwT, KwTb = SB("KwT", [65, S], BF16)
    Vs, Vsb = SB("Vs", [128, 64, 65], BF16)
    Vw, Vwb = SB("Vw", [128, 64, 65], BF16)
    KcT, KcTb = SB("KcT", [64, 512], BF16)
    Vc, Vcb = SB("Vc", [128, 4, 65], BF16)
    big1, big1b = SB("big1", [128, S], BF16)
    big2, big2b = SB("big2", [128, 4096], F32)
    w2b, w2bb = SB("w2b", [128, 2, 64], BF16)
    c1, c1b = SB("c1", [128, 2], F32)
    xg, xgb = SB("xg", [128, 512], F32)
    tg, tgb = SB("tg", [128, 512], F32)
    ge, geb = SB("ge", [128, 2, 512], BF16)
    kcf, kcfb = SB("kcf", [64, 512], F32)

    psS = Rot([(P.psum(f"psS{i}", [128, 512], F32), P.buf(f"psS{i}")) for i in range(2)])
    psC = Rot([(P.psum(f"psC{i}", [128, 512], F32), P.buf(f"psC{i}")) for i in range(2)])
    pOc, pOcb = P.psum("pOc", [128, 512], F32), P.buf("pOc")
    pOs, pOsb = P.psum("pOs", [128, 512], F32), P.buf("pOs")
    pOw, pOwb = P.psum("pOw", [128, 512], F32), P.buf("pOw")
    pT, pTb = P.psum("pT", [128, 512], F32), P.buf("pT")

    mset(P, "vector", ones[:], 1.0, w=[onesb])
    P.dma("sync", identf[:], ident_d, w=[identfb])
    cp(P, "vector", identb[:], identf[:], r=[identfb], w=[identbb])
    for h in range(4):
        cp(P, "vector", I4[:, h * 128:(h + 1) * 128], identf[:], r=[identfb], w=[I4b])
    P.dma("sync", selAB[:], selAB_d, w=[selABb])
    P.dma("sync", gains[:, 0:4], gains_d, w=[gainsb])
    ts(P, "vector", gains[:, 4:5], gains[:, 0:1], 0.125, None, ALU.mult, None, r=[gainsb], w=[gainsb])
    P.dma("sync", rb31[:], rb31_d, w=[rb31b])
    P.dma("sync", posf[:], posT_d, w=[posfb])
    cp(P, "vector", posTb[:], posf[:], r=[posfb], w=[posTbb])

    lc_n = [0]

    def load_cast(src, dst, parts, n, w, view=None):
        st, sb = stg.next()
        sv = st[0:parts, 0:n]
        if view is not None:
            sv = sv.rearrange(view[0], **view[1])
        P.dma("sync", sv, src, w=[sb])
        lc_n[0] += 1
        cp(P, "gpsimd" if lc_n[0] % 2 == 0 else "vector", dst, sv, r=[sb], w=w)

    load_cast(toep_d[0], T0b[:], 128, 512, [T0bb])
    load_cast(toep_d[1], T1b[:], 128, 512, [T1bb])
    load_cast(mw4_d, Mw4[:], 128, 512, [Mw4b])
    mset(P, "vector", KsT[64:65, :], 1.0, w=[KsTb])
    mset(P, "vector", KwT[64:65, :], 1.0, w=[KwTb])
    mset(P, "gpsimd", Vs[:], 1.0, w=[Vsb])
    mset(P, "gpsimd", Vw[:], 1.0, w=[Vwb])
    mset(P, "gpsimd", Vc[:], 0.0, w=[Vcb])
    mset(P, "gpsimd", ge[:], 0.0, w=[geb])
    mset(P, "gpsimd", KcT[:], 0.0, w=[KcTb])

    def rms_scale(src, srcb, n, gcol, dst, dstb, parts=64):
        sq, sqb = sqr.next()
        act(P, sq[0:parts, 0:n], src, AF.Square, r=[srcb], w=[sqb])
        pp, ppb = psC.next()
        mm_group(P, pp[0:parts, 0:n], [(ones[0:parts, 0:parts], sq[0:parts, 0:n])], r=[onesb, sqb], w=[ppb])
        ts(P, "vector", sq[0:parts, 0:n], pp[0:parts, 0:n], 1.0 / 64, EPS, ALU.mult, ALU.add, r=[ppb], w=[sqb])
        act(P, sq[0:parts, 0:n], sq[0:parts, 0:n], AF.Ln, r=[sqb], w=[sqb])
        act(P, sq[0:parts, 0:n], sq[0:parts, 0:n], AF.Exp, r=[sqb], w=[sqb], scale=-0.5)
        stt(P, "vector", dst, src, gains[0:parts, gcol:gcol + 1], sq[0:parts, 0:n], ALU.mult, ALU.mult,
            r=[srcb, gainsb, sqb], w=[dstb])

    for kidx, gcol, dT, dTb in ((2, 2, KsT, KsTb), (3, 3, KwT, KwTb)):
        for c4 in range(4):
            st, sb = stg.next()
            P.dma("sync", st[0:64, :], kT_d[kidx, :, c4 * 2048:(c4 + 1) * 2048], w=[sb])
            for c in range(4):
                col = c * 512
                rms_scale(st[0:64, col:col + 512], sb, 512, gcol, dT[0:64, c4 * 2048 + col:c4 * 2048 + col + 512], dTb)
    for vi_, (Vt, Vtb) in enumerate(((Vs, Vsb), (Vw, Vwb))):
        for j2 in range(2):
            load_cast(v_d[vi_, :, j2 * 32:(j2 + 1) * 32, :], Vt[:, j2 * 32:(j2 + 1) * 32, 0:64], 128, 2048, [Vtb],
                      view=("p (j d) -> p j d", dict(j=32)))

    kcv = big1[0:64, :]
    kcv3 = kcv.rearrange("p (n r) -> p n r", r=16)
    w1b = big2[0:64, :].bitcast(BF16)[:, 0:8192].rearrange("p (l c) -> p l c", l=32)
    for widx, kidx in ((0, 0), (1, 1)):
        for c4 in range(4):
            load_cast(kT_d[kidx, :, c4 * 2048:(c4 + 1) * 2048], kcv[:, c4 * 2048:(c4 + 1) * 2048], 64, 2048, [big1b])
        w1v = w1_d[widx].rearrange("(l d) c -> d l c", d=64)
        for l8 in range(4):
            load_cast(w1v[:, l8 * 8:(l8 + 1) * 8, :], w1b[:, l8 * 8:(l8 + 1) * 8, :], 64, 2048, [big2b],
                      view=("p (l c) -> p l c", dict(l=8)))
        load_cast(w2_d[widx].rearrange("(h p) d -> p h d", p=128), w2b[:], 128, 128, [w2bb],
                  view=("p (h d) -> p h d", dict(h=2)))
        for half in range(2):
            pp, ppb = psC.next()
            mm_group(P, pp[:, 0:1], [(w1b[:, l, half * 128:(half + 1) * 128], posTb[:, l:l + 1]) for l in range(32)],
                     r=[big2b, posTbb], w=[ppb])
            cp(P, "vector", c1[:, half:half + 1], pp[:, 0:1], r=[ppb], w=[c1b])
        for half in range(2):
            pp, ppb = psC.next()
            pairs = []
            for l in range(32):
                a_, r_ = l // 16, l % 16
                pairs.append((w1b[:, l, half * 128:(half + 1) * 128], kcv3[:, a_:a_ + NCMP, r_]))
            mm_group(P, pp[:, 0:NCMP], pairs, r=[big2b, big1b], w=[ppb])
            act(P, xg[:, 0:NCMP], pp[:, 0:NCMP], AF.Identity, r=[ppb, c1b], w=[xgb], bias=c1[:, half:half + 1])
            tt(P, "vector", tg[:, 0:NCMP], xg[:, 0:NCMP], xg[:, 0:NCMP], ALU.mult, r=[xgb], w=[tgb])
            ts(P, "vector", tg[:, 0:NCMP], tg[:, 0:NCMP], 0.044715, 1.0, ALU.mult, ALU.add, r=[tgb], w=[tgb])
            tt(P, "vector", tg[:, 0:NCMP], tg[:, 0:NCMP], xg[:, 0:NCMP], ALU.mult, r=[tgb, xgb], w=[tgb])
            act(P, tg[:, 0:NCMP], tg[:, 0:NCMP], AF.Sigmoid, r=[tgb], w=[tgb], scale=1.5957691216057308)
            tt(P, "vector", ge[:, half, 0:NCMP], xg[:, 0:NCMP], tg[:, 0:NCMP], ALU.mult, r=[xgb, tgb], w=[geb])
        if widx == 0:
            pp, ppb = psC.next()
            mm_group(P, pp[0:64, 0:NCMP], [(w2b[:, half, :], ge[:, half, 0:NCMP]) for half in range(2)],
                     r=[w2bb, geb], w=[ppb])
            cp(P, "vector", kcf[:, 0:NCMP], pp[0:64, 0:NCMP], r=[ppb], w=[kcfb])
            rms_scale(kcf[:, 0:NCMP], kcfb, NCMP, 1, KcT[:, 0:NCMP], KcTb)
        else:
            for nt in range(4):
                pp, ppb = psC.next()
                mm_group(P, pp[:, 0:64], [(ge[:, half, nt * 128:(nt + 1) * 128], w2b[:, half, :]) for half in range(2)],
                         r=[w2bb, geb], w=[ppb])
                cp(P, "vector", Vc[:, nt, 0:64], pp[:, 0:64], r=[ppb], w=[Vcb])
        if widx == 1:
            mset(P, "gpsimd", Vc[:, :, 64:65], 1.0, w=[Vcb])

    bcw = big2[:, :].rearrange("p (h m) -> p h m", h=4)
    P.dma("sync", bcw[:, :, 0:1015], bcw_d, w=[big2b])
    maskexp = big1
    qrawr = Rot([SB(f"qraw{i}", [64, 512], F32) for i in range(2)])
    QTr = Rot([SB(f"QT{i}", [65, 512], BF16) for i in range(2)])
    glr = Rot([SB(f"glr{i}", [128, 12], F32) for i in range(3)])
    eg, egb = SB("eg", [128, 12], F32)
    dm, dmb = SB("dm", [128, 12], F32)
    drow, drowb = SB("drow", [65, 1024], F32)
    I4f, I4fb = SB("I4f", [128, 512], F32)
    Fr = Rot([SB(f"Ft{i}", [128, 512], F32) for i in range(2)])
    for h in range(4):
        cp(P, "vector", I4f[:, h * 128:(h + 1) * 128], identf[:], r=[identfb], w=[I4fb])
    scr_ = Rot([SB(f"sc{i}", [128, 512], F32) for i in range(2)])
    ec, ecb = SB("ec", [128, 4, 512], BF16)
    ps1, ps1b = SB("psum1", [128, 520], F32)
    den, denb = SB("den", [128, 8], F32)
    imp, impb = SB("imp", [128, 128], F32)
    imp2, imp2b = SB("imp2", [128, 128], F32)
    mx8, mx8b = SB("mx8", [128, 16], F32)
    thr, thrb = SB("thr", [128, 1], F32)
    PTr = Rot([SB(f"PT{i}", [128, 512], BF16) for i in range(4)])
    ETr = Rot([SB(f"ET{i}", [128, 512], BF16) for i in range(2)])
    fbr = Rot([SB(f"fb{i}", [64, 512], F32) for i in range(2)])
    oaccr = Rot([SB(f"oacc{i}", [64, 512], F32) for i in range(2)])
    otmp, otmpb = SB("otmp", [64, 512], F32)
    mset(P, "gpsimd", ec[:], 0.0, w=[ecb])
    mset(P, "gpsimd", ps1[:], 0.0, w=[ps1b])
    for (QT, QTb) in QTr.items:
        for h in range(4):
            cp(P, "vector", QT[64:65, h * 128:(h + 1) * 128], rb31[64:65, h:h + 1].to_broadcast([1, 128]), r=[rb31b], w=[QTb])

    mask2, mask2b = SB("mask2", [128, S], BF16)
    masks = [(big1, big1b), (mask2, mask2b)]
    ec2, ec2b = SB("ec2", [128, 4, 512], BF16)
    mset(P, "gpsimd", ec2[:], 0.0, w=[ec2b])
    ecs = [(ec, ecb), (ec2, ec2b)]
    den2, den2b = SB("den2", [128, 8], F32)
    den3, den3b = SB("den3", [128, 8], F32)
    dens = [(den, denb), (den2, den2b), (den3, den3b)]
    occs = [SB(f"occ{k}", [65, 512], F32) for k in range(3)]
    psS3 = Rot(psS.items + [(pT, pTb)])
    osss = [SB(f"oss{k}", [65, 512], F32) for k in range(2)]
    owss = [SB(f"ows{k}", [65, 512], F32) for k in range(2)]
    state = {}

    def preamble(i):
        st = {}
        state[i] = st
        qraw, qrb = qrawr.next()
        P.dma("sync", qraw[:], q_d[i], w=[qrb])
        gl_, glb = glr.next()
        P.dma("sync", gl_[:], gl_d[i], w=[glb])
        QT, QTb = QTr.next()
        st["QT"] = (QT, QTb)
        st["gl"] = (gl_, glb)
        den_, denb_ = dens[i % 3]
        st["den"] = (den_, denb_)
        ec_, ecb_ = ecs[i % 2]
        maskexp, maskb = masks[i % 2]
        st["mask"] = (maskexp, maskb)
        occ, occb = occs[i % 3]
        st["occ"] = (occ, occb)
        rms_scale(qraw[:], qrb, 512, 4, QT[0:64, :], QTb)
        yield
        nvis = min(NCMP, 8 * i + 7)
        m0 = 504 - 8 * i
        mset(P, "vector", den_[:], 0.0, w=[denb_])
        for h in range(4):
            pp, ppb = psC.next()
            mm_group(P, pp[:, 0:nvis], [(QT[0:64, h * 128:(h + 1) * 128], KcT[0:64, 0:nvis])], r=[QTb, KcTb], w=[ppb])
            sc, scb = scr_.next()
            tt(P, "vector", sc[:, 0:nvis], pp[:, 0:nvis], bcw[:, h, m0:m0 + nvis], ALU.add, r=[ppb, big2b], w=[scb])
            yield
            act(P, sc[:, 0:nvis], sc[:, 0:nvis], AF.Exp, r=[scb], w=[scb, denb_], accum=den_[:, h:h + 1])
            yield
            ts(P, "vector", den_[:, 4 + h:5 + h], den_[:, h:h + 1], 1e-30, None, ALU.max, None, r=[denb_], w=[denb_])
            P.op("vector", lambda e, h=h, den_=den_: e.reciprocal(out=den_[:, 4 + h:5 + h], in_=den_[:, 4 + h:5 + h]), r=[denb_], w=[denb_])
            if h == 0:
                ts(P, "vector", ps1[:, 1:1 + nvis], sc[:, 0:nvis], den_[:, 4:5], None, ALU.mult, None, r=[scb, denb_], w=[ps1b])
            else:
                stt(P, "vector", ps1[:, 1:1 + nvis], sc[:, 0:nvis], den_[:, 4 + h:5 + h], ps1[:, 1:1 + nvis], ALU.mult, ALU.add,
                    r=[scb, denb_, ps1b], w=[ps1b])
            cp(P, "gpsimd", ec_[:, h, 0:nvis], sc[:, 0:nvis], r=[scb], w=[ecb_])
            yield
        P.op("vector", lambda e: e.tensor_reduce(out=imp[:], in_=ps1[:, 0:512].rearrange("p (s r) -> p s r", r=4),
                                                 axis=AX.X, op=ALU.add), r=[ps1b], w=[impb])
        tt(P, "vector", imp[:], imp[:], ps1[:, 4:516].rearrange("p (s r) -> p s r", r=4)[:, :, 0], ALU.add, r=[impb, ps1b], w=[impb])
        w0 = 126 - 2 * i
        tt(P, "vector", imp[:], imp[:], selAB[:, 0, w0:w0 + 128], ALU.mult, r=[impb, selABb], w=[impb])
        yield
        tt(P, "vector", imp[:], imp[:], selAB[:, 1, w0:w0 + 128], ALU.add, r=[impb, selABb], w=[impb])
        mset(P, "vector", imp[:, 0:1], FORCE0, w=[impb])
        yield
        P.op("vector", lambda e: e.max(out=mx8[:, 0:8], in_=imp[:]), r=[impb], w=[mx8b])
        yield
        P.op("vector", lambda e: e.match_replace(out=imp2[:], in_to_replace=mx8[:, 0:8], in_values=imp[:], imm_value=-2.0),
             r=[impb, mx8b], w=[imp2b])
        yield
        P.op("vector", lambda e: e.max(out=mx8[:, 8:16], in_=imp2[:]), r=[imp2b], w=[mx8b])
        yield
        P.op("vector", lambda e: e.tensor_reduce(out=thr[:], in_=mx8[:, 8:16], axis=AX.X, op=ALU.min), r=[mx8b], w=[thrb])
        yield
        nblk = 2 * (i + 1)
        half = max(2, (nblk // 2) // 2 * 2)
        for (b0, b1) in ((0, half), (half, nblk)):
            if b1 <= b0:
                continue
            ts(P, "gpsimd" if False else "vector", maskexp[:, b0 * 64:b1 * 64].rearrange("p (s k) -> p s k", k=64),
               imp[:, b0:b1].unsqueeze(2).to_broadcast([128, b1 - b0, 64]), thr[:, 0:1], NEG, ALU.is_lt, ALU.mult,
               r=[impb, thrb], w=[maskb])
            yield
        ntile = (nvis + 127) // 128
        pO_, pOb_ = pOc, pOcb
        for nt in range(ntile):
            pTc, pTcb = psC.next()
            pTv = pTc[:].bitcast(BF16)

            def trs(e, nt=nt, ec_=ec_, pTv=pTv):
                ins = None
                for h in range(4):
                    ins = e.transpose(out=pTv[:, h * 128:(h + 1) * 128], in_=ec_[:, h, nt * 128:(nt + 1) * 128], identity=identb[:])
                return ins
            P.op("tensor", trs, r=[ecb_, identbb], w=[pTcb])
            yield
            et, etb = ETr.next()
            cp(P, "vector", et[:], pTv[:, 0:512], r=[pTcb], w=[etb])
            yield
            P.op("tensor", lambda e, nt=nt, et=et, ntile=ntile, pO_=pO_: e.matmul(pO_[0:65, :], Vc[:, nt, :], et[:], start=(nt == 0), stop=(nt == ntile - 1)),
                 r=[Vcb, etb], w=[pOb_])
            yield
        cp(P, "vector", occ[:], pO_[0:65, :], r=[pOb_], w=[occb])
        yield

    def postamble(i):
        st = state[i]
        gl_, glb = st["gl"]
        den_, denb_ = st["den"]
        occ, occb = st["occ"]
        oss, ossb = osss[i % 2]
        ows, owsb = owss[i % 2]
        act(P, eg[:], gl_[:], AF.Exp, r=[glb], w=[egb], scale=-1.0)
        pd, pdb = psC.next()

        def dcols(e, pd=pd, oss=oss, ows=ows):
            ins = None
            for bi, src in enumerate((oss, ows)):
                for h in range(4):
                    c_ = 4 + bi * 4 + h
                    ins = e.transpose(out=pd[:, c_:c_ + 1], in_=src[64:65, h * 128:(h + 1) * 128], identity=ones[64:65, 0:1])
            return ins
        P.op("tensor", dcols, r=[ossb, owsb, onesb], w=[pdb])
        yield
        ts(P, "vector", dm[:, 0:4], den_[:, 0:4], 1e-30, None, ALU.max, None, r=[denb_], w=[dmb])
        ts(P, "vector", dm[:, 4:12], pd[:, 4:12], 1e-30, None, ALU.max, None, r=[pdb], w=[dmb])
        yield
        stt(P, "vector", dm[:], eg[:], 1.0, dm[:], ALU.add, ALU.mult, r=[egb, dmb], w=[dmb])
        yield
        P.op("vector", lambda e: e.reciprocal(out=dm[:], in_=dm[:]), r=[dmb], w=[dmb])
        yield
        oacc, oaccb = oaccr.next()
        for br, (pO, pOb) in enumerate(((occ, occb), (oss, ossb), (ows, owsb))):
            Ft, Ftb = Fr.next()
            tt(P, "vector", Ft[:].rearrange("p (h q) -> p h q", h=4), I4f[:].rearrange("p (h q) -> p h q", h=4),
               dm[:, br * 4:(br + 1) * 4].unsqueeze(2).to_broadcast([128, 4, 128]), ALU.mult, r=[I4fb, dmb], w=[Ftb])
            yield
            pb_, pbb_ = psC.next()
            mm_group(P, pb_[0:64, :], [(ones[:, 0:64], Ft[:])], r=[onesb, Ftb], w=[pbb_])
            yield
            if br == 0:
                tt(P, "vector", oacc[:], pO[0:64, :], pb_[0:64, :], ALU.mult, r=[pOb, pbb_], w=[oaccb])
            else:
                tt(P, "vector", otmp[:], pO[0:64, :], pb_[0:64, :], ALU.mult, r=[pOb, pbb_], w=[otmpb])
                tt(P, "gpsimd", oacc[:], oacc[:], otmp[:], ALU.add, r=[oaccb, otmpb], w=[oaccb])
            yield
        P.dma("sync", out_d[i], oacc[:], r=[oaccb], key=oaccb.name + "_o")
        yield

    def chain2(g1, g2):
        gs_ = [g for g in (g1, g2) if g is not None]
        while gs_:
            for g in list(gs_):
                try:
                    next(g)
                except StopIteration:
                    gs_.remove(g)
            yield

    def run_all(g):
        for _ in g:
            pass

    def step(g):
        if g is not None:
            next(g, None)

    run_all(preamble(0))
    for i in range(NQT):
        st = state[i]
        QT, QTb = st["QT"]
        maskexp, maskb = st["mask"]
        gen = chain2(postamble(i - 1) if i >= 1 else None, preamble(i + 1) if i + 1 < NQT else None)
        nkt = (i + 1) + min(5, i + 1)
        nstep = max(1, -(-28 // nkt))

        def branch(KT, KTb, Vt, Vtb, pO, pOb, j0, masked):
            def Sc(j):
                dj = i - j
                kk = 64 if dj <= 1 else 65
                ps, psb = psS3.next()
                pairs = [(KT[0:kk, j * 128:(j + 1) * 128], QT[0:kk, :])]
                rr = [KTb, QTb]
                if masked:
                    pairs.append((maskexp[:, j * 128:(j + 1) * 128], I4[:]))
                    rr += [maskb, I4b]
                if dj == 0:
                    pairs.append((identb[:], T0b[:]))
                    rr += [identbb, T0bb]
                elif dj == 1:
                    pairs.append((identb[:], T1b[:]))
                    rr += [identbb, T1bb]
                elif dj == 4 and not masked:
                    pairs.append((identb[:], Mw4[:]))
                    rr += [identbb, Mw4b]
                mm_group(P, ps[:], pairs, r=rr, w=[psb])
                return ps, psb
            q_ = [Sc(j) for j in range(j0, min(j0 + 2, i + 1))]
            for j in range(j0, i + 1):
                if j + 2 <= i:
                    q_.append(Sc(j + 2))
                ps, psb = q_.pop(0)
                pt, ptb = PTr.next()
                act(P, pt[:], ps[:], AF.Exp, r=[psb], w=[ptb])
                P.op("tensor", lambda e, j=j, pt=pt, i=i: e.matmul(pO[0:65, :], Vt[:, j, :], pt[:], start=(j == j0), stop=(j == i)),
                     r=[Vtb, ptb], w=[pOb])
                for _ in range(nstep):
                    step(gen)
        branch(KwT, KwTb, Vw, Vwb, pOw, pOwb, max(0, i - 4), False)
        branch(KsT, KsTb, Vs, Vsb, pOs, pOsb, 0, True)

        oss, ossb = osss[i % 2]
        ows, owsb = owss[i % 2]
        cp(P, "vector", oss[:], pOs[0:65, :], r=[pOsb], w=[ossb])
        cp(P, "vector", ows[:], pOw[0:65, :], r=[pOwb], w=[owsb])
        run_all(gen)
    run_all(postamble(NQT - 1))

    P.emit()
    return nc


def _t5_bucket_np(dist):
    n = np.maximum(dist, 0)
    nf = np.maximum(n, 1).astype(np.float32)
    large = 16 + (np.log(nf / np.float32(16)) / np.float32(np.log(8.0)) * np.float32(16)).astype(np.int32)
    large = np.minimum(large, 31)
    return np.where(n < 16, n, large)


def a_tables(rel_bias, g):
    rb = np.asarray(rel_bias, np.float32)[:, g * 4:(g + 1) * 4]
    k = np.arange(128)[:, None]
    q = np.arange(128)[None, :]
    d0 = q - k
    T0 = np.where((d0 >= 0)[:, None, :], rb[_t5_bucket_np(d0)].transpose(0, 2, 1), np.float32(NEG))
    T1 = rb[_t5_bucket_np(128 + d0)].transpose(0, 2, 1)
    toep = np.stack([T0.reshape(128, 512), T1.reshape(128, 512)]).astype(np.float32)
    mw4 = np.where(q < k, np.float32(0), np.float32(NEG))[:, None, :].repeat(4, axis=1).reshape(128, 512).astype(np.float32)
    m = np.arange(1015)[None, :]
    tq = np.arange(128)[:, None]
    dc = tq - 16 * (m - 504) - 31
    bcw = np.where((dc >= 0)[:, None, :], rb[_t5_bucket_np(dc)].transpose(0, 2, 1), np.float32(NEG)).astype(np.float32)
    w = np.arange(256)[None, :]
    srel = w - 126
    c = (tq >= 64).astype(np.int64)
    A = np.ones((128, 256), np.float32)
    Bv = np.zeros((128, 256), np.float32)
    fut = srel > c
    A[fut] = 0.0
    Bv[fut] = -1.0
    f1 = srel == c
    A[f1] = 0.0
    Bv[f1] = FORCE1
    f2 = srel == c - 1
    A[f2] = 0.0
    Bv[f2] = FORCE2
    selAB = np.stack([A, Bv], axis=1).astype(np.float32)
    rb31 = np.zeros((65, 4), np.float32)
    rb31[64] = rb[31]
    return dict(toep=np.ascontiguousarray(toep), mw4=np.ascontiguousarray(mw4), bcw=np.ascontiguousarray(bcw),
                selAB=np.ascontiguousarray(selAB), rb31=rb31, ident=np.eye(128, dtype=np.float32))


def a_inputs(proj_full, inputs, j):
    in_maps = []
    for c in range(NCORE):
        b, g = c // 4, c % 4
        pf = proj_full[b]
        Qg = pf[g * 256:(g + 1) * 256].reshape(4, 64, NQT, 128)
        m = {"q": np.ascontiguousarray(Qg.transpose(2, 1, 0, 3).reshape(NQT, 64, 512))}
        kts = []
        for idx in (0, 1, 2, 4):
            r0 = 1024 + idx * 256 + g * 64
            kts.append(pf[r0:r0 + 64])
        m["kT"] = np.ascontiguousarray(np.stack(kts))
        vts = []
        for idx in (3, 5):
            r0 = 1024 + idx * 256 + g * 64
            vts.append(pf[r0:r0 + 64].reshape(64, 64, 128).transpose(2, 1, 0))
        m["v"] = np.ascontiguousarray(np.stack(vts))
        G = pf[2560 + g * 12:2560 + (g + 1) * 12].reshape(4, 3, NQT, 128)
        m["gl"] = np.ascontiguousarray(G.transpose(2, 3, 1, 0).reshape(NQT, 128, 12))
        m["gains"] = np.ascontiguousarray(np.stack([inputs["nsa_q_gain"][j], inputs["nsa_k_gain"][j, 0],
                                                    inputs["nsa_k_gain"][j, 1], inputs["nsa_k_gain"][j, 2]], axis=1).astype(np.float32))
        m["posT"] = np.ascontiguousarray(np.asarray(inputs["nsa_cmp_pos"][j], np.float32).T)
        m["w1"] = np.ascontiguousarray(inputs["nsa_cmp_w1"][j])
        m["w2"] = np.ascontiguousarray(inputs["nsa_cmp_w2"][j])
        m.update(a_tables(inputs["rel_bias"], g))
        in_maps.append(m)
    return in_maps


_A = []


def run_A(proj_full, inputs, j):
    if not _A:
        _A.append(build_A())
    res = run_bass_kernel_spmd(_A[0], a_inputs(proj_full, inputs, j), core_ids=list(range(NCORE))).results
    o_full = [np.zeros((D, S), np.float32) for _ in range(B)]
    for c in range(NCORE):
        b, g = c // 4, c % 4
        o = res[c]["oT"].reshape(NQT, 64, 4, 128).transpose(2, 1, 0, 3).reshape(256, S)
        o_full[b][g * 256:(g + 1) * 256] = o
    return o_full


def _o_to_cores(o_full):
    outs = []
    for c in range(NCORE):
        b, ch = c // 4, c % 4
        t0 = ch * TOK
        ot = np.zeros((D, NT), np.float32)
        ot[:, HALO:] = o_full[b][:, t0:t0 + TOK]
        if ch > 0:
            ot[:, 0:HALO] = o_full[b][:, t0 - HALO:t0]
        outs.append(ot)
    return outs


def _proj_full(res, si):
    pf = []
    for b in range(B):
        pf.append(np.ascontiguousarray(np.concatenate([res[b * 4 + ch][f"projT{si}"] for ch in range(4)], axis=1)))
    return pf


def kernel(**inputs):
    inputs = {k: np.asarray(v) for k, v in inputs.items()}
    x = np.asarray(inputs["x"], np.float32)
    st1 = (("ada", 0), ("ffn", 0, 0), ("conv", 0), ("ffn", 0, 1), ("ada", 1), ("ffn", 1, 0), ("nsa_pre", 1))
    res = run_T(st1, x, inputs)
    x = cores_to_x([r["xoT"] for r in res])
    o_full = run_A(_proj_full(res, 6), inputs, 0)
    st2 = (("nsa_post", 1), ("ffn", 1, 1), ("ada", 2), ("ffn", 2, 0), ("conv", 2), ("ffn", 2, 1),
           ("ada", 3), ("ffn", 3, 0), ("nsa_pre", 3))
    oc = _o_to_cores(o_full)
    mods = [r["modout"] for r in res]
    res = run_T(st2, x, inputs, extra=[{"oT0": oc[c]} for c in range(NCORE)], modin=mods)
    x = cores_to_x([r["xoT"] for r in res])
    o_full = run_A(_proj_full(res, 8), inputs, 1)
    st3 = (("nsa_post", 3), ("ffn", 3, 1))
    oc = _o_to_cores(o_full)
    mods = [r["modout"] for r in res]
    res = run_T(st3, x, inputs, extra=[{"oT0": oc[c]} for c in range(NCORE)], modin=mods)
    x = cores_to_x([r["xoT"] for r in res])
    return x.astype(np.float32)
```

```python
import numpy as np
from contextlib import ExitStack
import concourse.bass as bass
import concourse.mybir as mybir
from concourse.bass_utils import run_bass_kernel_spmd

F32 = mybir.dt.float32
BF16 = mybir.dt.bfloat16
ALU = mybir.AluOpType
AF = mybir.ActivationFunctionType
AX = mybir.AxisListType

D = 1024
DFF = 2816
NCH = 8
NJ = 22
B = 2
S = 8192
NCORE = 8
TOK = 2048
EPS = 1e-6
NSA_IN = 2608
NEG = -30000.0


class Buf:
    __slots__ = ("name", "last_w", "readers")

    def __init__(self, name):
        self.name = name
        self.last_w = None
        self.readers = []


class Op:
    __slots__ = ("eng", "fn", "deps", "dma", "key", "signal", "count", "kcount")

    def __init__(self, eng, fn, dma, key):
        self.eng = eng
        self.fn = fn
        self.dma = dma
        self.key = key
        self.deps = []
        self.signal = False
        self.count = 0
        self.kcount = 0


ENGS = ["tensor", "vector", "scalar", "gpsimd", "sync"]


class Prog:
    def __init__(self, nc):
        self.nc = nc
        self.q = {e: [] for e in ENGS}
        self.stack = ExitStack()
        self.nbuf = 0

    def buf(self, name=None):
        self.nbuf += 1
        return Buf(name or f"b{self.nbuf}")

    def sbuf(self, name, shape, dt):
        return self.stack.enter_context(self.nc.sbuf_tensor(name, list(shape), dt))

    def psum(self, name, shape, dt):
        return self.stack.enter_context(self.nc.psum_tensor(name, list(shape), dt))

    def op(self, eng, fn, r=(), w=(), dma=False, key=None):
        if dma and key is None:
            key = (w[0].name if len(w) else r[0].name)
        o = Op(eng, fn, dma, key)
        deps = set()
        raw = set()
        for b in r:
            if b.last_w is not None:
                deps.add(b.last_w)
                raw.add(b.last_w)
        for b in w:
            if b.last_w is not None:
                deps.add(b.last_w)
            for rd in b.readers:
                deps.add(rd)
        for d in deps:
            if d is o:
                continue
            if d.dma or dma or d.eng != eng or (d in raw and eng != "tensor"):
                o.deps.append(d)
        for b in r:
            b.readers.append(o)
        for b in w:
            b.last_w = o
            b.readers = []
        self.q[eng].append(o)
        return o

    def dma(self, eng, out, in_, r=(), w=(), key=None, **kw):
        return self.op(eng, lambda e: e.dma_start(out=out, in_=in_, **kw), r=r, w=w, dma=True, key=key)

    def emit(self):
        nc = self.nc
        for e in ENGS:
            for o in self.q[e]:
                for d in o.deps:
                    d.signal = True
        keytot = {}
        for e in ENGS:
            cnt = 0
            for o in self.q[e]:
                if o.dma:
                    keytot[o.key] = keytot.get(o.key, 0) + 16
                    o.kcount = keytot[o.key]
                elif o.signal:
                    cnt += 1
                    o.count = cnt
        st = self.stack
        esem = {e: st.enter_context(nc.semaphore("se_" + e)) for e in ENGS}
        ksem = {}
        for i, k in enumerate(keytot):
            ksem[k] = st.enter_context(nc.semaphore(f"sk{i}"))
        self.nsem = len(ksem) + len(esem)

        def run(ename, e):
            waited = {}
            for o in self.q[ename]:
                need = {}
                for d in o.deps:
                    if d.dma:
                        k = ("k", d.key)
                        v = d.kcount
                    else:
                        k = ("e", d.eng)
                        v = d.count
                    if v > need.get(k, 0):
                        need[k] = v
                for k, v in need.items():
                    if waited.get(k, 0) >= v:
                        continue
                    waited[k] = v
                    sem = ksem[k[1]] if k[0] == "k" else esem[k[1]]
                    e.wait_ge(sem, v)
                ins = o.fn(e)
                if o.dma:
                    ins.then_inc(ksem[o.key], 16)
                elif o.signal:
                    ins.then_inc(esem[ename], 1)
            if ename == "sync":
                for k, tot in keytot.items():
                    if waited.get(("k", k), 0) < tot:
                        e.wait_ge(ksem[k], tot)

        with nc.Block() as block:
            @block.tensor
            def _(e):
                run("tensor", e)

            @block.vector
            def _(e):
                run("vector", e)

            @block.scalar
            def _(e):
                run("scalar", e)

            @block.gpsimd
            def _(e):
                run("gpsimd", e)

            @block.sync
            def _(e):
                run("sync", e)
        self.stack.close()


class WStream:
    def __init__(self, P, nstg=3, nw=4, elems=2048):
        self.P = P
        self.elems = elems
        self.stg = [P.sbuf(f"wstg{i}", [128, elems], F32) for i in range(nstg)]
        self.stgb = [P.buf(f"wstg{i}") for i in range(nstg)]
        self.wt = [P.sbuf(f"wbf{i}", [128, elems], BF16) for i in range(nw)]
        self.wtb = [P.buf(f"wbf{i}") for i in range(nw)]
        self.i = 0
        self.j = 0
        self.nd = 0

    def load(self, src, shape, parts=128):
        P = self.P
        n = int(np.prod(shape[1:]))
        assert n <= self.elems
        si = self.i % len(self.stg)
        wi = self.j % len(self.wt)
        self.i += 1
        self.j += 1
        stg = self.stg[si]
        wt = self.wt[wi]
        if len(shape) == 3:
            sv = stg[0:parts, 0:n].rearrange("p (a b) -> p a b", a=shape[1])
            wv = wt[0:parts, 0:n].rearrange("p (a b) -> p a b", a=shape[1])
        else:
            sv = stg[0:parts, 0:n]
            wv = wt[0:parts, 0:n]
        deng = "sync" if (self.nd % 2 == 0) else "gpsimd"
        self.nd += 1
        P.dma("sync", sv, src, w=[self.stgb[si]])
        ceng = "gpsimd"
        P.op(ceng, lambda e: e.tensor_copy(out=wv, in_=sv), r=[self.stgb[si]], w=[self.wtb[wi]])
        return wv, self.wtb[wi]


def mm_group(P, out, pairs, r, w):
    n = len(pairs)

    def fn(e):
        ins = None
        for i, (l, rr) in enumerate(pairs):
            ins = e.matmul(out, l, rr, start=(i == 0), stop=(i == n - 1))
        return ins
    return P.op("tensor", fn, r=r, w=w)


def act(P, out, in_, func, r, w, bias=None, scale=None, accum=None):
    kw = {}
    if bias is not None:
        kw["bias"] = bias
    if scale is not None:
        kw["scale"] = scale
    if accum is not None:
        kw["accum_out"] = accum
    return P.op("scalar", lambda e: e.activation(out=out, in_=in_, func=func, **kw), r=r, w=w)


def tt(P, eng, out, in0, in1, op, r, w):
    return P.op(eng, lambda e: e.tensor_tensor(out=out, in0=in0, in1=in1, op=op), r=r, w=w)


def ts(P, eng, out, in0, s1, s2, op0, op1, r, w, accum=None):
    kw = {}
    if accum is not None:
        kw["accum_out"] = accum
    if op1 is None:
        return P.op(eng, lambda e: e.tensor_scalar(out=out, in0=in0, scalar1=s1, scalar2=None, op0=op0, **kw), r=r, w=w)
    return P.op(eng, lambda e: e.tensor_scalar(out=out, in0=in0, scalar1=s1, scalar2=s2, op0=op0, op1=op1, **kw), r=r, w=w)


def stt(P, eng, out, in0, scalar, in1, op0, op1, r, w):
    return P.op(eng, lambda e: e.scalar_tensor_tensor(out=out, in0=in0, scalar=scalar, in1=in1, op0=op0, op1=op1), r=r, w=w)


def cp(P, eng, out, in_, r, w):
    return P.op(eng, lambda e: e.tensor_copy(out=out, in_=in_), r=r, w=w)


def mset(P, eng, out, val, w):
    return P.op(eng, lambda e: e.memset(out, val), r=[], w=w)


class Rot:
    def __init__(self, items):
        self.items = items
        self.i = 0

    def next(self):
        it = self.items[self.i % len(self.items)]
        self.i += 1
        return it


HALO = 2
NT = HALO + TOK
TILES = [(0, HALO)] + [(HALO + 512 * i, 512) for i in range(4)]
FFN_PARTS = [(0, 4), (4, 4), (8, 4), (12, 4), (16, 4), (20, 2)]


class TCtx:
    pass


def t_setup(P, nvec):
    C = TCtx()
    C.x = P.sbuf("x", [128, NCH, NT], F32)
    C.xb = [[P.buf(f"x{t}_{o}") for o in range(NCH)] for t in range(len(TILES))]
    C.xall = [bb for l in C.xb for bb in l]
    C.h = P.sbuf("h", [128, NCH, NT], BF16)
    C.hb = [P.buf(f"h{t}") for t in range(len(TILES))]
    C.a = [P.sbuf(f"a{i}", [128, 4, NT], BF16) for i in range(2)]
    C.ab = [[P.buf(f"a{i}_{t}") for t in range(len(TILES))] for i in range(2)]
    C.scr = P.sbuf("scr", [128, 4104], F32)
    C.scrL = [P.buf("scrA"), P.buf("scrB")]
    C.rstd = P.sbuf("rstd", [128, 512], F32)
    C.rstdb = P.buf("rstd")
    C.sg = Rot([(P.sbuf(f"sg{i}", [128, 512], F32), P.buf(f"sg{i}")) for i in range(2)])
    C.ostg = Rot(C.sg.items + [(P.sbuf(f"ostg{i}", [128, 512], F32), P.buf(f"ostg{i}")) for i in range(3)])
    C.ones = P.sbuf("ones", [128, 128], F32)
    C.onesb = P.buf("ones")
    C.vec = P.sbuf("vec_sb", [128, nvec + 8 + 36, 8], F32)
    C.vecb = P.buf("vec")
    C.nvec = nvec
    C.ntmp = 0
    C.halo_on = P.sbuf("halo_sb", [128, 1], F32)
    C.halob = P.buf("halo_on")
    ps = [(P.psum(f"ps{i}", [128, 512], F32), P.buf(f"ps{i}")) for i in range(8)]
    C.ps_g = Rot(ps[0:2])
    C.ps_u = Rot(ps[2:4])
    C.ps_y = Rot(ps[4:7])
    C.ps_s = Rot(ps[7:8])
    C.ws = WStream(P, nstg=3, nw=6)
    mset(P, "vector", C.ones[:], 1.0, w=[C.onesb])
    return C


def t_norm(P, C, gs, shift, halo=True):
    for t, (t0, n) in enumerate(TILES):
        if t == 0 and not halo:
            continue
        sq = C.scr[:, 0:NCH * n].rearrange("p (k n) -> p k n", k=NCH)
        act(P, sq, C.x[:, :, t0:t0 + n], AF.Square, r=C.xb[t], w=C.scrL)
        ps, psb = C.ps_s.next()
        mm_group(P, ps[:, 0:n], [(C.ones[:], sq[:, k, :]) for k in range(NCH)], r=C.scrL + [C.onesb], w=[psb])
        ts(P, "vector", C.rstd[:, 0:n], ps[:, 0:n], 1.0 / D, EPS, ALU.mult, ALU.add, r=[psb], w=[C.rstdb])
        act(P, C.rstd[:, 0:n], C.rstd[:, 0:n], AF.Sqrt, r=[C.rstdb], w=[C.rstdb])
        P.op("vector", lambda e, n=n: e.reciprocal(out=C.rstd[:, 0:n], in_=C.rstd[:, 0:n]), r=[C.rstdb], w=[C.rstdb])
        for k in range(NCH):
            tt(P, "vector", sq[:, k, :], C.x[:, k, t0:t0 + n], C.rstd[:, 0:n], ALU.mult,
               r=[C.xb[t][k], C.rstdb], w=C.scrL)
        for k in range(NCH):
            act(P, C.h[:, k, t0:t0 + n], sq[:, k, :], AF.Identity, r=C.scrL + [C.vecb], w=[C.hb[t]],
                bias=shift[:, k:k + 1], scale=gs[:, k:k + 1])


def t_ffn(P, C, w_in, w_out, gs, shift, ghalf, halo=True, gen=None):
    t_norm(P, C, gs, shift, halo)
    TL = [(t, t0, n) for t, (t0, n) in enumerate(TILES) if (halo or t > 0)]
    win_v = w_in.rearrange("(k p) c -> p k c", p=128)
    wout_v = w_out.rearrange("(j p) c -> p j c", p=128)

    def phase1(pi, j0, nj):
        ab = pi % 2
        for jp in range(0, nj, 2):
            c0 = (j0 + jp) * 128
            wg, wgb = C.ws.load(win_v[:, :, c0:c0 + 256], [128, NCH, 256])
            wu, wub = C.ws.load(win_v[:, :, DFF + c0:DFF + c0 + 256], [128, NCH, 256])
            for jj in range(2):
                for t, t0, n in TL:
                    pg, pgb = C.ps_g.next()
                    pu, pub = C.ps_u.next()
                    mm_group(P, pg[:, 0:n], [(wg[:, k, jj * 128:(jj + 1) * 128], C.h[:, k, t0:t0 + n]) for k in range(NCH)],
                             r=[wgb, C.hb[t]], w=[pgb])
                    mm_group(P, pu[:, 0:n], [(wu[:, k, jj * 128:(jj + 1) * 128], C.h[:, k, t0:t0 + n]) for k in range(NCH)],
                             r=[wub, C.hb[t]], w=[pub])
                    sg, sgb = C.sg.next()
                    act(P, sg[:, 0:n], pg[:, 0:n], AF.Silu, r=[pgb], w=[sgb])
                    tt(P, "vector", C.a[ab][:, jp + jj, t0:t0 + n], sg[:, 0:n], pu[:, 0:n], ALU.mult,
                       r=[sgb, pub], w=[C.ab[ab][t]])
                    if gen is not None and t % 2 == 0:
                        next(gen, None)

    def phase2(pi, j0, nj):
        ab = pi % 2
        wos = []
        for jp in range(0, nj, 2):
            r0 = j0 + jp
            wo, wob = C.ws.load(wout_v[:, r0:r0 + 2, :], [128, 2, D])
            wos.append((wo, wob))
        for t, t0, n in TL:
            for o in range(NCH):
                py, pyb = C.ps_y.next()
                pairs = []
                for ji in range(nj):
                    wo, wob = wos[ji // 2]
                    pairs.append((wo[:, ji % 2, o * 128:(o + 1) * 128], C.a[ab][:, ji, t0:t0 + n]))
                mm_group(P, py[:, 0:n], pairs, r=[b for _, b in wos] + [C.ab[ab][t]], w=[pyb])
                stt(P, "vector", C.x[:, o, t0:t0 + n], py[:, 0:n], ghalf[:, o:o + 1], C.x[:, o, t0:t0 + n],
                    ALU.mult, ALU.add, r=[pyb, C.vecb, C.xb[t][o]], w=[C.xb[t][o]])

    prev = None
    for pi, (j0, nj) in enumerate(FFN_PARTS):
        phase1(pi, j0, nj)
        if prev is not None:
            phase2(*prev)
        prev = (pi, j0, nj)
    phase2(*prev)


def t_conv(P, C, w_in, w_out, gs, shift, gate, cw):
    t_norm(P, C, gs, shift)
    win_v = w_in.rearrange("(k p) c -> p k c", p=128)
    wout_v = w_out.rearrange("(k p) c -> p k c", p=128)
    bsb = C.scr[:, 0:NT]
    usb = C.scr[:, 2052:2052 + NT]
    ws3 = None
    for o in range(NCH):
        oi = o % 2
        if oi == 0:
            ws3 = []
            for part in range(3):
                c0 = part * D + o * 128
                ws3.append(C.ws.load(win_v[:, :, c0:c0 + 256], [128, NCH, 256]))
        for t, (t0, n) in enumerate(TILES):
            pb, pbb = C.ps_g.next()
            pc, pcb = C.ps_u.next()
            pv, pvb = C.ps_y.next()
            for (pp, ppb), (wv, wb) in zip([(pb, pbb), (pc, pcb), (pv, pvb)], ws3):
                mm_group(P, pp[:, 0:n], [(wv[:, k, oi * 128:(oi + 1) * 128], C.h[:, k, t0:t0 + n]) for k in range(NCH)],
                         r=[wb, C.hb[t]], w=[ppb])
            act(P, bsb[:, t0:t0 + n], pb[:, 0:n], AF.Identity, r=[pbb], w=[C.scrL[0]])
            sg, sgb = C.sg.next()
            act(P, sg[:, 0:n], pc[:, 0:n], AF.Identity, r=[pcb], w=[sgb])
            tt(P, "vector", usb[:, t0:t0 + n], sg[:, 0:n], pv[:, 0:n], ALU.mult, r=[sgb, pvb], w=[C.scrL[1]])
        ts(P, "vector", usb[:, 0:HALO], usb[:, 0:HALO], C.halo_on[:, 0:1], None, ALU.mult, None, r=[C.scrL[1], C.halob], w=[C.scrL[1]])
        zt = C.a[o // 4]
        for t, (t0, n) in enumerate(TILES):
            if t == 0:
                continue
            ysb, ysbb = C.sg.next()
            ts(P, "vector", ysb[:, 0:n], usb[:, t0 - 2:t0 - 2 + n], cw[:, 0, o:o + 1], None, ALU.mult, None, r=[C.scrL[1], C.vecb], w=[ysbb])
            stt(P, "vector", ysb[:, 0:n], usb[:, t0 - 1:t0 - 1 + n], cw[:, 1, o:o + 1], ysb[:, 0:n], ALU.mult, ALU.add, r=[C.scrL[1], C.vecb, ysbb], w=[ysbb])
            stt(P, "vector", ysb[:, 0:n], usb[:, t0:t0 + n], cw[:, 2, o:o + 1], ysb[:, 0:n], ALU.mult, ALU.add, r=[C.scrL[1], C.vecb, ysbb], w=[ysbb])
            tt(P, "gpsimd", zt[:, o % 4, t0:t0 + n], bsb[:, t0:t0 + n], ysb[:, 0:n], ALU.mult, r=[C.scrL[0], ysbb], w=[C.ab[o // 4][t]])
    for oo2 in range(0, NCH, 2):
        wo, wob = C.ws.load(wout_v[:, :, oo2 * 128:(oo2 + 2) * 128], [128, NCH, 256])
        for oi in range(2):
            o = oo2 + oi
            for t, (t0, n) in enumerate(TILES):
                if t == 0:
                    continue
                py, pyb = C.ps_y.next()
                mm_group(P, py[:, 0:n], [(wo[:, k, oi * 128:(oi + 1) * 128], C.a[k // 4][:, k % 4, t0:t0 + n]) for k in range(NCH)],
                         r=[wob, C.ab[0][t], C.ab[1][t]], w=[pyb])
                stt(P, "vector", C.x[:, o, t0:t0 + n], py[:, 0:n], gate[:, o:o + 1], C.x[:, o, t0:t0 + n],
                    ALU.mult, ALU.add, r=[pyb, C.vecb, C.xb[t][o]], w=[C.xb[t][o]])


def t_nsa_pre(P, C, w_in, projT, gs, shift):
    t_norm(P, C, gs, shift, False)
    win_v = w_in.rearrange("(k p) c -> p k c", p=128)
    nchunk = (NSA_IN + 127) // 128
    wcur = None
    for oc in range(nchunk):
        rows = min(128, NSA_IN - oc * 128)
        if oc % 2 == 0:
            cols = min(256, NSA_IN - oc * 128)
            wcur = C.ws.load(win_v[:, :, oc * 128:oc * 128 + cols], [128, NCH, cols])
        wv, wb = wcur
        c0 = (oc % 2) * 128
        for t, (t0, n) in enumerate(TILES):
            if t == 0:
                continue
            pp, ppb = C.ps_y.next()
            mm_group(P, pp[0:rows, 0:n], [(wv[:, k, c0:c0 + rows], C.h[:, k, t0:t0 + n]) for k in range(NCH)],
                     r=[wb, C.hb[t]], w=[ppb])
            sg, sgb = C.ostg.next()
            act(P, sg[0:rows, 0:n], pp[0:rows, 0:n], AF.Identity, r=[ppb], w=[sgb])
            P.dma("sync", projT[oc * 128:oc * 128 + rows, t0 - HALO:t0 - HALO + n], sg[0:rows, 0:n], r=[sgb], key=sgb.name + "_o")


def t_nsa_post(P, C, oT, w_out, gate):
    wout_v = w_out.rearrange("(k p) c -> p k c", p=128)
    oT_v = oT.rearrange("(k p) t -> p k t", p=128)
    for k in range(NCH):
        for hf in range(2):
            c0 = hf * (NT // 2)
            n = NT // 2
            stg = C.scr[:, hf * 2052: hf * 2052 + n]
            P.dma("sync", stg, oT_v[:, k, c0:c0 + n], w=[C.scrL[hf]])
            cp(P, "gpsimd" if hf == 0 else "vector", C.h[:, k, c0:c0 + n], stg, r=[C.scrL[hf]], w=C.hb)
    for oo2 in range(0, NCH, 2):
        wo, wob = C.ws.load(wout_v[:, :, oo2 * 128:(oo2 + 2) * 128], [128, NCH, 256])
        for oi in range(2):
            o = oo2 + oi
            for t, (t0, n) in enumerate(TILES):
                py, pyb = C.ps_y.next()
                mm_group(P, py[:, 0:n], [(wo[:, k, oi * 128:(oi + 1) * 128], C.h[:, k, t0:t0 + n]) for k in range(NCH)],
                         r=[wob, C.hb[t]], w=[pyb])
                stt(P, "vector", C.x[:, o, t0:t0 + n], py[:, 0:n], gate[:, o:o + 1], C.x[:, o, t0:t0 + n],
                    ALU.mult, ALU.add, r=[pyb, C.vecb, C.xb[t][o]], w=[C.xb[t][o]])


def vec_names(stages):
    names = []
    for st in stages:
        kind, L = st[0], st[1]
        if kind == "ffn":
            s = 0 if st[2] == 0 else 2
            names += [f"L{L}_g{s}"]
        elif kind in ("conv", "nsa_pre"):
            names += [f"L{L}_g1"]
            if kind == "conv":
                names += [f"L{L}_cw0", f"L{L}_cw1", f"L{L}_cw2"]
    out = []
    for n in names:
        if n not in out:
            out.append(n)
    if not out:
        out = ["L0_g0"]
    return out


def t_ada_gen(P, C, adaw, L, modrow_d, modb):
    wv = adaw.rearrange("(k p) c -> p k c", p=128)
    rowbufs = []
    for c4 in range(36):
        hf = c4 % 2
        stgv = C.scr[:, hf * 2052:hf * 2052 + 2048].rearrange("p (k c) -> p k c", k=NCH)
        P.dma("sync", stgv, wv[:, :, c4 * 256:(c4 + 1) * 256], w=[C.scrL[hf]])
        ps, psb = C.ps_y.next()
        mm_group(P, ps[0:1, 0:256], [(C.cond[:, k:k + 1], stgv[:, k, :]) for k in range(NCH)], r=[C.scrL[hf], C.condb], w=[psb])
        sg, sgb = C.sg.next()
        act(P, sg[0:1, 0:256], ps[0:1, 0:256], AF.Identity, r=[psb], w=[sgb])
        rb_ = P.buf(f"modrow{L}_{c4}")
        P.dma("sync", modrow_d[L:L + 1, c4 * 256:(c4 + 1) * 256], sg[0:1, 0:256], r=[sgb], w=[rb_], key=sgb.name + "_o")
        rowbufs.append(rb_)
        yield
    base = C.nvec + 8 + L * 9
    dst = C.vec[:, base:base + 9, :].rearrange("p a b -> p (a b)")
    P.dma("sync", dst, modrow_d[L].rearrange("(c p) -> p c", p=128), r=rowbufs, w=[modb], key="modT",
          allow_slow_non_contiguous=True)
    yield


def t_ada_finish(P, C, adab, adabb, L, modb):
    base = C.nvec + 8 + L * 9
    dst = C.vec[:, base:base + 9, :].rearrange("p a b -> p (a b)")
    tt(P, "vector", dst, dst, adab, ALU.add, r=[adabb, modb, C.vecb], w=[C.vecb, modb])


def build_T(stages):
    nc = bass.Bass("TRN2", target_bir_lowering=False)
    names = vec_names(stages)
    nvec = len(names)
    vi = {n: i for i, n in enumerate(names)}
    xT = nc.dram_tensor("xT", [D, NT], F32, kind="ExternalInput").ap()
    vec = nc.dram_tensor("vec", [128, nvec, 8], F32, kind="ExternalInput").ap()
    halo_on = nc.dram_tensor("halo_on", [128, 1], F32, kind="ExternalInput").ap()
    xoT = nc.dram_tensor("xoT", [D, NT], F32, kind="ExternalOutput").ap()
    wd = {}
    cT_d = nc.dram_tensor("cT", [128, NCH], F32, kind="ExternalInput").ap()
    modin_d = nc.dram_tensor("modin", [128, 36, 8], F32, kind="ExternalInput").ap()
    modout_d = nc.dram_tensor("modout", [128, 36, 8], F32, kind="ExternalOutput").ap()
    modrow_d = nc.dram_tensor("modrow", [4, 9 * D], F32).ap()
    for si, st in enumerate(stages):
        kind = st[0]
        if kind == "ada":
            wd[si] = (nc.dram_tensor(f"adaw{st[1]}", [D, 9 * D], F32, kind="ExternalInput").ap(),
                      nc.dram_tensor(f"adab{st[1]}", [128, 72], F32, kind="ExternalInput").ap())
        elif kind == "ffn":
            wd[si] = (nc.dram_tensor(f"w{si}_in", [D, 2 * DFF], F32, kind="ExternalInput").ap(),
                      nc.dram_tensor(f"w{si}_out", [DFF, D], F32, kind="ExternalInput").ap())
        elif kind == "conv":
            wd[si] = (nc.dram_tensor(f"w{si}_in", [D, 3 * D], F32, kind="ExternalInput").ap(),
                      nc.dram_tensor(f"w{si}_out", [D, D], F32, kind="ExternalInput").ap())
        elif kind == "nsa_pre":
            wd[si] = (nc.dram_tensor(f"w{si}_in", [D, NSA_IN], F32, kind="ExternalInput").ap(),
                      nc.dram_tensor(f"projT{si}", [NSA_IN, TOK], F32, kind="ExternalOutput").ap())
        elif kind == "nsa_post":
            wd[si] = (nc.dram_tensor(f"oT{si}", [D, NT], F32, kind="ExternalInput").ap(),
                      nc.dram_tensor(f"w{si}_out", [D, D], F32, kind="ExternalInput").ap())
    P = Prog(nc)
    C = t_setup(P, nvec)
    P.dma("sync", C.vec[:, 0:nvec, :], vec, w=[C.vecb])
    P.dma("sync", C.halo_on[:], halo_on, w=[C.halob])
    xT_v = xT.rearrange("(k p) t -> p k t", p=128)
    P.dma("sync", C.x[:], xT_v, w=C.xall, key="xin")

    C.cond = P.sbuf("cond", [128, NCH], F32)
    C.condb = P.buf("cond")
    P.dma("sync", C.cond[:], cT_d, w=[C.condb])
    act(P, C.cond[:], C.cond[:], AF.Silu, r=[C.condb], w=[C.condb])
    P.dma("sync", C.vec[:, nvec + 8:nvec + 44, :], modin_d, w=[C.vecb])
    adab_sb = {}
    for si, st in enumerate(stages):
        if st[0] == "ada":
            t_ = P.sbuf(f"adab_sb{st[1]}", [128, 72], F32)
            tb_ = P.buf(f"adab_sb{st[1]}")
            P.dma("sync", t_[:], wd[si][1], w=[tb_])
            adab_sb[st[1]] = (t_, tb_)

    def V(name):
        L_ = int(name[1:name.index("_")])
        f = name[name.index("_") + 1:]
        for kind_i, kn in enumerate(("shift", "scale", "gate")):
            if f.startswith(kn):
                sub = int(f[len(kn):])
                return C.vec[:, nvec + 8 + L_ * 9 + sub * 3 + kind_i, :]
        return C.vec[:, vi[name], :]

    def tmpvec():
        i = nvec + C.ntmp % 8
        C.ntmp += 1
        return C.vec[:, i, :]

    ada_done = set()
    for si, st in enumerate(stages):
        kind, L = st[0], st[1]
        if kind == "ada":
            if si in ada_done:
                continue
            mb_ = P.buf(f"modb{L}")
            for _ in t_ada_gen(P, C, wd[si][0], L, modrow_d, mb_):
                pass
            t_ada_finish(P, C, adab_sb[L][0][:], adab_sb[L][1], L, mb_)
        elif kind == "ffn":
            s = 0 if st[2] == 0 else 2
            gs = tmpvec()
            gh = tmpvec()
            stt(P, "vector", gs, V(f"L{L}_scale{s}"), 1.0, V(f"L{L}_g{s}"), ALU.add, ALU.mult, r=[C.vecb], w=[C.vecb])
            ts(P, "vector", gh, V(f"L{L}_gate{s}"), 0.5, None, ALU.mult, None, r=[C.vecb], w=[C.vecb])
            need_halo = any(st2[0] == "conv" for st2 in stages[si + 1:])
            g_ = None
            if si + 1 < len(stages) and stages[si + 1][0] == "ada":
                L2 = stages[si + 1][1]
                mb_ = P.buf(f"modb{L2}")
                g_ = t_ada_gen(P, C, wd[si + 1][0], L2, modrow_d, mb_)
                ada_done.add(si + 1)
            t_ffn(P, C, wd[si][0], wd[si][1], gs, V(f"L{L}_shift{s}"), gh, halo=need_halo, gen=g_)
            if g_ is not None:
                for _ in g_:
                    pass
                t_ada_finish(P, C, adab_sb[L2][0][:], adab_sb[L2][1], L2, mb_)
        elif kind == "conv":
            gs = tmpvec()
            stt(P, "vector", gs, V(f"L{L}_scale1"), 1.0, V(f"L{L}_g1"), ALU.add, ALU.mult, r=[C.vecb], w=[C.vecb])
            cwv = C.vec[:, vi[f"L{L}_cw0"]:vi[f"L{L}_cw0"] + 3, :]
            t_conv(P, C, wd[si][0], wd[si][1], gs, V(f"L{L}_shift1"), V(f"L{L}_gate1"), cwv)
        elif kind == "nsa_pre":
            gs = tmpvec()
            stt(P, "vector", gs, V(f"L{L}_scale1"), 1.0, V(f"L{L}_g1"), ALU.add, ALU.mult, r=[C.vecb], w=[C.vecb])
            t_nsa_pre(P, C, wd[si][0], wd[si][1], gs, V(f"L{L}_shift1"))
        elif kind == "nsa_post":
            t_nsa_post(P, C, wd[si][0], wd[si][1], V(f"L{L}_gate1"))
    P.dma("sync", modout_d, C.vec[:, nvec + 8:nvec + 44, :], r=[C.vecb], key="modout")
    xo_v = xoT.rearrange("(k p) t -> p k t", p=128)
    P.dma("sync", xo_v, C.x[:], r=C.xall, key="xout")
    P.emit()
    return nc, names


def vec_pack(names, inputs):
    cols = []
    for n in names:
        L = int(n[1:n.index("_")])
        f = n[n.index("_") + 1:]
        if f.startswith("g"):
            v = inputs["norm_g"][L, int(f[1:])]
        elif f.startswith("cw"):
            v = inputs["conv_w"][L // 2, int(f[2:])]
        else:
            raise KeyError(n)
        cols.append(np.asarray(v, np.float32).reshape(8, 128).T)
    return np.ascontiguousarray(np.stack(cols, axis=1))


def stage_weights(stages, inputs):
    m = {}
    for si, st in enumerate(stages):
        kind, L = st[0], st[1]
        if kind == "ada":
            m[f"adaw{L}"] = inputs["ada_w"][L]
            m[f"adab{L}"] = np.ascontiguousarray(np.asarray(inputs["ada_b"][L], np.float32).reshape(72, 128).T)
        elif kind == "ffn":
            m[f"w{si}_in"] = inputs["ffn_w_in"][L, st[2]]
            m[f"w{si}_out"] = inputs["ffn_w_out"][L, st[2]]
        elif kind == "conv":
            m[f"w{si}_in"] = inputs["conv_w_in"][L // 2]
            m[f"w{si}_out"] = inputs["conv_w_out"][L // 2]
        elif kind == "nsa_pre":
            m[f"w{si}_in"] = inputs["nsa_w_in"][L // 2]
        elif kind == "nsa_post":
            m[f"w{si}_out"] = inputs["nsa_w_out"][L // 2]
    return m


def x_to_cores(xfull):
    outs = []
    for c in range(NCORE):
        b, ch = c // 4, c % 4
        t0 = ch * TOK
        xt = np.zeros((D, NT), np.float32)
        xt[:, HALO:] = xfull[b, t0:t0 + TOK].T
        if ch > 0:
            xt[:, 0:HALO] = xfull[b, t0 - HALO:t0].T
        outs.append(xt)
    return outs


def cores_to_x(xoTs):
    x = np.zeros((B, S, D), np.float32)
    for c in range(NCORE):
        b, ch = c // 4, c % 4
        x[b, ch * TOK:(ch + 1) * TOK] = xoTs[c][:, HALO:].T
    return x


_T_CACHE = {}


def run_T(stages, xfull, inputs, extra=None, modin=None):
    key = tuple(stages)
    if key not in _T_CACHE:
        _T_CACHE[key] = build_T(list(stages))
    nc, names = _T_CACHE[key]
    xs = x_to_cores(xfull)
    wm = stage_weights(stages, inputs)
    in_maps = []
    for c in range(NCORE):
        b, ch = c // 4, c % 4
        m = {"xT": xs[c], "vec": vec_pack(names, inputs),
             "modin": (modin[c] if modin is not None else np.zeros((128, 36, 8), np.float32)),
             "cT": np.ascontiguousarray(np.asarray(inputs["c"], np.float32)[b].reshape(8, 128).T),
             "halo_on": np.full((128, 1), 0.0 if ch == 0 else 1.0, np.float32)}
        m.update(wm)
        if extra is not None:
            m.update(extra[c])
        in_maps.append(m)
    res = run_bass_kernel_spmd(nc, in_maps, core_ids=list(range(NCORE)))
    return res.results


ADA_COLS = 4608


def build_ada():
    nc = bass.Bass("TRN2", target_bir_lowering=False)
    cT = nc.dram_tensor("cT", [128, NCH, B], F32, kind="ExternalInput").ap()
    w = nc.dram_tensor("w", [D, ADA_COLS], F32, kind="ExternalInput").ap()
    bias = nc.dram_tensor("bias", [1, ADA_COLS], F32, kind="ExternalInput").ap()
    out = nc.dram_tensor("mod", [B, ADA_COLS], F32, kind="ExternalOutput").ap()
    P = Prog(nc)
    ct = P.sbuf("ct", [128, NCH, B], F32)
    ctb = P.buf("ct")
    bsb = P.sbuf("bsb", [1, ADA_COLS], F32)
    bsbb = P.buf("bsb")
    ones = P.sbuf("ones1", [1, B], F32)
    onesb = P.buf("ones1")
    osb = P.sbuf("osb", [B, ADA_COLS], F32)
    osbb = P.buf("osb")
    wt = [(P.sbuf(f"wt{i}", [128, NCH, 512], F32), P.buf(f"wt{i}")) for i in range(2)]
    ps = [(P.psum(f"ps{i}", [128, 512], F32), P.buf(f"ps{i}")) for i in range(2)]
    P.dma("sync", ct[:], cT, w=[ctb])
    P.dma("sync", bsb[:], bias, w=[bsbb])
    mset(P, "vector", ones[:], 1.0, w=[onesb])
    act(P, ct[:], ct[:], AF.Silu, r=[ctb], w=[ctb])
    wv = w.rearrange("(k p) c -> p k c", p=128)
    for ci in range(ADA_COLS // 512):
        wtile, wb = wt[ci % 2]
        P.dma("sync", wtile[:], wv[:, :, ci * 512:(ci + 1) * 512], w=[wb])
        pp, ppb = ps[ci % 2]
        pairs = [(ct[:, k, :], wtile[:, k, :]) for k in range(NCH)]
        pairs.append((ones[:], bsb[:, ci * 512:(ci + 1) * 512]))
        mm_group(P, pp[0:B, :], pairs, r=[ctb, wb, onesb, bsbb], w=[ppb])
        cp(P, "vector", osb[:, ci * 512:(ci + 1) * 512], pp[0:B, :], r=[ppb], w=[osbb])
    P.dma("sync", out, osb[:], r=[osbb])
    P.emit()
    return nc


_ADA = []


def run_ada(inputs):
    if not _ADA:
        _ADA.append(build_ada())
    nc = _ADA[0]
    cT = np.ascontiguousarray(np.asarray(inputs["c"], np.float32).T.reshape(NCH, 128, B).transpose(1, 0, 2))
    in_maps = []
    for c in range(NCORE):
        L, hf = c // 2, c % 2
        in_maps.append({"cT": cT,
                        "w": np.ascontiguousarray(inputs["ada_w"][L][:, hf * ADA_COLS:(hf + 1) * ADA_COLS]),
                        "bias": np.ascontiguousarray(inputs["ada_b"][L][None, hf * ADA_COLS:(hf + 1) * ADA_COLS])})
    res = run_bass_kernel_spmd(nc, in_maps, core_ids=list(range(NCORE))).results
    mod = np.zeros((B, 4, 9 * D), np.float32)
    for c in range(NCORE):
        L, hf = c // 2, c % 2
        mod[:, L, hf * ADA_COLS:(hf + 1) * ADA_COLS] = res[c]["mod"]
    return mod.reshape(B, 4, 3, 3, D)


NQT = 64
NCMP = 511
FORCE0, FORCE1, FORCE2 = 3.0e9, 1.0e9, 2.0e9


DBG_I = None


def build_A():
    nc = bass.Bass("TRN2", target_bir_lowering=False)

    def din(name, shape):
        return nc.dram_tensor(name, list(shape), F32, kind="ExternalInput").ap()
    q_d = din("q", [NQT, 64, 512])
    kT_d = din("kT", [4, 64, S])
    v_d = din("v", [2, 128, 64, 64])
    gl_d = din("gl", [NQT, 128, 12])
    gains_d = din("gains", [64, 4])
    posT_d = din("posT", [64, 32])
    w1_d = din("w1", [2, 2048, 256])
    w2_d = din("w2", [2, 256, 64])
    rb31_d = din("rb31", [65, 4])
    toep_d = din("toep", [2, 128, 512])
    mw4_d = din("mw4", [128, 512])
    bcw_d = din("bcw", [128, 4, 1015])
    selAB_d = din("selAB", [128, 2, 256])
    ident_d = din("ident", [128, 128])
    out_d = nc.dram_tensor("oT", [NQT, 64, 512], F32, kind="ExternalOutput").ap()
    dbg_d = nc.dram_tensor("dbg", [128, 1024], F32, kind="ExternalOutput").ap() if DBG_I is not None else None

    P = Prog(nc)

    def SB(name, shape, dt):
        return P.sbuf(name, shape, dt), P.buf(name)

    stg = Rot([SB(f"stg{i}", [128, 2048], F32) for i in range(2)])
    sqr = Rot([SB(f"sq{i}", [128, 512], F32) for i in range(2)])
    ones, onesb = SB("ones", [128, 64], F32)
    identf, identfb = SB("identf", [128, 128], F32)
    identb, identbb = SB("identb", [128, 128], BF16)
    I4, I4b = SB("I4", [128, 512], BF16)
    T0b, T0bb = SB("T0b", [128, 512], BF16)
    T1b, T1bb = SB("T1b", [128, 512], BF16)
    Mw4, Mw4b = SB("Mw4", [128, 512], BF16)
    selAB, selABb = SB("selAB_sb", [128, 2, 256], F32)
    gains, gainsb = SB("gains_sb", [64, 8], F32)
    rb31, rb31b = SB("rb31_sb", [65, 4], F32)
    posf, posfb = SB("posf", [64, 32], F32)
    posTb, posTbb = SB("posTb", [64, 32], BF16)
    KsT, KsTb = SB("KsT", [65, S], BF16)
    KwT, KwTb = SB("KwT", [65, S], BF16)
    Vs, Vsb = SB("Vs", [128, 64, 65], BF16)
    Vw, Vwb = SB("Vw", [128, 64, 65], BF16)
    KcT, KcTb = SB("KcT", [64, 512], BF16)
    Vc, Vcb = SB("Vc", [128, 4, 65], BF16)
    big1, big1b = SB("big1", [128, S], BF16)
    big2, big2b = SB("big2", [128, 4096], F32)
    w2b, w2bb = SB("w2b", [128, 2, 64], BF16)
    c1, c1b = SB("c1", [128, 2], F32)
    xg, xgb = SB("xg", [128, 512], F32)
    tg, tgb = SB("tg", [128, 512], F32)
    ge, geb = SB("ge", [128, 2, 512], BF16)
    kcf, kcfb = SB("kcf", [64, 512], F32)

    psS = Rot([(P.psum(f"psS{i}", [128, 512], F32), P.buf(f"psS{i}")) for i in range(2)])
    psC = Rot([(P.psum(f"psC{i}", [128, 512], F32), P.buf(f"psC{i}")) for i in range(2)])
    pOc, pOcb = P.psum("pOc", [128, 512], F32), P.buf("pOc")
    pOs, pOsb = P.psum("pOs", [128, 512], F32), P.buf("pOs")
    pOw, pOwb = P.psum("pOw", [128, 512], F32), P.buf("pOw")
    pT, pTb = P.psum("pT", [128, 512], F32), P.buf("pT")

    mset(P, "vector", ones[:], 1.0, w=[onesb])
    P.dma("sync", identf[:], ident_d, w=[identfb])
    cp(P, "vector", identb[:], identf[:], r=[identfb], w=[identbb])
    for h in range(4):
        cp(P, "vector", I4[:, h * 128:(h + 1) * 128], identf[:], r=[identfb], w=[I4b])
    P.dma("sync", selAB[:], selAB_d, w=[selABb])
    P.dma("sync", gains[:, 0:4], gains_d, w=[gainsb])
    ts(P, "vector", gains[:, 4:5], gains[:, 0:1], 0.125, None, ALU.mult, None, r=[gainsb], w=[gainsb])
    P.dma("sync", rb31[:], rb31_d, w=[rb31b])
    P.dma("sync", posf[:], posT_d, w=[posfb])
    cp(P, "vector", posTb[:], posf[:], r=[posfb], w=[posTbb])

    lc_n = [0]

    def load_cast(src, dst, parts, n, w, view=None):
        st, sb = stg.next()
        sv = st[0:parts, 0:n]
        if view is not None:
            sv = sv.rearrange(view[0], **view[1])
        P.dma("sync", sv, src, w=[sb])
        lc_n[0] += 1
        cp(P, "gpsimd" if lc_n[0] % 2 == 0 else "vector", dst, sv, r=[sb], w=w)

    load_cast(toep_d[0], T0b[:], 128, 512, [T0bb])
    load_cast(toep_d[1], T1b[:], 128, 512, [T1bb])
    load_cast(mw4_d, Mw4[:], 128, 512, [Mw4b])
    mset(P, "vector", KsT[64:65, :], 1.0, w=[KsTb])
    mset(P, "vector", KwT[64:65, :], 1.0, w=[KwTb])
    mset(P, "gpsimd", Vs[:], 1.0, w=[Vsb])
    mset(P, "gpsimd", Vw[:], 1.0, w=[Vwb])
    mset(P, "gpsimd", Vc[:], 0.0, w=[Vcb])
    mset(P, "gpsimd", ge[:], 0.0, w=[geb])
    mset(P, "gpsimd", KcT[:], 0.0, w=[KcTb])

    def rms_scale(src, srcb, n, gcol, dst, dstb, parts=64):
        sq, sqb = sqr.next()
        act(P, sq[0:parts, 0:n], src, AF.Square, r=[srcb], w=[sqb])
        pp, ppb = psC.next()
        mm_group(P, pp[0:parts, 0:n], [(ones[0:parts, 0:parts], sq[0:parts, 0:n])], r=[onesb, sqb], w=[ppb])
        ts(P, "vector", sq[0:parts, 0:n], pp[0:parts, 0:n], 1.0 / 64, EPS, ALU.mult, ALU.add, r=[ppb], w=[sqb])
        act(P, sq[0:parts, 0:n], sq[0:parts, 0:n], AF.Ln, r=[sqb], w=[sqb])
        act(P, sq[0:parts, 0:n], sq[0:parts, 0:n], AF.Exp, r=[sqb], w=[sqb], scale=-0.5)
        stt(P, "vector", dst, src, gains[0:parts, gcol:gcol + 1], sq[0:parts, 0:n], ALU.mult, ALU.mult,
            r=[srcb, gainsb, sqb], w=[dstb])

    for kidx, gcol, dT, dTb in ((2, 2, KsT, KsTb), (3, 3, KwT, KwTb)):
        for c4 in range(4):
            st, sb = stg.next()
            P.dma("sync", st[0:64, :], kT_d[kidx, :, c4 * 2048:(c4 + 1) * 2048], w=[sb])
            for c in range(4):
                col = c * 512
                rms_scale(st[0:64, col:col + 512], sb, 512, gcol, dT[0:64, c4 * 2048 + col:c4 * 2048 + col + 512], dTb)
    for vi_, (Vt, Vtb) in enumerate(((Vs, Vsb), (Vw, Vwb))):
        for j2 in range(2):
            load_cast(v_d[vi_, :, j2 * 32:(j2 + 1) * 32, :], Vt[:, j2 * 32:(j2 + 1) * 32, 0:64], 128, 2048, [Vtb],
                      view=("p (j d) -> p j d", dict(j=32)))

    kcv = big1[0:64, :]
    kcv3 = kcv.rearrange("p (n r) -> p n r", r=16)
    w1b = big2[0:64, :].bitcast(BF16)[:, 0:8192].rearrange("p (l c) -> p l c", l=32)
    for widx, kidx in ((0, 0), (1, 1)):
        for c4 in range(4):
            load_cast(kT_d[kidx, :, c4 * 2048:(c4 + 1) * 2048], kcv[:, c4 * 2048:(c4 + 1) * 2048], 64, 2048, [big1b])
        w1v = w1_d[widx].rearrange("(l d) c -> d l c", d=64)
        for l8 in range(4):
            load_cast(w1v[:, l8 * 8:(l8 + 1) * 8, :], w1b[:, l8 * 8:(l8 + 1) * 8, :], 64, 2048, [big2b],
                      view=("p (l c) -> p l c", dict(l=8)))
        load_cast(w2_d[widx].rearrange("(h p) d -> p h d", p=128), w2b[:], 128, 128, [w2bb],
                  view=("p (h d) -> p h d", dict(h=2)))
        for half in range(2):
            pp, ppb = psC.next()
            mm_group(P, pp[:, 0:1], [(w1b[:, l, half * 128:(half + 1) * 128], posTb[:, l:l + 1]) for l in range(32)],
                     r=[big2b, posTbb], w=[ppb])
            cp(P, "vector", c1[:, half:half + 1], pp[:, 0:1], r=[ppb], w=[c1b])
        for half in range(2):
            pp, ppb = psC.next()
            pairs = []
            for l in range(32):
                a_, r_ = l // 16, l % 16
                pairs.append((w1b[:, l, half * 128:(half + 1) * 128], kcv3[:, a_:a_ + NCMP, r_]))
            mm_group(P, pp[:, 0:NCMP], pairs, r=[big2b, big1b], w=[ppb])
            act(P, xg[:, 0:NCMP], pp[:, 0:NCMP], AF.Identity, r=[ppb, c1b], w=[xgb], bias=c1[:, half:half + 1])
            tt(P, "vector", tg[:, 0:NCMP], xg[:, 0:NCMP], xg[:, 0:NCMP], ALU.mult, r=[xgb], w=[tgb])
            ts(P, "vector", tg[:, 0:NCMP], tg[:, 0:NCMP], 0.044715, 1.0, ALU.mult, ALU.add, r=[tgb], w=[tgb])
            tt(P, "vector", tg[:, 0:NCMP], tg[:, 0:NCMP], xg[:, 0:NCMP], ALU.mult, r=[tgb, xgb], w=[tgb])
            act(P, tg[:, 0:NCMP], tg[:, 0:NCMP], AF.Sigmoid, r=[tgb], w=[tgb], scale=1.5957691216057308)
            tt(P, "vector", ge[:, half, 0:NCMP], xg[:, 0:NCMP], tg[:, 0:NCMP], ALU.mult, r=[xgb, tgb], w=[geb])
        if widx == 0:
            pp, ppb = psC.next()
            mm_group(P, pp[0:64, 0:NCMP], [(w2b[:, half, :], ge[:, half, 0:NCMP]) for half in range(2)],
                     r=[w2bb, geb], w=[ppb])
            cp(P, "vector", kcf[:, 0:NCMP], pp[0:64, 0:NCMP], r=[ppb], w=[kcfb])
            rms_scale(kcf[:, 0:NCMP], kcfb, NCMP, 1, KcT[:, 0:NCMP], KcTb)
        else:
            for nt in range(4):
                pp, ppb = psC.next()
                mm_group(P, pp[:, 0:64], [(ge[:, half, nt * 128:(nt + 1) * 128], w2b[:, half, :]) for half in range(2)],
                         r=[w2bb, geb], w=[ppb])
                cp(P, "vector", Vc[:, nt, 0:64], pp[:, 0:64], r=[ppb], w=[Vcb])
        if widx == 1:
            mset(P, "gpsimd", Vc[:, :, 64:65], 1.0, w=[Vcb])

    bcw = big2[:, :].rearrange("p (h m) -> p h m", h=4)
    P.dma("sync", bcw[:, :, 0:1015], bcw_d, w=[big2b])
    maskexp = big1
    qrawr = Rot([SB(f"qraw{i}", [64, 512], F32) for i in range(2)])
    QTr = Rot([SB(f"QT{i}", [65, 512], BF16) for i in range(2)])
    glr = Rot([SB(f"glr{i}", [128, 12], F32) for i in range(3)])
    eg, egb = SB("eg", [128, 12], F32)
    dm, dmb = SB("dm", [128, 12], F32)
    drow, drowb = SB("drow", [65, 1024], F32)
    I4f, I4fb = SB("I4f", [128, 512], F32)
    Fr = Rot([SB(f"Ft{i}", [128, 512], F32) for i in range(2)])
    for h in range(4):
        cp(P, "vector", I4f[:, h * 128:(h + 1) * 128], identf[:], r=[identfb], w=[I4fb])
    scr_ = Rot([SB(f"sc{i}", [128, 512], F32) for i in range(2)])
    ec, ecb = SB("ec", [128, 4, 512], BF16)
    ps1, ps1b = SB("psum1", [128, 520], F32)
    den, denb = SB("den", [128, 8], F32)
    imp, impb = SB("imp", [128, 128], F32)
    imp2, imp2b = SB("imp2", [128, 128], F32)
    mx8, mx8b = SB("mx8", [128, 16], F32)
    thr, thrb = SB("thr", [128, 1], F32)
    PTr = Rot([SB(f"PT{i}", [128, 512], BF16) for i in range(4)])
    ETr = Rot([SB(f"ET{i}", [128, 512], BF16) for i in range(2)])
    fbr = Rot([SB(f"fb{i}", [64, 512], F32) for i in range(2)])
    oaccr = Rot([SB(f"oacc{i}", [64, 512], F32) for i in range(2)])
    otmp, otmpb = SB("otmp", [64, 512], F32)
    mset(P, "gpsimd", ec[:], 0.0, w=[ecb])
    mset(P, "gpsimd", ps1[:], 0.0, w=[ps1b])
    for (QT, QTb) in QTr.items:
        for h in range(4):
            cp(P, "vector", QT[64:65, h * 128:(h + 1) * 128], rb31[64:65, h:h + 1].to_broadcast([1, 128]), r=[rb31b], w=[QTb])

    mask2, mask2b = SB("mask2", [128, S], BF16)
    masks = [(big1, big1b), (mask2, mask2b)]
    ec2, ec2b = SB("ec2", [128, 4, 512], BF16)
    mset(P, "gpsimd", ec2[:], 0.0, w=[ec2b])
    ecs = [(ec, ecb), (ec2, ec2b)]
    den2, den2b = SB("den2", [128, 8], F32)
    den3, den3b = SB("den3", [128, 8], F32)
    dens = [(den, denb), (den2, den2b), (den3, den3b)]
    occs = [SB(f"occ{k}", [65, 512], F32) for k in range(3)]
    psS3 = Rot(psS.items + [(pT, pTb)])
    osss = [SB(f"oss{k}", [65, 512], F32) for k in range(2)]
    owss = [SB(f"ows{k}", [65, 512], F32) for k in range(2)]
    state = {}

    def preamble(i):
        st = {}
        state[i] = st
        qraw, qrb = qrawr.next()
        P.dma("sync", qraw[:], q_d[i], w=[qrb])
        gl_, glb = glr.next()
        P.dma("sync", gl_[:], gl_d[i], w=[glb])
        QT, QTb = QTr.next()
        st["QT"] = (QT, QTb)
        st["gl"] = (gl_, glb)
        den_, denb_ = dens[i % 3]
        st["den"] = (den_, denb_)
        ec_, ecb_ = ecs[i % 2]
        maskexp, maskb = masks[i % 2]
        st["mask"] = (maskexp, maskb)
        occ, occb = occs[i % 3]
        st["occ"] = (occ, occb)
        rms_scale(qraw[:], qrb, 512, 4, QT[0:64, :], QTb)
        yield
        nvis = min(NCMP, 8 * i + 7)
        m0 = 504 - 8 * i
        mset(P, "vector", den_[:], 0.0, w=[denb_])
        for h in range(4):
            pp, ppb = psC.next()
            mm_group(P, pp[:, 0:nvis], [(QT[0:64, h * 128:(h + 1) * 128], KcT[0:64, 0:nvis])], r=[QTb, KcTb], w=[ppb])
            sc, scb = scr_.next()
            tt(P, "vector", sc[:, 0:nvis], pp[:, 0:nvis], bcw[:, h, m0:m0 + nvis], ALU.add, r=[ppb, big2b], w=[scb])
            yield
            act(P, sc[:, 0:nvis], sc[:, 0:nvis], AF.Exp, r=[scb], w=[scb, denb_], accum=den_[:, h:h + 1])
            yield
            ts(P, "vector", den_[:, 4 + h:5 + h], den_[:, h:h + 1], 1e-30, None, ALU.max, None, r=[denb_], w=[denb_])
            P.op("vector", lambda e, h=h, den_=den_: e.reciprocal(out=den_[:, 4 + h:5 + h], in_=den_[:, 4 + h:5 + h]), r=[denb_], w=[denb_])
            if h == 0:
                ts(P, "vector", ps1[:, 1:1 + nvis], sc[:, 0:nvis], den_[:, 4:5], None, ALU.mult, None, r=[scb, denb_], w=[ps1b])
            else:
                stt(P, "vector", ps1[:, 1:1 + nvis], sc[:, 0:nvis], den_[:, 4 + h:5 + h], ps1[:, 1:1 + nvis], ALU.mult, ALU.add,
                    r=[scb, denb_, ps1b], w=[ps1b])
            cp(P, "gpsimd", ec_[:, h, 0:nvis], sc[:, 0:nvis], r=[scb], w=[ecb_])
            yield
        P.op("vector", lambda e: e.tensor_reduce(out=imp[:], in_=ps1[:, 0:512].rearrange("p (s r) -> p s r", r=4),
                                                 axis=AX.X, op=ALU.add), r=[ps1b], w=[impb])
        tt(P, "vector", imp[:], imp[:], ps1[:, 4:516].rearrange("p (s r) -> p s r", r=4)[:, :, 0], ALU.add, r=[impb, ps1b], w=[impb])
        w0 = 126 - 2 * i
        tt(P, "vector", imp[:], imp[:], selAB[:, 0, w0:w0 + 128], ALU.mult, r=[impb, selABb], w=[impb])
        yield
        tt(P, "vector", imp[:], imp[:], selAB[:, 1, w0:w0 + 128], ALU.add, r=[impb, selABb], w=[impb])
        mset(P, "vector", imp[:, 0:1], FORCE0, w=[impb])
        yield
        P.op("vector", lambda e: e.max(out=mx8[:, 0:8], in_=imp[:]), r=[impb], w=[mx8b])
        yield
        P.op("vector", lambda e: e.match_replace(out=imp2[:], in_to_replace=mx8[:, 0:8], in_values=imp[:], imm_value=-2.0),
             r=[impb, mx8b], w=[imp2b])
        yield
        P.op("vector", lambda e: e.max(out=mx8[:, 8:16], in_=imp2[:]), r=[imp2b], w=[mx8b])
        yield
        P.op("vector", lambda e: e.tensor_reduce(out=thr[:], in_=mx8[:, 8:16], axis=AX.X, op=ALU.min), r=[mx8b], w=[thrb])
        yield
        nblk = 2 * (i + 1)
        half = max(2, (nblk // 2) // 2 * 2)
        for (b0, b1) in ((0, half), (half, nblk)):
            if b1 <= b0:
                continue
            ts(P, "gpsimd" if False else "vector", maskexp[:, b0 * 64:b1 * 64].rearrange("p (s k) -> p s k", k=64),
               imp[:, b0:b1].unsqueeze(2).to_broadcast([128, b1 - b0, 64]), thr[:, 0:1], NEG, ALU.is_lt, ALU.mult,
               r=[impb, thrb], w=[maskb])
            yield
        ntile = (nvis + 127) // 128
        pO_, pOb_ = pOc, pOcb
        for nt in range(ntile):
            pTc, pTcb = psC.next()
            pTv = pTc[:].bitcast(BF16)

            def trs(e, nt=nt, ec_=ec_, pTv=pTv):
                ins = None
                for h in range(4):
                    ins = e.transpose(out=pTv[:, h * 128:(h + 1) * 128], in_=ec_[:, h, nt * 128:(nt + 1) * 128], identity=identb[:])
                return ins
            P.op("tensor", trs, r=[ecb_, identbb], w=[pTcb])
            yield
            et, etb = ETr.next()
            cp(P, "vector", et[:], pTv[:, 0:512], r=[pTcb], w=[etb])
            yield
            P.op("tensor", lambda e, nt=nt, et=et, ntile=ntile, pO_=pO_: e.matmul(pO_[0:65, :], Vc[:, nt, :], et[:], start=(nt == 0), stop=(nt == ntile - 1)),
                 r=[Vcb, etb], w=[pOb_])
            yield
        cp(P, "vector", occ[:], pO_[0:65, :], r=[pOb_], w=[occb])
        yield

    def postamble(i):
        st = state[i]
        gl_, glb = st["gl"]
        den_, denb_ = st["den"]
        occ, occb = st["occ"]
        oss, ossb = osss[i % 2]
        ows, owsb = owss[i % 2]
        act(P, eg[:], gl_[:], AF.Exp, r=[glb], w=[egb], scale=-1.0)
        pd, pdb = psC.next()

        def dcols(e, pd=pd, oss=oss, ows=ows):
            ins = None
            for bi, src in enumerate((oss, ows)):
                for h in range(4):
                    c_ = 4 + bi * 4 + h
                    ins = e.transpose(out=pd[:, c_:c_ + 1], in_=src[64:65, h * 128:(h + 1) * 128], identity=ones[64:65, 0:1])
            return ins
        P.op("tensor", dcols, r=[ossb, owsb, onesb], w=[pdb])
        yield
        ts(P, "vector", dm[:, 0:4], den_[:, 0:4], 1e-30, None, ALU.max, None, r=[denb_], w=[dmb])
        ts(P, "vector", dm[:, 4:12], pd[:, 4:12], 1e-30, None, ALU.max, None, r=[pdb], w=[dmb])
        yield
        stt(P, "vector", dm[:], eg[:], 1.0, dm[:], ALU.add, ALU.mult, r=[egb, dmb], w=[dmb])
        yield
        P.op("vector", lambda e: e.reciprocal(out=dm[:], in_=dm[:]), r=[dmb], w=[dmb])
        yield
        oacc, oaccb = oaccr.next()
        for br, (pO, pOb) in enumerate(((occ, occb), (oss, ossb), (ows, owsb))):
            Ft, Ftb = Fr.next()
            tt(P, "vector", Ft[:].rearrange("p (h q) -> p h q", h=4), I4f[:].rearrange("p (h q) -> p h q", h=4),
               dm[:, br * 4:(br + 1) * 4].unsqueeze(2).to_broadcast([128, 4, 128]), ALU.mult, r=[I4fb, dmb], w=[Ftb])
            yield
            pb_, pbb_ = psC.next()
            mm_group(P, pb_[0:64, :], [(ones[:, 0:64], Ft[:])], r=[onesb, Ftb], w=[pbb_])
            yield
            if br == 0:
                tt(P, "vector", oacc[:], pO[0:64, :], pb_[0:64, :], ALU.mult, r=[pOb, pbb_], w=[oaccb])
            else:
                tt(P, "vector", otmp[:], pO[0:64, :], pb_[0:64, :], ALU.mult, r=[pOb, pbb_], w=[otmpb])
                tt(P, "gpsimd", oacc[:], oacc[:], otmp[:], ALU.add, r=[oaccb, otmpb], w=[oaccb])
            yield
        P.dma("sync", out_d[i], oacc[:], r=[oaccb], key=oaccb.name + "_o")
        yield

    def chain2(g1, g2):
        gs_ = [g for g in (g1, g2) if g is not None]
        while gs_:
            for g in list(gs_):
                try:
                    next(g)
                except StopIteration:
                    gs_.remove(g)
            yield

    def run_all(g):
        for _ in g:
            pass

    def step(g):
        if g is not None:
            next(g, None)

    run_all(preamble(0))
    for i in range(NQT):
        st = state[i]
        QT, QTb = st["QT"]
        maskexp, maskb = st["mask"]
        gen = chain2(postamble(i - 1) if i >= 1 else None, preamble(i + 1) if i + 1 < NQT else None)
        nkt = (i + 1) + min(5, i + 1)
        nstep = max(1, -(-28 // nkt))

        def branch(KT, KTb, Vt, Vtb, pO, pOb, j0, masked):
            def Sc(j):
                dj = i - j
                kk = 64 if dj <= 1 else 65
                ps, psb = psS3.next()
                pairs = [(KT[0:kk, j * 128:(j + 1) * 128], QT[0:kk, :])]
                rr = [KTb, QTb]
                if masked:
                    pairs.append((maskexp[:, j * 128:(j + 1) * 128], I4[:]))
                    rr += [maskb, I4b]
                if dj == 0:
                    pairs.append((identb[:], T0b[:]))
                    rr += [identbb, T0bb]
                elif dj == 1:
                    pairs.append((identb[:], T1b[:]))
                    rr += [identbb, T1bb]
                elif dj == 4 and not masked:
                    pairs.append((identb[:], Mw4[:]))
                    rr += [identbb, Mw4b]
                mm_group(P, ps[:], pairs, r=rr, w=[psb])
                return ps, psb
            q_ = [Sc(j) for j in range(j0, min(j0 + 2, i + 1))]
            for j in range(j0, i + 1):
                if j + 2 <= i:
                    q_.append(Sc(j + 2))
                ps, psb = q_.pop(0)
                pt, ptb = PTr.next()
                act(P, pt[:], ps[:], AF.Exp, r=[psb], w=[ptb])
                P.op("tensor", lambda e, j=j, pt=pt, i=i: e.matmul(pO[0:65, :], Vt[:, j, :], pt[:], start=(j == j0), stop=(j == i)),
                     r=[Vtb, ptb], w=[pOb])
                for _ in range(nstep):
                    step(gen)
        branch(KwT, KwTb, Vw, Vwb, pOw, pOwb, max(0, i - 4), False)
        branch(KsT, KsTb, Vs, Vsb, pOs, pOsb, 0, True)

        oss, ossb = osss[i % 2]
        ows, owsb = owss[i % 2]
        cp(P, "vector", oss[:], pOs[0:65, :], r=[pOsb], w=[ossb])
        cp(P, "vector", ows[:], pOw[0:65, :], r=[pOwb], w=[owsb])
        run_all(gen)
    run_all(postamble(NQT - 1))

    P.emit()
    return nc


def _t5_bucket_np(dist):
    n = np.maximum(dist, 0)
    nf = np.maximum(n, 1).astype(np.float32)
    large = 16 + (np.log(nf / np.float32(16)) / np.float32(np.log(8.0)) * np.float32(16)).astype(np.int32)
    large = np.minimum(large, 31)
    return np.where(n < 16, n, large)


def a_tables(rel_bias, g):
    rb = np.asarray(rel_bias, np.float32)[:, g * 4:(g + 1) * 4]
    k = np.arange(128)[:, None]
    q = np.arange(128)[None, :]
    d0 = q - k
    T0 = np.where((d0 >= 0)[:, None, :], rb[_t5_bucket_np(d0)].transpose(0, 2, 1), np.float32(NEG))
    T1 = rb[_t5_bucket_np(128 + d0)].transpose(0, 2, 1)
    toep = np.stack([T0.reshape(128, 512), T1.reshape(128, 512)]).astype(np.float32)
    mw4 = np.where(q < k, np.float32(0), np.float32(NEG))[:, None, :].repeat(4, axis=1).reshape(128, 512).astype(np.float32)
    m = np.arange(1015)[None, :]
    tq = np.arange(128)[:, None]
    dc = tq - 16 * (m - 504) - 31
    bcw = np.where((dc >= 0)[:, None, :], rb[_t5_bucket_np(dc)].transpose(0, 2, 1), np.float32(NEG)).astype(np.float32)
    w = np.arange(256)[None, :]
    srel = w - 126
    c = (tq >= 64).astype(np.int64)
    A = np.ones((128, 256), np.float32)
    Bv = np.zeros((128, 256), np.float32)
    fut = srel > c
    A[fut] = 0.0
    Bv[fut] = -1.0
    f1 = srel == c
    A[f1] = 0.0
    Bv[f1] = FORCE1
    f2 = srel == c - 1
    A[f2] = 0.0
    Bv[f2] = FORCE2
    selAB = np.stack([A, Bv], axis=1).astype(np.float32)
    rb31 = np.zeros((65, 4), np.float32)
    rb31[64] = rb[31]
    return dict(toep=np.ascontiguousarray(toep), mw4=np.ascontiguousarray(mw4), bcw=np.ascontiguousarray(bcw),
                selAB=np.ascontiguousarray(selAB), rb31=rb31, ident=np.eye(128, dtype=np.float32))


def a_inputs(proj_full, inputs, j):
    in_maps = []
    for c in range(NCORE):
        b, g = c // 4, c % 4
        pf = proj_full[b]
        Qg = pf[g * 256:(g + 1) * 256].reshape(4, 64, NQT, 128)
        m = {"q": np.ascontiguousarray(Qg.transpose(2, 1, 0, 3).reshape(NQT, 64, 512))}
        kts = []
        for idx in (0, 1, 2, 4):
            r0 = 1024 + idx * 256 + g * 64
            kts.append(pf[r0:r0 + 64])
        m["kT"] = np.ascontiguousarray(np.stack(kts))
        vts = []
        for idx in (3, 5):
            r0 = 1024 + idx * 256 + g * 64
            vts.append(pf[r0:r0 + 64].reshape(64, 64, 128).transpose(2, 1, 0))
        m["v"] = np.ascontiguousarray(np.stack(vts))
        G = pf[2560 + g * 12:2560 + (g + 1) * 12].reshape(4, 3, NQT, 128)
        m["gl"] = np.ascontiguousarray(G.transpose(2, 3, 1, 0).reshape(NQT, 128, 12))
        m["gains"] = np.ascontiguousarray(np.stack([inputs["nsa_q_gain"][j], inputs["nsa_k_gain"][j, 0],
                                                    inputs["nsa_k_gain"][j, 1], inputs["nsa_k_gain"][j, 2]], axis=1).astype(np.float32))
        m["posT"] = np.ascontiguousarray(np.asarray(inputs["nsa_cmp_pos"][j], np.float32).T)
        m["w1"] = np.ascontiguousarray(inputs["nsa_cmp_w1"][j])
        m["w2"] = np.ascontiguousarray(inputs["nsa_cmp_w2"][j])
        m.update(a_tables(inputs["rel_bias"], g))
        in_maps.append(m)
    return in_maps


_A = []


def run_A(proj_full, inputs, j):
    if not _A:
        _A.append(build_A())
    res = run_bass_kernel_spmd(_A[0], a_inputs(proj_full, inputs, j), core_ids=list(range(NCORE))).results
    o_full = [np.zeros((D, S), np.float32) for _ in range(B)]
    for c in range(NCORE):
        b, g = c // 4, c % 4
        o = res[c]["oT"].reshape(NQT, 64, 4, 128).transpose(2, 1, 0, 3).reshape(256, S)
        o_full[b][g * 256:(g + 1) * 256] = o
    return o_full


def _o_to_cores(o_full):
    outs = []
    for c in range(NCORE):
        b, ch = c // 4, c % 4
        t0 = ch * TOK
        ot = np.zeros((D, NT), np.float32)
        ot[:, HALO:] = o_full[b][:, t0:t0 + TOK]
        if ch > 0:
            ot[:, 0:HALO] = o_full[b][:, t0 - HALO:t0]
        outs.append(ot)
    return outs


def _proj_full(res, si):
    pf = []
    for b in range(B):
        pf.append(np.ascontiguousarray(np.concatenate([res[b * 4 + ch][f"projT{si}"] for ch in range(4)], axis=1)))
    return pf


def kernel(**inputs):
    inputs = {k: np.asarray(v) for k, v in inputs.items()}
    x = np.asarray(inputs["x"], np.float32)
    st1 = (("ada", 0), ("ffn", 0, 0), ("conv", 0), ("ffn", 0, 1), ("ada", 1), ("ffn", 1, 0), ("nsa_pre", 1))
    res = run_T(st1, x, inputs)
    x = cores_to_x([r["xoT"] for r in res])
    o_full = run_A(_proj_full(res, 6), inputs, 0)
    st2 = (("nsa_post", 1), ("ffn", 1, 1), ("ada", 2), ("ffn", 2, 0), ("conv", 2), ("ffn", 2, 1),
           ("ada", 3), ("ffn", 3, 0), ("nsa_pre", 3))
    oc = _o_to_cores(o_full)
    mods = [r["modout"] for r in res]
    res = run_T(st2, x, inputs, extra=[{"oT0": oc[c]} for c in range(NCORE)], modin=mods)
    x = cores_to_x([r["xoT"] for r in res])
    o_full = run_A(_proj_full(res, 8), inputs, 1)
    st3 = (("nsa_post", 3), ("ffn", 3, 1))
    oc = _o_to_cores(o_full)
    mods = [r["modout"] for r in res]
    res = run_T(st3, x, inputs, extra=[{"oT0": oc[c]} for c in range(NCORE)], modin=mods)
    x = cores_to_x([r["xoT"] for r in res])
    return x.astype(np.float32)
```
